# Optimizing a Trainium2 kernel written in Bass

```python
import math
import jax, jax.numpy as jnp
from jax import lax
import numpy as np

D_MODEL = 1024
BATCH = 4
SEQ = 8192
DEPTH = 1

CTX_LEN = 256
GRID_W = 64
N_MOD = 6
EPS = 1e-6
RET_HEADS = 4
RET_DK = 128
RET_DV = 256
RET_QK = RET_HEADS * RET_DK
RET_V = RET_HEADS * RET_DV
RET_CHUNK = 128
ROPE_BASE = 10000.0
D_RNN = 1024
LRU_BLOCKS = 8
LRU_BS = D_RNN // LRU_BLOCKS
CONV_W = 4
CONV_LEFT = 2
LRU_C = 8.0
PEER_HEADS = 8
PEER_DKEY = 128
PEER_DHALF = PEER_DKEY // 2
PEER_NKEYS = 128
PEER_TOPK = 16
PEER_NEXP = PEER_NKEYS * PEER_NKEYS
PEER_BLOCK = 128
IN_SIZES = (RET_QK, RET_QK, RET_V, RET_V, D_RNN, D_RNN, D_MODEL, D_MODEL)
IN_COLS = sum(IN_SIZES)

kernel_name = 'hybrid_retention_rglru_peer_block'


def rms_norm(x, g):
    x32 = x.astype(jnp.float32)
    y = x32 * lax.rsqrt(jnp.mean(x32 * x32, axis=-1, keepdims=True) + EPS)
    return (y * g.astype(jnp.float32)).astype(x.dtype)


def modulate(h, shift, scale):
    return h * (1.0 + scale) + shift


def split_proj(p):
    idx = [int(s) for s in np.cumsum(IN_SIZES)[:-1]]
    return jnp.split(p, idx, axis=-1)


def head_norm(o):
    mu = jnp.mean(o, axis=-1, keepdims=True)
    var = jnp.mean(jnp.square(o - mu), axis=-1, keepdims=True)
    return (o - mu) * lax.rsqrt(var + EPS)


def axial_rope(t, rows):
    quarter = RET_DK // 4
    row = jnp.repeat(jnp.arange(rows), GRID_W)
    col = jnp.tile(jnp.arange(GRID_W), rows)
    inv = ROPE_BASE ** (-jnp.arange(quarter, dtype=jnp.float32) / quarter)

    def rot(u, pos):
        ang = pos.astype(jnp.float32)[:, None] * inv[None, :]
        cos = jnp.cos(ang)[None, :, None, :]
        sin = jnp.sin(ang)[None, :, None, :]
        u1, u2 = u[..., :quarter], u[..., quarter:]
        return jnp.concatenate([u1 * cos - u2 * sin, u2 * cos + u1 * sin], axis=-1)

    half = RET_DK // 2
    return jnp.concatenate([rot(t[..., :half], row), rot(t[..., half:], col)], axis=-1)


def retention_dir(q, k, v, log_gamma, s0, inclusive):
    b_, L, H, _ = q.shape
    dv = v.shape[-1]
    n = L // RET_CHUNK

    def chunks(t):
        return t.reshape(b_, n, RET_CHUNK, H, t.shape[-1]).transpose(1, 0, 3, 2, 4)

    idx = jnp.arange(RET_CHUNK, dtype=jnp.float32)
    diff = idx[:, None] - idx[None, :]
    keep = (diff >= 0) if inclusive else (diff > 0)
    dmat = jnp.where(keep[None], jnp.exp(log_gamma[:, None, None] * jnp.maximum(diff, 0.0)[None]), 0.0)
    xi = jnp.exp(log_gamma[:, None] * (idx + 1.0)[None])[..., None]
    zeta = jnp.exp(log_gamma[:, None] * (RET_CHUNK - 1.0 - idx)[None])[..., None]
    chunk_decay = jnp.exp(log_gamma * RET_CHUNK)[:, None, None]

    def step(S, blk):
        qb, kb, vb = blk
        scores = jnp.einsum('bhid,bhjd->bhij', qb, kb) * dmat
        o = jnp.einsum('bhij,bhjv->bhiv', scores, vb) + jnp.einsum('bhid,bhdv->bhiv', qb * xi, S)
        S = chunk_decay * S + jnp.einsum('bhjd,bhjv->bhdv', kb * zeta, vb)
        return S, o

    S, o = lax.scan(step, s0, (chunks(q), chunks(k), chunks(v)))
    o = o.transpose(1, 0, 3, 2, 4).reshape(b_, L, H, dv)
    return o, S


def retention_mixer(pc, px, ret_decay_l, rows):
    f32 = jnp.float32
    b_ = px[0].shape[0]

    def heads(t, d):
        return t.astype(f32).reshape(t.shape[0], t.shape[1], RET_HEADS, d)

    kscale = RET_DK ** -0.5
    qc, kc, vc = heads(pc[0], RET_DK), heads(pc[1], RET_DK) * kscale, heads(pc[2], RET_DV)
    qx = axial_rope(heads(px[0], RET_DK), rows)
    kx = axial_rope(heads(px[1], RET_DK), rows) * kscale
    vx = heads(px[2], RET_DV)
    log_gamma = jax.nn.log_sigmoid(ret_decay_l.astype(f32))
    zeros = jnp.zeros((b_, RET_HEADS, RET_DK, RET_DV), f32)

    def bidir(q, k, v, s0f, s0b):
        of, sf = retention_dir(q, k, v, log_gamma[0], s0f, True)
        ob, sb = retention_dir(q[:, ::-1], k[:, ::-1], v[:, ::-1], log_gamma[1], s0b, False)
        return of + ob[:, ::-1], sf, sb

    oc, sf, sb = bidir(qc, kc, vc, zeros, zeros)
    ox, _, _ = bidir(qx, kx, vx, sf, sb)
    return head_norm(oc), head_norm(ox)


def short_conv(u, w, b):
    L = u.shape[1]
    up = jnp.pad(u, ((0, 0), (CONV_LEFT, CONV_W - 1 - CONV_LEFT), (0, 0)))
    out = b.astype(jnp.float32)
    for tap in range(CONV_W):
        out = out + up[:, tap:tap + L] * w[tap].astype(jnp.float32)
    return out


def lru_coeffs(u, wa, ba, wx, bx, lam):
    f32 = jnp.float32
    b_, L, _ = u.shape
    ub = u.reshape(b_, L, LRU_BLOCKS, LRU_BS)
    r = jax.nn.sigmoid(jnp.einsum('blnc,ncd->blnd', ub, wa.astype(f32)).reshape(b_, L, D_RNN) + ba.astype(f32))
    i = jax.nn.sigmoid(jnp.einsum('blnc,ncd->blnd', ub, wx.astype(f32)).reshape(b_, L, D_RNN) + bx.astype(f32))
    log_a = -LRU_C * r * jax.nn.softplus(-lam.astype(f32))
    a = jnp.exp(log_a)
    bterm = jnp.sqrt(-jnp.expm1(2.0 * log_a)) * (i * u)
    return a, bterm


def _lin_combine(e1, e2):
    a1, b1 = e1
    a2, b2 = e2
    return a1 * a2, a2 * b1 + b2


def lru_scan(a, b, h0, reverse):
    A, Bc = lax.associative_scan(_lin_combine, (a, b), axis=1, reverse=reverse)
    h = A * h0[:, None, :] + Bc
    final = h[:, 0] if reverse else h[:, -1]
    return h, final


def lru_mixer(uc_in, ux_in, cw, cb, wa, ba, wx, bx, lam):
    uc = short_conv(uc_in.astype(jnp.float32), cw, cb)
    ux = short_conv(ux_in.astype(jnp.float32), cw, cb)
    ys_c, ys_x = [], []
    for d in range(2):
        rev = d == 1
        ac, bc = lru_coeffs(uc, wa[d], ba[d], wx[d], bx[d], lam[d])
        hc, fin = lru_scan(ac, bc, jnp.zeros_like(uc[:, 0]), rev)
        ax, bxx = lru_coeffs(ux, wa[d], ba[d], wx[d], bx[d], lam[d])
        hx, _ = lru_scan(ax, bxx, fin, rev)
        ys_c.append(hc)
        ys_x.append(hx)
    return ys_c[0] + ys_c[1], ys_x[0] + ys_x[1]


def merge_branches(ret_o, lru_y, p, w_ret_out, w_lru_out, w_out):
    dt = p[3].dtype
    b_, L = ret_o.shape[0], ret_o.shape[1]
    ret = (ret_o.reshape(b_, L, RET_V).astype(dt) * jax.nn.silu(p[3])) @ w_ret_out
    lru = (lru_y.astype(dt) * jax.nn.gelu(p[5])) @ w_lru_out
    y = jax.nn.sigmoid(p[6]) * ret + jax.nn.sigmoid(p[7]) * lru
    return y @ w_out


def peer_ffn(h, wq, keys, u_tab, v_tab):
    b_, L, D = h.shape
    tok = h.reshape(-1, PEER_BLOCK, D)
    K = PEER_TOPK

    def block(xb):
        q = (xb @ wq).reshape(PEER_BLOCK, PEER_HEADS, 2, PEER_DHALF)
        s = jnp.einsum('thpd,hpkd->thpk', q, keys).astype(jnp.float32)
        s1, i1 = lax.top_k(s[:, :, 0], K)
        s2, i2 = lax.top_k(s[:, :, 1], K)
        cand = (s1[..., :, None] + s2[..., None, :]).reshape(PEER_BLOCK, PEER_HEADS, K * K)
        top, ci = lax.top_k(cand, K)
        e = jnp.take_along_axis(i1, ci // K, axis=-1) * PEER_NKEYS + jnp.take_along_axis(i2, ci % K, axis=-1)
        g = jax.nn.softmax(top, axis=-1).astype(xb.dtype)
        act = jax.nn.gelu(jnp.einsum('thkd,td->thk', u_tab[e], xb))
        return jnp.einsum('thk,thkd->td', g * act, v_tab[e])

    return lax.map(block, tok).reshape(b_, L, D)


def setup_inputs(seed: int = 0) -> dict:
    key = jax.random.key(seed)
    ks = jax.random.split(key, 32)
    f = jnp.float32
    D = D_MODEL

    def nrm(k, shape, s):
        return jax.random.normal(k, shape, f) * s

    gamma = 1.0 - jnp.exp2(-5.0 - jnp.arange(RET_HEADS, dtype=f))
    gamma_logit = jnp.log(gamma) - jnp.log1p(-gamma)
    ret_decay = jnp.broadcast_to(gamma_logit, (DEPTH, 2, RET_HEADS)) + nrm(ks[9], (DEPTH, 2, RET_HEADS), 0.01)
    a8 = jax.random.uniform(ks[17], (DEPTH, 2, D_RNN), f, 0.9, 0.999)
    s_lam = a8 ** (1.0 / LRU_C)
    lru_lambda = jnp.log(s_lam) - jnp.log1p(-s_lam)
    return {
        'x': nrm(ks[0], (BATCH, SEQ, D), 1.0),
        'c': nrm(ks[1], (BATCH, D), 1.0),
        'ctx': nrm(ks[2], (BATCH, CTX_LEN, D), 1.0),
        'c_ctx': nrm(ks[3], (D,), 1.0),
        'mod_w': nrm(ks[4], (DEPTH, D, N_MOD * D), 0.5 * D ** -0.5),
        'mod_b': nrm(ks[5], (DEPTH, N_MOD * D), 0.02),
        'norm1_g': 1.0 + nrm(ks[6], (DEPTH, D), 0.02),
        'norm2_g': 1.0 + nrm(ks[7], (DEPTH, D), 0.02),
        'w_in': nrm(ks[8], (DEPTH, D, IN_COLS), D ** -0.5),
        'ret_decay': ret_decay,
        'conv_w': nrm(ks[10], (DEPTH, CONV_W, D_RNN), CONV_W ** -0.5),
        'conv_b': nrm(ks[11], (DEPTH, D_RNN), 0.02),
        'lru_wa': nrm(ks[12], (DEPTH, 2, LRU_BLOCKS, LRU_BS, LRU_BS), LRU_BS ** -0.5),
        'lru_ba': nrm(ks[13], (DEPTH, 2, D_RNN), 0.1),
        'lru_wx': nrm(ks[14], (DEPTH, 2, LRU_BLOCKS, LRU_BS, LRU_BS), LRU_BS ** -0.5),
        'lru_bx': nrm(ks[15], (DEPTH, 2, D_RNN), 0.1),
        'lru_lambda': lru_lambda,
        'w_ret_out': nrm(ks[18], (DEPTH, RET_V, D), RET_V ** -0.5),
        'w_lru_out': nrm(ks[19], (DEPTH, D_RNN, D), D_RNN ** -0.5),
        'w_out': nrm(ks[20], (DEPTH, D, D), D ** -0.5),
        'peer_wq': nrm(ks[21], (DEPTH, D, PEER_HEADS * PEER_DKEY), D ** -0.5),
        'peer_keys': nrm(ks[22], (DEPTH, PEER_HEADS, 2, PEER_NKEYS, PEER_DHALF), PEER_DHALF ** -0.5),
        'peer_u': nrm(ks[23], (DEPTH, PEER_NEXP, D), D ** -0.5),
        'peer_v': nrm(ks[24], (DEPTH, PEER_NEXP, D), PEER_HEADS ** -0.5),
        'final_g': 1.0 + nrm(ks[25], (D,), 0.02),
    }


def reference(x, c, ctx, c_ctx, mod_w, mod_b, norm1_g, norm2_g, w_in, ret_decay, conv_w, conv_b,
              lru_wa, lru_ba, lru_wx, lru_bx, lru_lambda, w_ret_out, w_lru_out, w_out,
              peer_wq, peer_keys, peer_u, peer_v, final_g):
    n_tok = x.shape[1]
    rows = n_tok // GRID_W
    sc = jax.nn.silu(c)
    scc = jax.nn.silu(c_ctx)
    for l in range(DEPTH):
        last = l == DEPTH - 1
        mod_x = (sc @ mod_w[l] + mod_b[l])[:, None, :]
        mod_c = (scc @ mod_w[l] + mod_b[l])[None, None, :]
        sh1x, sc1x, g1x, sh2x, sc2x, g2x = jnp.split(mod_x, N_MOD, axis=-1)
        sh1c, sc1c, g1c, sh2c, sc2c, g2c = jnp.split(mod_c, N_MOD, axis=-1)

        hx = modulate(rms_norm(x, norm1_g[l]), sh1x, sc1x)
        hc = modulate(rms_norm(ctx, norm1_g[l]), sh1c, sc1c)
        px = split_proj(hx @ w_in[l])
        pc = split_proj(hc @ w_in[l])
        ret_c, ret_x = retention_mixer(pc, px, ret_decay[l], rows)
        lru_c, lru_x = lru_mixer(pc[4], px[4], conv_w[l], conv_b[l], lru_wa[l], lru_ba[l],
                                 lru_wx[l], lru_bx[l], lru_lambda[l])
        x = x + g1x * merge_branches(ret_x, lru_x, px, w_ret_out[l], w_lru_out[l], w_out[l])
        if not last:
            ctx = ctx + g1c * merge_branches(ret_c, lru_c, pc, w_ret_out[l], w_lru_out[l], w_out[l])
            hc2 = modulate(rms_norm(ctx, norm2_g[l]), sh2c, sc2c)
            ctx = ctx + g2c * peer_ffn(hc2, peer_wq[l], peer_keys[l], peer_u[l], peer_v[l])

        hx2 = modulate(rms_norm(x, norm2_g[l]), sh2x, sc2x)
        x = x + g2x * peer_ffn(hx2, peer_wq[l], peer_keys[l], peer_u[l], peer_v[l])
    return rms_norm(x, final_g)
```

```python
import math
from contextlib import ExitStack

import numpy as np
import concourse.bass as bass
import concourse.mybir as mybir
from concourse.bass_utils import run_bass_kernel_spmd

F32 = mybir.dt.float32
BF16 = mybir.dt.bfloat16
ALU = mybir.AluOpType
AF = mybir.ActivationFunctionType

ENG = ("pe", "act", "dve", "pool", "sp")
NDMASEM = 12


class _Op:
    __slots__ = ("eng", "fn", "deps", "is_dma", "sig", "dsem", "dval", "dprev")

    def __init__(self, eng, fn, is_dma):
        self.eng = eng
        self.fn = fn
        self.deps = []
        self.is_dma = is_dma
        self.sig = None
        self.dsem = None
        self.dval = None
        self.dprev = None


class Sched:
    def __init__(self):
        self.ops = []
        self.lastw = {}
        self.readers = {}
        self.ndma = 0
        self.dsem_last = [None] * NDMASEM
        self.last_eng = {}

    def add(self, eng, fn, reads=(), writes=(), dma=False):
        op = _Op(eng, fn, dma)
        deps = []
        for k in reads:
            w = self.lastw.get(k)
            if w is not None:
                deps.append(w)
        for k in writes:
            w = self.lastw.get(k)
            if w is not None:
                deps.append(w)
            deps.extend(self.readers.get(k, ()))
        for k in reads:
            lst = self.readers.setdefault(k, [])
            if not dma:
                lst[:] = [o for o in lst if o.is_dma or o.eng != eng]
            lst.append(op)
        for k in writes:
            self.lastw[k] = op
            self.readers[k] = []
        if dma:
            s = self.ndma % NDMASEM
            self.ndma += 1
            op.dsem = s
            prev = self.dsem_last[s]
            op.dprev = prev
            op.dval = 16 if prev is None else prev.dval + 16
            self.dsem_last[s] = op
        seen = set()
        for d in deps:
            if id(d) in seen or d is op:
                continue
            seen.add(id(d))
            if d.is_dma or d.eng != eng or eng != "pe":
                op.deps.append(d)
        self.ops.append(op)
        if not dma:
            self.last_eng[eng] = op
        return op

    def pe(self, fn, r=(), w=()):
        return self.add("pe", fn, r, w)

    def act(self, fn, r=(), w=()):
        return self.add("act", fn, r, w)

    def dve(self, fn, r=(), w=()):
        return self.add("dve", fn, r, w)

    def pool(self, fn, r=(), w=()):
        return self.add("pool", fn, r, w)

    def dma(self, fn, r=(), w=()):
        return self.add("sp", fn, r, w, dma=True)

    def barrier(self):
        lasts = [o for o in self.last_eng.values()] + [o for o in self.dsem_last if o is not None]
        for e in ENG:
            op = _Op(e, None, False)
            op.deps = [d for d in lasts]
            self.ops.append(op)
        self.lastw.clear()
        self.readers.clear()

    def emit(self, nc, EPOCH=20000):
        need = set()
        for op in self.ops:
            for d in op.deps:
                if not d.is_dma:
                    need.add(id(d))
        cnt = {e: 0 for e in ENG}
        for op in self.ops:
            if not op.is_dma and op.fn is not None and id(op) in need:
                c = cnt[op.eng]
                op.sig = (c // EPOCH, c % EPOCH + 1)
                cnt[op.eng] = c + 1
        nep = {e: (cnt[e] + EPOCH - 1) // EPOCH for e in ENG}
        with ExitStack() as st:
            esem = {e: [st.enter_context(nc.semaphore(f"s_{e}_{i}")) for i in range(nep[e])] for e in ENG}
            dsem = [st.enter_context(nc.semaphore(f"s_dma_{i}")) for i in range(NDMASEM)]
            block = st.enter_context(nc.Block())
            per = {e: [o for o in self.ops if o.eng == e] for e in ENG}

            def run(e, eng):
                waited = {}
                for op in per[e]:
                    wants = {}
                    for d in op.deps:
                        if d.is_dma:
                            key = ("d", d.dsem)
                            val = d.dval
                            sem = dsem[d.dsem]
                        else:
                            if d.sig is None:
                                continue
                            ep, val = d.sig
                            key = (d.eng, ep)
                            sem = esem[d.eng][ep]
                        if wants.get(key, (None, 0))[1] < val:
                            wants[key] = (sem, val)
                    if op.is_dma and op.dprev is not None:
                        key = ("d", op.dsem)
                        if wants.get(key, (None, 0))[1] < op.dprev.dval:
                            wants[key] = (dsem[op.dsem], op.dprev.dval)
                    for key, (sem, val) in wants.items():
                        if waited.get(key, 0) >= val:
                            continue
                        waited[key] = val
                        eng.wait_ge(sem, val)
                    if op.fn is None:
                        continue
                    ins = op.fn(eng)
                    if op.is_dma:
                        ins.then_inc(dsem[op.dsem], 16)
                    elif op.sig is not None:
                        ins.then_inc(esem[e][op.sig[0]], 1)
                if e == "sp":
                    for s in range(NDMASEM):
                        last = self.dsem_last[s]
                        if last is not None and waited.get(("d", s), 0) < last.dval:
                            eng.wait_ge(dsem[s], last.dval)

            @block.tensor
            def _(eng):
                run("pe", eng)

            @block.scalar
            def _(eng):
                run("act", eng)

            @block.vector
            def _(eng):
                run("dve", eng)

            @block.gpsimd
            def _(eng):
                run("pool", eng)

            @block.sync
            def _(eng):
                run("sp", eng)
        return cnt


D = 1024
NTOK = 8192
OWN = 4096
NCH = 64
NOWN = 32
EPS = 1e-6
KSCALE = 128 ** -0.5
ARENA_W = 45056
CQ, CK, CV, CP3, CLX, CP5, CP6, CP7 = 0, 512, 1024, 2048, 3072, 4096, 5120, 6144


def _prod(s):
    n = 1
    for v in s:
        n *= v
    return n


class Arena:
    def __init__(self, t, words):
        self.t = t
        self.n = words
        self.off = 0

    def reset(self):
        self.off = 0

    def _shape(self, v, shape):
        if len(shape) == 1:
            return v
        names = "abcdefg"[: len(shape)]
        pat = "p (" + " ".join(names) + ") -> p " + " ".join(names)
        return v.rearrange(pat, **{n: s for n, s in zip(names, shape)})

    def f32(self, *shape, parts=128):
        n = _prod(shape)
        assert self.off + n <= self.n, ("arena overflow", self.off, n)
        v = self.t[0:parts, self.off:self.off + n]
        self.off += n
        return self._shape(v, shape)

    def bf16(self, *shape, parts=128):
        n = _prod(shape)
        w = (n + 1) // 2
        assert self.off + w <= self.n, ("arena overflow", self.off, w)
        v = self.t[0:parts, self.off:self.off + w].bitcast(BF16)[:, 0:n]
        self.off += w
        return self._shape(v, shape)


def _rev(ap2d):
    dims = ap2d.ap
    pstep, pcnt = dims[0]
    n = dims[-1][1]
    assert len(dims) == 2 and dims[-1][0] == 1
    return bass.AP(ap2d.tensor, ap2d.offset + n - 1, [[pstep, pcnt], [-1, n]])


def build(debug=None, phases="all"):
    nc = bass.Bass("TRN2", target_bir_lowering=False)
    S = Sched()

    def din(name, shape, dt=F32):
        return nc.dram_tensor(name, list(shape), dt, kind="ExternalInput")

    xf_t = din("xf", [NTOK, D])
    ctx_t = din("ctxf", [256, D])
    rope_t = din("rope", [NTOK, 256])
    cst_t = din("cst", [128, 642])
    par_t = din("par", [128, 112])
    dec_t = din("dec", [1, 8])
    gains_t = din("gains", [3, D])
    modb_t = din("mod_b", [1, 6 * D])
    modw_t = din("mod_w", [D, 6 * D])
    win_t = din("w_in", [D, 7168])
    wro_t = din("w_ret_out", [D, D])
    wlo_t = din("w_lru_out", [D, D])
    wo_t = din("w_out", [D, D])
    wq_t = din("peer_wq", [D, D])
    keys_t = din("keysbd", [128, 8 * 256])
    gw_t = din("gatew", [128, 4 * 8 * 128])
    ut_t = din("peer_uT", [D, 16384])
    v_t = din("peer_v", [16384, D])
    out_t = nc.dram_tensor("out", [OWN, D], F32, kind="ExternalOutput")
    modd_t = nc.dram_tensor("modd", [2, 6 * D], F32, kind="Internal")
    utb_t = nc.dram_tensor("utb", [32, 128, 8, 512], BF16, kind="Internal")
    vb_t = nc.dram_tensor("vb", [32, 128, 4096], BF16, kind="Internal")
    sbs_t = nc.dram_tensor("sbs", [NOWN, 128, 1024], BF16, kind="Internal")
    ret_t = nc.dram_tensor("retsc", [OWN, D], F32, kind="Internal")
    lru_t = nc.dram_tensor("lrusc", [OWN, D], F32, kind="Internal")
    xn_t = nc.dram_tensor("xnsc", [OWN, D], F32, kind="Internal")
    dbg = {}
    for nm in (debug.split(",") if debug else []):
        dbg[nm] = nc.dram_tensor("dbg_" + nm, [OWN, D], F32, kind="ExternalOutput")

    xf = xf_t.ap()
    ctxf = ctx_t.ap()
    rope = rope_t.ap()

    with ExitStack() as st:
        def sbt(name, shape, dt=F32):
            return st.enter_context(nc.sbuf_tensor(name, list(shape), dt))

        arena_t = sbt("arena", [128, ARENA_W])
        AR = Arena(arena_t, ARENA_W)
        ps = st.enter_context(nc.psum_tensor("ps", [128, 4096], F32))
        cst = sbt("cstsb", [128, 642])
        identb = sbt("identb", [128, 128], BF16)
        par = sbt("parsb", [128, 112])
        lg = sbt("lg", [128, 8])
        dct = sbt("dct", [128, 4, 128])
        xiA = sbt("xiA", [128, 4, 128])
        xiB = sbt("xiB", [128, 4, 128])
        zeta = sbt("zeta", [128, 8])
        decay = sbt("decay", [128, 8])
        nsp8 = sbt("nsp8", [128, 16])
        hprev = sbt("hprev", [128, 2, 8])
        hbinit = sbt("hbinit", [128, NOWN, 8])
        SA = sbt("SA", [128, 4, 256])
        SB = sbt("SB", [128, 4, 256])
        gwb = sbt("gwb", [128, 4, 8, 128], BF16)
        small = sbt("small", [128, 64])

        ident = cst[:, 0:128]
        Pm = cst[:, 128:256]
        Nm = cst[:, 256:384]
        rampI1 = cst[:, 384:512]
        rampI2 = cst[:, 512:640]
        rampJ = cst[:, 640:641]
        rampJ2 = cst[:, 641:642]
        cvec = par[:, 0:16].rearrange("p (k j) -> p k j", j=2)
        lam = par[:, 16:32]
        ba = par[:, 32:48].rearrange("p (d b) -> p d b", d=2)
        bx = par[:, 48:64].rearrange("p (d b) -> p d b", d=2)
        w5 = par[:, 64:104].rearrange("p (o b) -> p o b", o=5)
        cb = par[:, 104:112]

        def PSB(b, nb=1):
            return ps[:, b * 512:(b + nb) * 512]

        def PSBb(b):
            return ps[:, b * 512:(b + 1) * 512].bitcast(BF16)

        def pk(b, nb=1):
            return [f"ps{i}" for i in range(b, b + nb)]

        def DMA(out, in_, r=(), w=()):
            S.dma(lambda e: e.dma_start(out=out, in_=in_), r, w)

        def MM(out, lhsT, rhs, start, stop, r, w):
            S.pe(lambda e: e.matmul(out, lhsT=lhsT, rhs=rhs, start=start, stop=stop), r, w)

        def TR(out, in_, idn, r, w):
            S.pe(lambda e: e.transpose(out=out, in_=in_, identity=idn), r, w)

        def ACT(out, in_, func, r, w, bias=None, scale=None, accum=None):
            kw = {}
            if bias is not None:
                kw["bias"] = bias
            if scale is not None:
                kw["scale"] = scale
            if accum is not None:
                kw["accum_out"] = accum
            S.act(lambda e: e.activation(out=out, in_=in_, func=func, **kw), r, w)

        def TT(eng, out, in0, in1, op, r, w):
            S.add(eng, lambda e: e.tensor_tensor(out=out, in0=in0, in1=in1, op=op), r, w)

        def TS(eng, out, in0, s1, s2, op0, op1, r, w):
            if s2 is None:
                S.add(eng, lambda e: e.tensor_scalar(out=out, in0=in0, scalar1=s1, scalar2=None, op0=op0), r, w)
            else:
                S.add(eng, lambda e: e.tensor_scalar(out=out, in0=in0, scalar1=s1, scalar2=s2, op0=op0, op1=op1), r, w)

        def STT(eng, out, in0, scalar, in1, op0, op1, r, w):
            S.add(eng, lambda e: e.scalar_tensor_tensor(out=out, in0=in0, scalar=scalar, in1=in1, op0=op0, op1=op1), r, w)

        def CP(eng, out, in_, r, w):
            if eng == "act":
                S.act(lambda e: e.copy(out=out, in_=in_), r, w)
            else:
                S.add(eng, lambda e: e.tensor_copy(out=out, in_=in_), r, w)

        def MEMSET(eng, out, val, w):
            S.add(eng, lambda e: e.memset(out, val), (), w)

        rr = [0]

        def rot3():
            rr[0] += 1
            return ("dve", "pool", "act")[rr[0] % 3]

        def bc_last(ap, n):
            shp = list(ap.shape)
            shp[-1] = n
            return ap.broadcast_to(shp)

        def wload(dst, src2d, ncols, stg, key, piece=256):
            npc = ncols // piece
            for i in range(npc):
                sb = stg[i % 2]
                sk = f"stg{i % 2}"
                DMA(sb[:, :, 0:piece], src2d[:, i * piece:(i + 1) * piece].rearrange("(k p) n -> p k n", p=128), (), [sk])
                CP(rot3(), dst[:, :, i * piece:(i + 1) * piece], sb[:, :, 0:piece], [sk], [key])

        def rep_load(dst, row_ap, key):
            DMA(dst, row_ap.partition_broadcast(128), (), [key])

        modd = modd_t.ap()
        gains = gains_t.ap()

        def modrow(r, i):
            return modd[r:r + 1, i * D:(i + 1) * D]

        def make_A(dst, tmp, r, i_scale, gain_idx, key, tkey):
            rep_load(dst, modrow(r, i_scale), key)
            rep_load(tmp, gains[gain_idx:gain_idx + 1, :], tkey)
            STT("dve", dst, dst, 1.0, tmp, ALU.add, ALU.mult, [key, tkey], [key])

        epsc = small[:, 8:9]
        MEMSET("dve", epsc, EPS, ["epsc"])
        DMA(cst[:], cst_t.ap(), (), ["cst"])
        DMA(par[:], par_t.ap(), (), ["par"])
        DMA(lg[:], dec_t.ap().partition_broadcast(128), (), ["lg"])
        CP("dve", identb[:], ident, ["cst"], ["identb"])
        ACT(lg[:], lg[:], AF.Exp, ["lg"], ["lg"], scale=-1.0)
        ACT(lg[:], lg[:], AF.Ln, ["lg"], ["lg"], bias=1.0)
        TS("dve", lg[:], lg[:], -1.0, None, ALU.mult, None, ["lg"], ["lg"])
        lnks = math.log(KSCALE)
        tmpm = small[:, 0:1]
        for h in range(4):
            t0 = AR.f32(128)
            TS("dve", t0, Pm, lg[:, h:h + 1], None, ALU.mult, None, ["cst", "lg"], [f"t0_{h}"])
            STT("dve", t0, Nm, lg[:, 4 + h:5 + h], t0, ALU.mult, ALU.add, ["cst", "lg", f"t0_{h}"], [f"t0_{h}"])
            TS("dve", t0, t0, lnks, None, ALU.add, None, [f"t0_{h}"], [f"t0_{h}"])
            ACT(dct[:, h, :], t0, AF.Exp, [f"t0_{h}"], ["dct"])
            ACT(xiA[:, h, :], rampI1, AF.Exp, ["cst", "lg"], ["xiA"], scale=lg[:, h:h + 1])
            ACT(xiB[:, h, :], rampI2, AF.Exp, ["cst", "lg"], ["xiB"], scale=lg[:, 4 + h:5 + h])
            t1 = AR.f32(2)
            TS("dve", t1[:, 0:1], rampJ, lg[:, h:h + 1], lnks, ALU.mult, ALU.add, ["cst", "lg"], [f"t1_{h}"])
            TS("dve", t1[:, 1:2], rampJ2, lg[:, 4 + h:5 + h], lnks, ALU.mult, ALU.add, ["cst", "lg"], [f"t1_{h}"])
            ACT(zeta[:, h:h + 1], t1[:, 0:1], AF.Exp, [f"t1_{h}"], ["zeta"])
            ACT(zeta[:, 4 + h:5 + h], t1[:, 1:2], AF.Exp, [f"t1_{h}"], ["zeta"])
        ACT(decay[:], lg[:], AF.Exp, ["lg"], ["decay"], scale=128.0)
        ACT(nsp8[:], lam, AF.Exp, ["par"], ["nsp8"], scale=-1.0)
        ACT(nsp8[:], nsp8[:], AF.Ln, ["nsp8"], ["nsp8"], bias=1.0)
        TS("dve", nsp8[:], nsp8[:], -8.0, None, ALU.mult, None, ["nsp8"], ["nsp8"])
        nsp8v = nsp8[:].rearrange("p (d b) -> p d b", d=2)
        gst = AR.f32(4 * 8 * 128)
        DMA(gst, gw_t.ap(), (), ["gst"])
        CP("pool", gwb[:].rearrange("p a b c -> p (a b c)"), gst, ["gst"], ["gwb"])
        cvs = AR.f32(8, 2)
        ACT(cvs, cvec, AF.Silu, ["par"], ["cvs"])
        mwb = [AR.f32(8, 512), AR.f32(8, 512)]
        mb2 = AR.f32(6 * D, parts=2)
        DMA(mb2[0:1, :], modb_t.ap(), (), ["mb2"])
        DMA(mb2[1:2, :], modb_t.ap(), (), ["mb2"])
        msb = AR.f32(6 * D, parts=2)
        for n in range(12):
            mw = mwb[n % 2]
            DMA(mw, modw_t.ap()[:, n * 512:(n + 1) * 512].rearrange("(k p) n -> p k n", p=128), (), [f"mw{n % 2}"])
            for kc in range(8):
                MM(ps[0:2, (n % 2) * 512:(n % 2) * 512 + 512], cvs[:, kc, :], mw[:, kc, :], kc == 0, kc == 7,
                   ["cvs", f"mw{n % 2}"], [f"ps{n % 2}"])
            TT("dve", msb[:, n * 512:(n + 1) * 512], ps[0:2, (n % 2) * 512:(n % 2) * 512 + 512],
               mb2[:, n * 512:(n + 1) * 512], ALU.add, [f"ps{n % 2}", "mb2"], ["msb"])
        DMA(modd, msb, ["msb"], ["modd"])
        S.barrier()
        AR.reset()

        if phases in ("all", "peer"):
            sf = [AR.f32(4096), AR.f32(4096)]
            sbf = [AR.bf16(4096), AR.bf16(4096)]
            utb = utb_t.ap()
            vb = vb_t.ap()
            i = 0
            for kc in range(8):
                for eq in range(4):
                    b = i % 2
                    DMA(sf[b], ut_t.ap()[kc * 128:(kc + 1) * 128, eq * 4096:(eq + 1) * 4096], (), [f"sf{b}"])
                    CP(rot3(), sbf[b], sf[b], [f"sf{b}"], [f"sbf{b}"])
                    DMA(utb[eq * 8:(eq + 1) * 8, :, kc, :].rearrange("e p n -> p e n"),
                        sbf[b].rearrange("p (e n) -> p e n", e=8), [f"sbf{b}"], ())
                    i += 1
            for eb in range(32):
                b = i % 2
                DMA(sf[b].rearrange("p (c d) -> p c d", c=4),
                    v_t.ap()[eb * 512:(eb + 1) * 512, :].rearrange("(c p) d -> p c d", p=128), (), [f"sf{b}"])
                CP(rot3(), sbf[b], sf[b], [f"sf{b}"], [f"sbf{b}"])
                DMA(vb[eb], sbf[b], [f"sbf{b}"], ())
                i += 1
            S.barrier()
            AR.reset()

        def front(xt, xk, Arep, shrep, repkeys, hx, junk, hxT_dst, hxTkey, psbank, ssq, parts=128, ncol=128):
            ACT(junk[0:parts, :], xt, AF.Square, [xk], ["junk", "ssq"], accum=ssq[0:parts, 0:1])
            ACT(ssq[0:parts, 1:2], ssq[0:parts, 0:1], AF.Ln, ["ssq", "epsc"], ["ssq"], bias=epsc[0:parts, 0:1], scale=1.0 / D)
            ACT(ssq[0:parts, 2:3], ssq[0:parts, 1:2], AF.Exp, ["ssq"], ["ssq"], scale=-0.5)
            STT("dve", junk[0:parts, :], xt, ssq[0:parts, 2:3], Arep[0:parts, :], ALU.mult, ALU.mult,
                [xk, "ssq"] + repkeys, ["junk"])
            TT("pool", hx[0:parts, :], junk[0:parts, :], shrep[0:parts, :], ALU.add, ["junk"] + repkeys, ["hx"])
            pt = PSBb(psbank)
            for kc in range(8):
                TR(pt[:, kc * ncol:(kc + 1) * ncol], hx[0:parts, kc * 128:(kc + 1) * 128], identb[0:parts, 0:parts],
                   ["hx", "identb"], pk(psbank))
            CP("act", hxT_dst, pt[:, 0:8 * ncol].rearrange("p (k n) -> p k n", k=8), pk(psbank), [hxTkey])

        def load_phase_common(need_halo):
            d = {}
            d["stg"] = [AR.f32(8, 256), AR.f32(8, 256)]
            d["x"] = [AR.f32(D), AR.f32(D)]
            d["junk"] = AR.f32(D)
            d["hx"] = AR.bf16(D)
            d["ssq"] = AR.f32(4)
            if need_halo:
                d["xh"] = AR.f32(D)
                d["hxh"] = AR.bf16(D)
                d["ssqh"] = AR.f32(4)
                d["junkh"] = AR.f32(D)
                d["hxT"] = AR.bf16(8, 132)
            else:
                d["hxT"] = AR.bf16(8, 128)
            return d

        def lru_gates(W, dr, uc, ucb, hkey_prefix):
            r_, i_, la, a2, bt = W["r"], W["i"], W["la"], W["a2"], W["bt"]
            for blk in range(8):
                MM(PSB(4, 2)[:, blk * 128:(blk + 1) * 128], gwb[:, 2 * dr, blk, :], ucb[:, blk, :], True, True,
                   ["gwb", "ucb"], pk(4, 2))
            for blk in range(8):
                MM(PSB(6, 2)[:, blk * 128:(blk + 1) * 128], gwb[:, 2 * dr + 1, blk, :], ucb[:, blk, :], True, True,
                   ["gwb", "ucb"], pk(6, 2))
            TT("dve", r_, PSB(4, 2).rearrange("p (b t) -> p b t", b=8), bc_last(ba[:, dr, :].unsqueeze(2), 128), ALU.add,
               pk(4, 2) + ["par"], ["r"])
            TT("dve", i_, PSB(6, 2).rearrange("p (b t) -> p b t", b=8), bc_last(bx[:, dr, :].unsqueeze(2), 128), ALU.add,
               pk(6, 2) + ["par"], ["i"])
            ACT(r_, r_, AF.Sigmoid, ["r"], ["r"])
            ACT(i_, i_, AF.Sigmoid, ["i"], ["i"])
            TT("dve", la, r_, bc_last(nsp8v[:, dr, :].unsqueeze(2), 128), ALU.mult, ["r", "nsp8"], ["la"])
            ACT(a2, la, AF.Exp, ["la"], ["a2"], scale=2.0)
            ACT(la, la, AF.Exp, ["la"], ["la"])
            TS("dve", a2, a2, -1.0, 1.0, ALU.mult, ALU.add, ["a2"], ["a2"])
            ACT(a2, a2, AF.Sqrt, ["a2"], ["a2"])
            TT("pool", bt, i_, uc, ALU.mult, ["i", "uc"], ["bt"])
            TT("pool", bt, bt, a2, ALU.mult, ["bt", "a2"], ["bt"])

        def lru_scan(W, dr, hout, hkey):
            la, bt = W["la"], W["bt"]
            first = 0 if dr == 0 else 127
            last = 127 if dr == 0 else 0
            tmp8 = W["tmp8"]
            TT("dve", tmp8, la[:, :, first], hprev[:, dr, :], ALU.mult, ["la", "hprev"], ["tmp8"])
            TT("dve", bt[:, :, first], bt[:, :, first], tmp8, ALU.add, ["bt", "tmp8"], ["bt"])
            MEMSET("dve", la[:, :, first], 0.0, ["la"])
            la2 = la.rearrange("p b t -> p (b t)")
            bt2 = bt.rearrange("p b t -> p (b t)")
            h2 = hout.rearrange("p b t -> p (b t)")
            if dr == 0:
                S.dve(lambda e: e.tensor_tensor_scan(out=h2, data0=la2, data1=bt2, initial=0.0, op0=ALU.mult, op1=ALU.add),
                      ["la", "bt"], [hkey])
            else:
                ro, ra, rb = _rev(h2), _rev(la2), _rev(bt2)
                S.dve(lambda e: e.tensor_tensor_scan(out=ro, data0=ra, data1=rb, initial=0.0, op0=ALU.mult, op1=ALU.add),
                      ["la", "bt"], [hkey])
            CP("dve", hprev[:, dr, :], hout[:, :, last], [hkey], ["hprev"])

        def conv(W, uT, uc, ucb):
            m0, m1 = W["m0"], W["m1"]
            TT("dve", uc, uT[:, :, 0:128], bc_last(w5[:, 0, :].unsqueeze(2), 128), ALU.mult, ["uT", "par"], ["uc"])
            TT("dve", uc, uc, bc_last(cb.unsqueeze(2), 128), ALU.add, ["uc", "par"], ["uc"])
            for o in range(1, 5):
                m = m0 if o % 2 else m1
                mk = "m0" if o % 2 else "m1"
                TT("pool", m, uT[:, :, o:o + 128], bc_last(w5[:, o, :].unsqueeze(2), 128), ALU.mult, ["uT", "par"], [mk])
                TT("dve", uc, uc, m, ALU.add, ["uc", mk], ["uc"])
            CP("act", ucb, uc, ["uc"], ["ucb"])

        def halo_front(W, src, j, nchunks, Arep, shrep, repkeys, psbank):
            xh, hxh = W["xh"], W["hxh"]
            lo = j * 128 - 2
            hi = j * 128 + 128
            lo_c = max(lo, 0)
            hi_c = min(hi, nchunks * 128 - 2)
            DMA(xh[0:2, :], src[lo_c:lo_c + 2, :], (), ["xh"])
            DMA(xh[2:4, :], src[hi_c:hi_c + 2, :], (), ["xh"])
            front(xh[0:4, :], "xh", Arep, shrep, repkeys, hxh, W["junkh"], W["hxT4"], "hxT4", psbank, W["ssqh"], parts=4, ncol=4)
            CP("dve", W["hxT"][:, :, 0:2], W["hxT4"][:, :, 0:2], ["hxT4"], ["hxT"])
            CP("dve", W["hxT"][:, :, 130:132], W["hxT4"][:, :, 2:4], ["hxT4"], ["hxT"])

        def proj_uT(W, wlx, j, nchunks):
            hxT, uT = W["hxT"], W["uT"]
            pu = PSB(4, 4).rearrange("p (b t) -> p b t", b=8)
            for blk in range(8):
                for kc in range(8):
                    MM(pu[:, blk, 0:132], wlx[:, kc, blk * 128:(blk + 1) * 128], hxT[:, kc, :], kc == 0, kc == 7,
                       ["wlx", "hxT"], pk(4, 4))
            CP("act", uT, pu[:, :, 0:132], pk(4, 4), ["uT"])
            if j == 0:
                MEMSET("dve", uT[:, :, 0:2], 0.0, ["uT"])
            if j == nchunks - 1:
                MEMSET("dve", uT[:, :, 130:132], 0.0, ["uT"])

        def rope_tok(W, src_ps, srckeys, rp, rpk, dst, dstkey):
            t1, t2 = W["t1"], W["t2"]
            s4 = src_ps.rearrange("p (h c) -> p h c", h=4)
            Cb = rp[:, 0:128].unsqueeze(1).broadcast_to([128, 4, 128])
            TT("dve", t1, s4, Cb, ALU.mult, srckeys + [rpk], ["t1"])
            s5 = src_ps.rearrange("p (h a b c) -> p h a b c", h=4, a=2, b=2)
            t25 = t2.rearrange("p h (a b c) -> p h a b c", a=2, b=2)
            sg5 = rp[:, 128:256].rearrange("p (a b c) -> p a b c", a=2, b=2)
            TT("dve", t25[:, :, :, 0, :], s5[:, :, :, 1, :], sg5[:, :, 0, :].unsqueeze(1).broadcast_to([128, 4, 2, 32]),
               ALU.mult, srckeys + [rpk], ["t2"])
            TT("dve", t25[:, :, :, 1, :], s5[:, :, :, 0, :], sg5[:, :, 1, :].unsqueeze(1).broadcast_to([128, 4, 2, 32]),
               ALU.mult, srckeys + [rpk], ["t2"])
            TT("pool", t1, t1, t2, ALU.add, ["t1", "t2"], ["t1"])

        sbs = sbs_t.ap()

        if phases in ("all", "mix"):
            W = load_phase_common(True)
            W["hxT4"] = AR.bf16(8, 4)
            wk = AR.bf16(8, 512)
            wv = AR.bf16(8, 1024)
            wlx = AR.bf16(8, 1024)
            rep = [AR.f32(D) for _ in range(4)]
            reptmp = W["junk"]
            W["rp"] = [AR.f32(256), AR.f32(256)]
            W["t1"] = AR.f32(4, 128)
            W["t2"] = AR.f32(4, 128)
            kz = AR.bf16(4, 128)
            vbf = AR.bf16(4, 256)
            sbf_ = AR.bf16(4, 256)
            W["uT"] = AR.f32(8, 132)
            W["m0"] = AR.f32(8, 128)
            W["m1"] = AR.f32(8, 128)
            uc = AR.f32(8, 128)
            ucb = AR.bf16(8, 128)
            for nm in ("r", "i", "la", "a2", "bt"):
                W[nm] = AR.f32(8, 128)
            hh_ = AR.f32(8, 128)
            W["tmp8"] = AR.f32(8)
            wload(wk, win_t.ap()[:, CK:CK + 512], 512, W["stg"], "wk")
            wload(wv, win_t.ap()[:, CV:CV + 1024], 1024, W["stg"], "wv")
            wload(wlx, win_t.ap()[:, CLX:CLX + 1024], 1024, W["stg"], "wlx")
            make_A(rep[0], reptmp, 0, 1, 0, "rep0", "junk")
            rep_load(rep[1], modrow(0, 0), "rep1")
            make_A(rep[2], reptmp, 1, 1, 0, "rep2", "junk")
            rep_load(rep[3], modrow(1, 0), "rep3")
            MEMSET("dve", SA[:], 0.0, ["SA"])
            MEMSET("dve", SB[:], 0.0, ["SB"])
            MEMSET("dve", hprev[:], 0.0, ["hprev"])

            def p1_chunk(src, nchunks, j, dr, is_ctx, xi, store_idx):
                Arep, shrep, rk = (rep[2], rep[3], ["rep2", "rep3"]) if is_ctx else (rep[0], rep[1], ["rep0", "rep1"])
                xt = W["x"][xi]
                xk = f"x{xi}"
                DMA(xt, src[j * 128:(j + 1) * 128, :], (), [xk])
                if not is_ctx:
                    DMA(W["rp"][xi], rope[j * 128:(j + 1) * 128, :], (), [f"rp{xi}"])
                front(xt, xk, Arep, shrep, rk, W["hx"], W["junk"], W["hxT"][:, :, 2:130], "hxT", 0, W["ssq"])
                halo_front(W, src, j, nchunks, Arep, shrep, rk, 0)
                for kc in range(8):
                    MM(PSB(1), W["hxT"][:, kc, 2:130], wk[:, kc, :], kc == 0, kc == 7, ["hxT", "wk"], pk(1))
                for n in range(2):
                    for kc in range(8):
                        MM(PSB(2 + n), W["hxT"][:, kc, 2:130], wv[:, kc, n * 512:(n + 1) * 512], kc == 0, kc == 7,
                           ["hxT", "wv"], pk(2 + n))
                Sd = SA if dr == 0 else SB
                Sk = "SA" if dr == 0 else "SB"
                zt = zeta[:, 4 * dr:4 * dr + 4]
                if is_ctx:
                    TT("dve", kz, PSB(1).rearrange("p (h c) -> p h c", h=4), bc_last(zt.unsqueeze(2), 128), ALU.mult,
                       pk(1) + ["zeta"], ["kz"])
                else:
                    rope_tok(W, PSB(1), pk(1), W["rp"][xi], f"rp{xi}", None, None)
                    TT("dve", kz, W["t1"], bc_last(zt.unsqueeze(2), 128), ALU.mult, ["t1", "zeta"], ["kz"])
                CP("act", vbf, PSB(2, 2).rearrange("p (h c) -> p h c", h=4), pk(2, 2), ["vbf"])
                if store_idx is not None:
                    CP("pool", sbf_, Sd[:], [Sk], ["sbf_"])
                    DMA(sbs[store_idx], sbf_.rearrange("p h c -> p (h c)"), ["sbf_"], ())
                    CP("dve", hbinit[:, store_idx, :], hprev[:, 1, :], ["hprev"], ["hbinit"])
                for hh in range(4):
                    MM(PSB(2, 2)[:, hh * 256:(hh + 1) * 256], kz[:, hh, :], vbf[:, hh, :], True, True, ["kz", "vbf"], pk(2, 2))
                for hh in range(4):
                    STT("dve", Sd[:, hh, :], Sd[:, hh, :], decay[:, 4 * dr + hh:4 * dr + hh + 1], PSB(2, 2)[:, hh * 256:(hh + 1) * 256],
                        ALU.mult, ALU.add, [Sk, "decay"] + pk(2, 2), [Sk])
                proj_uT(W, wlx, j, nchunks)
                conv(W, W["uT"], uc, ucb)
                lru_gates(W, dr, uc, ucb, "h")
                lru_scan(W, dr, hh_, "hh")

            xi = 0
            for (j, dr) in ((0, 0), (1, 0), (1, 1), (0, 1)):
                p1_chunk(ctxf, 2, j, dr, True, xi, None)
                xi ^= 1
            for j in range(NCH - 1, -1, -1):
                p1_chunk(xf, NCH, j, 1, False, xi, j if j < NOWN else None)
                xi ^= 1
            S.barrier()
            AR.reset()

            W = load_phase_common(False)
            wqk = AR.bf16(8, 1024)
            wv = AR.bf16(8, 1024)
            wp3 = AR.bf16(8, 1024)
            wro = AR.bf16(8, 1024)
            rep = [AR.f32(D) for _ in range(2)]
            W["rp"] = [AR.f32(256), AR.f32(256)]
            W["t1"] = AR.f32(4, 128)
            W["t2"] = AR.f32(4, 128)
            qr = AR.bf16(4, 128)
            kr = AR.bf16(4, 128)
            kz = AR.bf16(4, 128)
            qT = AR.bf16(4, 128)
            qxA = AR.bf16(4, 128)
            qxB = AR.bf16(4, 128)
            kT = AR.bf16(4, 128)
            vbf = AR.bf16(4, 256)
            p3s = AR.f32(D)
            sT = AR.bf16(4, 128)
            osb = AR.f32(4, 256)
            osq = AR.f32(4, 256)
            st4 = AR.f32(16)
            retg = AR.bf16(D)
            retgT = AR.bf16(8, 128)
            rout = [AR.f32(D), AR.f32(D)]
            sabf = AR.bf16(4, 256)
            sbbf = [AR.bf16(4, 256), AR.bf16(4, 256)]
            wload(wqk, win_t.ap()[:, CQ:CQ + 1024], 1024, W["stg"], "wqk")
            wload(wv, win_t.ap()[:, CV:CV + 1024], 1024, W["stg"], "wv")
            wload(wp3, win_t.ap()[:, CP3:CP3 + 1024], 1024, W["stg"], "wp3")
            wload(wro, wro_t.ap(), 1024, W["stg"], "wro")
            make_A(rep[0], W["junk"], 0, 1, 0, "rep0", "junk")
            rep_load(rep[1], modrow(0, 0), "rep1")
            CP("pool", sabf, SA[:], ["SA"], ["sabf"])
            retsc = ret_t.ap()
            for j in range(NOWN):
                xi = j % 2
                xt = W["x"][xi]
                xk = f"x{xi}"
                DMA(xt, xf[j * 128:(j + 1) * 128, :], (), [xk])
                DMA(W["rp"][xi], rope[j * 128:(j + 1) * 128, :], (), [f"rp{xi}"])
                DMA(sbbf[xi].rearrange("p h c -> p (h c)"), sbs[j], (), [f"sbbf{xi}"])
                front(xt, xk, rep[0], rep[1], ["rep0", "rep1"], W["hx"], W["junk"], W["hxT"], "hxT", 0, W["ssq"])
                for n in range(2):
                    for kc in range(8):
                        MM(PSB(1 + n), W["hxT"][:, kc, :], wqk[:, kc, n * 512:(n + 1) * 512], kc == 0, kc == 7,
                           ["hxT", "wqk"], pk(1 + n))
                for n in range(2):
                    for kc in range(8):
                        MM(PSB(3 + n), W["hxT"][:, kc, :], wv[:, kc, n * 512:(n + 1) * 512], kc == 0, kc == 7,
                           ["hxT", "wv"], pk(3 + n))
                for n in range(2):
                    for kc in range(8):
                        MM(PSB(5 + n), W["hxT"][:, kc, :], wp3[:, kc, n * 512:(n + 1) * 512], kc == 0, kc == 7,
                           ["hxT", "wp3"], pk(5 + n))
                rope_tok(W, PSB(1), pk(1), W["rp"][xi], f"rp{xi}", None, None)
                CP("act", qr, W["t1"], ["t1"], ["qr"])
                rope_tok(W, PSB(2), pk(2), W["rp"][xi], f"rp{xi}", None, None)
                CP("act", kr, W["t1"], ["t1"], ["kr"])
                TT("dve", kz, W["t1"], bc_last(zeta[:, 0:4].unsqueeze(2), 128), ALU.mult, ["t1", "zeta"], ["kz"])
                CP("act", vbf, PSB(3, 2).rearrange("p (h c) -> p h c", h=4), pk(3, 2), ["vbf"])
                ACT(p3s, PSB(5, 2), AF.Silu, pk(5, 2), ["p3s"])
                pt = PSBb(7)
                for hh in range(4):
                    TR(pt[:, hh * 128:(hh + 1) * 128], qr[:, hh, :], identb[:], ["qr", "identb"], pk(7))
                for hh in range(4):
                    TR(pt[:, 512 + hh * 128:512 + (hh + 1) * 128], kr[:, hh, :], identb[:], ["kr", "identb"], pk(7))
                ptq = pt[:, 0:512].rearrange("p (h t) -> p h t", h=4)
                CP("act", qT, ptq, pk(7), ["qT"])
                TT("dve", qxA, ptq, xiA[:], ALU.mult, pk(7) + ["xiA"], ["qxA"])
                TT("dve", qxB, ptq, xiB[:], ALU.mult, pk(7) + ["xiB"], ["qxB"])
                CP("act", kT, pt[:, 512:1024].rearrange("p (h t) -> p h t", h=4), pk(7), ["kT"])
                for hh in range(4):
                    MM(PSB(1)[:, hh * 128:(hh + 1) * 128], kT[:, hh, :], qT[:, hh, :], True, True, ["kT", "qT"], pk(1))
                TT("dve", sT, PSB(1).rearrange("p (h t) -> p h t", h=4), dct[:], ALU.mult, pk(1) + ["dct"], ["sT"])
                for hh in range(4):
                    o_ps = PSB(3, 2)[:, hh * 256:(hh + 1) * 256]
                    MM(o_ps, sT[:, hh, :], vbf[:, hh, :], True, False, ["sT", "vbf"], pk(3, 2))
                    MM(o_ps, qxA[:, hh, :], sabf[:, hh, :], False, False, ["qxA", "sabf"], pk(3, 2))
                    MM(o_ps, qxB[:, hh, :], sbbf[xi][:, hh, :], False, True, ["qxB", f"sbbf{xi}"], pk(3, 2))
                for hh in range(4):
                    MM(PSB(5, 2)[:, hh * 256:(hh + 1) * 256], kz[:, hh, :], vbf[:, hh, :], True, True, ["kz", "vbf"], pk(5, 2))
                CP("act", osb, PSB(3, 2).rearrange("p (h c) -> p h c", h=4), pk(3, 2), ["osb"])
                ACT(osq, osb, AF.Square, ["osb"], ["osq"])
                S.dve(lambda e, o=st4[:, 0:4], i=osb: e.reduce_sum(out=o, in_=i, axis=mybir.AxisListType.X), ["osb"], ["st4"])
                S.dve(lambda e, o=st4[:, 4:8], i=osq: e.reduce_sum(out=o, in_=i, axis=mybir.AxisListType.X), ["osq"], ["st4"])
                TS("dve", st4[:, 0:8], st4[:, 0:8], 1.0 / 256, None, ALU.mult, None, ["st4"], ["st4"])
                TT("dve", st4[:, 8:12], st4[:, 0:4], st4[:, 0:4], ALU.mult, ["st4"], ["st4"])
                TT("dve", st4[:, 8:12], st4[:, 4:8], st4[:, 8:12], ALU.subtract, ["st4"], ["st4"])
                ACT(st4[:, 8:12], st4[:, 8:12], AF.Ln, ["st4", "epsc"], ["st4"], bias=epsc[:, 0:1])
                ACT(st4[:, 8:12], st4[:, 8:12], AF.Exp, ["st4"], ["st4"], scale=-0.5)
                TT("dve", osb, osb, bc_last(st4[:, 0:4].unsqueeze(2), 256), ALU.subtract, ["osb", "st4"], ["osb"])
                TT("dve", osb, osb, bc_last(st4[:, 8:12].unsqueeze(2), 256), ALU.mult, ["osb", "st4"], ["osb"])
                if "ret" in dbg:
                    DMA(dbg["ret"].ap()[j * 128:(j + 1) * 128, :], osb.rearrange("p h c -> p (h c)"), ["osb"], ())
                TT("pool", retg, osb.rearrange("p h c -> p (h c)"), p3s, ALU.mult, ["osb", "p3s"], ["retg"])
                pt0 = PSBb(0)
                for kc in range(8):
                    TR(pt0[:, kc * 128:(kc + 1) * 128], retg[:, kc * 128:(kc + 1) * 128], identb[:], ["retg", "identb"], pk(0))
                CP("act", retgT, pt0.rearrange("p (k n) -> p k n", k=8), pk(0), ["retgT"])
                for n in range(2):
                    for kc in range(8):
                        MM(PSB(1 + n), retgT[:, kc, :], wro[:, kc, n * 512:(n + 1) * 512], kc == 0, kc == 7,
                           ["retgT", "wro"], pk(1 + n))
                CP("act", rout[xi], PSB(1, 2), pk(1, 2), [f"rout{xi}"])
                DMA(retsc[j * 128:(j + 1) * 128, :], rout[xi], [f"rout{xi}"], ())
                for hh in range(4):
                    STT("dve", SA[:, hh, :], SA[:, hh, :], decay[:, hh:hh + 1], PSB(5, 2)[:, hh * 256:(hh + 1) * 256],
                        ALU.mult, ALU.add, ["SA", "decay"] + pk(5, 2), ["SA"])
                CP("pool", sabf, SA[:], ["SA"], ["sabf"])
            S.barrier()
            AR.reset()

            W = load_phase_common(True)
            W["hxT4"] = AR.bf16(8, 4)
            wlx = AR.bf16(8, 1024)
            wp5 = AR.bf16(8, 1024)
            wlo = AR.bf16(8, 1024)
            rep = [AR.f32(D) for _ in range(2)]
            W["uT"] = AR.f32(8, 132)
            W["m0"] = AR.f32(8, 128)
            W["m1"] = AR.f32(8, 128)
            uc = AR.f32(8, 128)
            ucb = AR.bf16(8, 128)
            for nm in ("r", "i", "la", "a2", "bt"):
                W[nm] = AR.f32(8, 128)
            hA = AR.f32(8, 128)
            hB = AR.f32(8, 128)
            W["tmp8"] = AR.f32(8)
            p5g = AR.f32(8, 128)
            yg = AR.bf16(8, 128)
            lout = [AR.f32(D), AR.f32(D)]
            wload(wlx, win_t.ap()[:, CLX:CLX + 1024], 1024, W["stg"], "wlx")
            wload(wp5, win_t.ap()[:, CP5:CP5 + 1024], 1024, W["stg"], "wp5")
            wload(wlo, wlo_t.ap(), 1024, W["stg"], "wlo")
            make_A(rep[0], W["junk"], 0, 1, 0, "rep0", "junk")
            rep_load(rep[1], modrow(0, 0), "rep1")
            lrusc = lru_t.ap()
            for j in range(NOWN):
                xi = j % 2
                xt = W["x"][xi]
                xk = f"x{xi}"
                DMA(xt, xf[j * 128:(j + 1) * 128, :], (), [xk])
                front(xt, xk, rep[0], rep[1], ["rep0", "rep1"], W["hx"], W["junk"], W["hxT"][:, :, 2:130], "hxT", 0, W["ssq"])
                halo_front(W, xf, j, NCH, rep[0], rep[1], ["rep0", "rep1"], 0)
                proj_uT(W, wlx, j, NCH)
                p5ps = PSB(1, 2).rearrange("p (b t) -> p b t", b=8)
                for blk in range(8):
                    for kc in range(8):
                        MM(p5ps[:, blk, :], wp5[:, kc, blk * 128:(blk + 1) * 128], W["hxT"][:, kc, 2:130], kc == 0, kc == 7,
                           ["wp5", "hxT"], pk(1, 2))
                ACT(p5g, p5ps, AF.Gelu_apprx_tanh, pk(1, 2), ["p5g"])
                conv(W, W["uT"], uc, ucb)
                lru_gates(W, 0, uc, ucb, "hA")
                lru_scan(W, 0, hA, "hA")
                CP("dve", hprev[:, 1, :], hbinit[:, j, :], ["hbinit"], ["hprev"])
                lru_gates(W, 1, uc, ucb, "hB")
                lru_scan(W, 1, hB, "hB")
                TT("pool", hA, hA, hB, ALU.add, ["hA", "hB"], ["hA"])
                TT("dve", yg, hA, p5g, ALU.mult, ["hA", "p5g"], ["yg"])
                for n in range(2):
                    for blk in range(8):
                        MM(PSB(1 + n), yg[:, blk, :], wlo[:, blk, n * 512:(n + 1) * 512], blk == 0, blk == 7,
                           ["yg", "wlo"], pk(1 + n))
                CP("act", lout[xi], PSB(1, 2), pk(1, 2), [f"lout{xi}"])
                DMA(lrusc[j * 128:(j + 1) * 128, :], lout[xi], [f"lout{xi}"], ())
            S.barrier()
            AR.reset()

            W = load_phase_common(False)
            wp67 = AR.bf16(8, 2048)
            wo = AR.bf16(8, 1024)
            rep = [AR.f32(D) for _ in range(3)]
            rin = [AR.f32(D), AR.f32(D)]
            lin = [AR.f32(D), AR.f32(D)]
            g6 = AR.f32(D)
            g7 = AR.f32(D)
            ym = AR.bf16(D)
            ymT = AR.bf16(8, 128)
            xnew = [AR.f32(D), AR.f32(D)]
            wload(wp67, win_t.ap()[:, CP6:CP6 + 2048], 2048, W["stg"], "wp67")
            wload(wo, wo_t.ap(), 1024, W["stg"], "wo")
            make_A(rep[0], W["junk"], 0, 1, 0, "rep0", "junk")
            rep_load(rep[1], modrow(0, 0), "rep1")
            rep_load(rep[2], modrow(0, 2), "rep2")
            xnsc = xn_t.ap()
            for j in range(NOWN):
                xi = j % 2
                xt = W["x"][xi]
                xk = f"x{xi}"
                DMA(xt, xf[j * 128:(j + 1) * 128, :], (), [xk])
                DMA(rin[xi], retsc[j * 128:(j + 1) * 128, :], (), [f"rin{xi}"])
                DMA(lin[xi], lrusc[j * 128:(j + 1) * 128, :], (), [f"lin{xi}"])
                front(xt, xk, rep[0], rep[1], ["rep0", "rep1"], W["hx"], W["junk"], W["hxT"], "hxT", 0, W["ssq"])
                for n in range(4):
                    for kc in range(8):
                        MM(PSB(1 + n), W["hxT"][:, kc, :], wp67[:, kc, n * 512:(n + 1) * 512], kc == 0, kc == 7,
                           ["hxT", "wp67"], pk(1 + n))
                ACT(g6, PSB(1, 2), AF.Sigmoid, pk(1, 2), ["g6"])
                ACT(g7, PSB(3, 2), AF.Sigmoid, pk(3, 2), ["g7"])
                TT("dve", g6, g6, rin[xi], ALU.mult, ["g6", f"rin{xi}"], ["g6"])
                TT("pool", g7, g7, lin[xi], ALU.mult, ["g7", f"lin{xi}"], ["g7"])
                TT("dve", ym, g6, g7, ALU.add, ["g6", "g7"], ["ym"])
                pt0 = PSBb(5)
                for kc in range(8):
                    TR(pt0[:, kc * 128:(kc + 1) * 128], ym[:, kc * 128:(kc + 1) * 128], identb[:], ["ym", "identb"], pk(5))
                CP("act", ymT, pt0.rearrange("p (k n) -> p k n", k=8), pk(5), ["ymT"])
                for n in range(2):
                    for kc in range(8):
                        MM(PSB(6 + n), ymT[:, kc, :], wo[:, kc, n * 512:(n + 1) * 512], kc == 0, kc == 7,
                           ["ymT", "wo"], pk(6 + n))
                TT("dve", xnew[xi], PSB(6, 2), rep[2], ALU.mult, pk(6, 2) + ["rep2"], [f"xnew{xi}"])
                TT("pool", xnew[xi], xnew[xi], xt, ALU.add, [f"xnew{xi}", xk], [f"xnew{xi}"])
                DMA(xnsc[j * 128:(j + 1) * 128, :], xnew[xi], [f"xnew{xi}"], ())
                if "xn" in dbg:
                    DMA(dbg["xn"].ap()[j * 128:(j + 1) * 128, :], xnew[xi], [f"xnew{xi}"], ())
            S.barrier()
            AR.reset()

        if phases in ("all", "peer"):
            xnsc = xn_t.ap() if phases == "all" else xf
            stg = [AR.f32(8, 128), AR.f32(8, 128)]
            wq = AR.bf16(8, 1024)
            keysb = AR.bf16(8, 256)
            rep = [AR.f32(D) for _ in range(4)]
            junk = AR.f32(D)
            hx2 = AR.bf16(D)
            qTs = AR.bf16(8, 128)
            ssq = AR.f32(4)
            T = []
            for t in range(2):
                T.append(dict(xn=AR.f32(D), hT=AR.bf16(8, 128), s=AR.f32(8, 2, 128), a16=AR.f32(8, 2, 16),
                              top=AR.f32(8, 16), st=AR.f32(32), wtot=AR.f32(8, 128)))
            Cb_ = AR.f32(8, 128)
            Eb_ = AR.f32(8, 128)
            Wt_ = AR.f32(8, 128)
            ublk = [AR.bf16(8, 512), AR.bf16(8, 512)]
            vblk = [AR.bf16(4, 1024), AR.bf16(4, 1024)]
            G = [AR.f32(512), AR.f32(512)]
            coef = [AR.bf16(512), AR.bf16(512)]
            coefT = [AR.bf16(4, 128), AR.bf16(4, 128)]
            outt = AR.f32(D)
            wload(wq, wq_t.ap(), 1024, stg, "wq", piece=128)
            kst = Cb_.rearrange("p a b -> p (a b)")
            DMA(Cb_.rearrange("p a b -> p (a b)"), keys_t.ap()[:, 0:1024], (), ["C"])
            CP("dve", keysb.rearrange("p a b -> p (a b)")[:, 0:1024], Cb_.rearrange("p a b -> p (a b)"), ["C"], ["keysb"])
            DMA(Eb_.rearrange("p a b -> p (a b)"), keys_t.ap()[:, 1024:2048], (), ["E"])
            CP("dve", keysb.rearrange("p a b -> p (a b)")[:, 1024:2048], Eb_.rearrange("p a b -> p (a b)"), ["E"], ["keysb"])
            make_A(rep[0], junk, 0, 4, 1, "rep0", "junk")
            rep_load(rep[1], modrow(0, 3), "rep1")
            rep_load(rep[2], modrow(0, 5), "rep2")
            rep_load(rep[3], gains[2:3, :], "rep3")
            utb = utb_t.ap()
            vb = vb_t.ap()
            outd = out_t.ap()
            NBLK = OWN // 256
            if "peer1" in dbg:
                NBLK = 1
            for blk in range(NBLK):
                for t in range(2):
                    Tt = T[t]
                    tk = f"T{t}"
                    row0 = blk * 256 + t * 128
                    DMA(Tt["xn"], xnsc[row0:row0 + 128, :], (), [tk + "xn"])
                    front(Tt["xn"], tk + "xn", rep[0], rep[1], ["rep0", "rep1"], hx2, junk, Tt["hT"], tk + "hT", 7, ssq)
                    qps = PSB(5, 2).rearrange("p (h t) -> p h t", h=8)
                    for hh in range(8):
                        for kc in range(8):
                            MM(qps[:, hh, :], wq[:, kc, hh * 128:(hh + 1) * 128], Tt["hT"][:, kc, :], kc == 0, kc == 7,
                               ["wq", tk + "hT"], pk(5, 2))
                    CP("act", qTs, qps, pk(5, 2), ["qTs"])
                    sps = PSB(0, 4).rearrange("p (h c) -> p h c", h=8)
                    for hh in range(8):
                        MM(sps[:, hh, :], qTs[:, hh, :], keysb[:, hh, :], True, True, ["qTs", "keysb"], pk(0, 4))
                    s = Tt["s"]
                    CP("act", s.rearrange("p h a k -> p h (a k)"), sps, pk(0, 4), [tk + "s"])
                    a16 = Tt["a16"]
                    for g in range(2):
                        sl = [(g * 4 + q, p_) for q in range(4) for p_ in range(2)]
                        for n_, (hh, p_) in enumerate(sl):
                            S.dve(lambda e, o=a16[:, hh, p_, 0:8], i=s[:, hh, p_, :]: e.max(out=o, in_=i), [tk + "s"], [tk + "a16"])
                        for n_, (hh, p_) in enumerate(sl):
                            S.dve(lambda e, o=Cb_[:, n_, :], r_=a16[:, hh, p_, 0:8], i=s[:, hh, p_, :]:
                                  e.match_replace(out=o, in_to_replace=r_, in_values=i, imm_value=-1e30),
                                  [tk + "s", tk + "a16"], ["C"])
                        for n_, (hh, p_) in enumerate(sl):
                            S.dve(lambda e, o=a16[:, hh, p_, 8:16], i=Cb_[:, n_, :]: e.max(out=o, in_=i), ["C"], [tk + "a16"])
                    for g in range(2):
                        cbuf = (Eb_, Wt_)[g].rearrange("p a b -> p (a b)").rearrange("p (h r q) -> p h r q", h=4, r=16)
                        ckey = ("E", "W")[g]
                        in0 = a16[:, g * 4:(g + 1) * 4, 0, :].unsqueeze(3).broadcast_to([128, 4, 16, 16])
                        in1 = a16[:, g * 4:(g + 1) * 4, 1, :].unsqueeze(2).broadcast_to([128, 4, 16, 16])
                        TT("dve", cbuf, in0, in1, ALU.add, [tk + "a16"], [ckey])
                    top = Tt["top"]
                    cflat = [Eb_.rearrange("p a b -> p (a b)"), Wt_.rearrange("p a b -> p (a b)")]
                    for hh in range(8):
                        cv_ = cflat[hh // 4][:, (hh % 4) * 256:(hh % 4 + 1) * 256]
                        ck_ = ("E", "W")[hh // 4]
                        S.dve(lambda e, o=top[:, hh, 0:8], i=cv_: e.max(out=o, in_=i), [ck_], [tk + "top"])
                    for hh in range(8):
                        cv_ = cflat[hh // 4][:, (hh % 4) * 256:(hh % 4 + 1) * 256]
                        ck_ = ("E", "W")[hh // 4]
                        S.dve(lambda e, o=Cb_.rearrange("p a b -> p (a b)")[:, (hh % 4) * 256:(hh % 4 + 1) * 256],
                              r_=top[:, hh, 0:8], i=cv_: e.match_replace(out=o, in_to_replace=r_, in_values=i, imm_value=-1e30),
                              [ck_, tk + "top"], ["C"])
                        S.dve(lambda e, o=top[:, hh, 8:16], i=Cb_.rearrange("p a b -> p (a b)")[:, (hh % 4) * 256:(hh % 4 + 1) * 256]:
                              e.max(out=o, in_=i), ["C"], [tk + "top"])
                    stt_ = Tt["st"]
                    TS("dve", stt_[:, 0:8], top[:, :, 0], -1.0, None, ALU.mult, None, [tk + "top"], [tk + "st"])
                    for hh in range(8):
                        ACT(junk[:, hh * 16:(hh + 1) * 16], top[:, hh, :], AF.Exp, [tk + "top", tk + "st"], ["junk", tk + "st"],
                            bias=stt_[:, hh:hh + 1], accum=stt_[:, 8 + hh:9 + hh])
                    ACT(stt_[:, 16:24], stt_[:, 8:16], AF.Ln, [tk + "st"], [tk + "st"])
                    TT("dve", stt_[:, 16:24], stt_[:, 0:8], stt_[:, 16:24], ALU.subtract, [tk + "st"], [tk + "st"])
                for cr in range(16):
                    for t in range(2):
                        Tt = T[t]
                        tk = f"T{t}"
                        s = Tt["s"]
                        stt_ = Tt["st"]
                        for hh in range(8):
                            in0 = s[:, hh, 0, cr * 8:(cr + 1) * 8].unsqueeze(2).broadcast_to([128, 8, 128])
                            in1 = s[:, hh, 1, :].unsqueeze(1).broadcast_to([128, 8, 128])
                            TT("pool", Cb_, in0, in1, ALU.add, [tk + "s"], ["C"])
                            ACT(Eb_, Cb_, AF.Exp, ["C", tk + "st"], ["E"], bias=stt_[:, 16 + hh:17 + hh])
                            if hh == 0:
                                STT("dve", Tt["wtot"], Cb_, Tt["top"][:, hh, 15:16], Eb_, ALU.is_ge, ALU.mult,
                                    ["C", "E", tk + "top"], [tk + "wtot"])
                            else:
                                STT("dve", Wt_, Cb_, Tt["top"][:, hh, 15:16], Eb_, ALU.is_ge, ALU.mult,
                                    ["C", "E", tk + "top"], ["W"])
                                TT("dve", Tt["wtot"], Tt["wtot"], Wt_, ALU.add, [tk + "wtot", "W"], [tk + "wtot"])
                    for e2 in range(2):
                        eb = cr * 2 + e2
                        bi = eb % 2
                        DMA(ublk[bi].rearrange("p k n -> p (k n)"), utb[eb].rearrange("p k n -> p (k n)"), (), [f"ublk{bi}"])
                        DMA(vblk[bi].rearrange("p c d -> p (c d)"), vb[eb], (), [f"vblk{bi}"])
                        for t in range(2):
                            Tt = T[t]
                            tk = f"T{t}"
                            ab = 4 + t
                            for kc in range(8):
                                MM(PSB(ab), Tt["hT"][:, kc, :], ublk[bi][:, kc, :], kc == 0, kc == 7,
                                   [tk + "hT", f"ublk{bi}"], pk(ab))
                            ACT(G[t], PSB(ab), AF.Gelu_apprx_tanh, pk(ab), [f"G{t}"])
                            TT("pool", coef[t], G[t], Tt["wtot"].rearrange("p a b -> p (a b)")[:, e2 * 512:(e2 + 1) * 512], ALU.mult,
                               [f"G{t}", tk + "wtot"], [f"coef{t}"])
                            ptc = PSBb(6 + t)
                            for cc in range(4):
                                TR(ptc[:, cc * 128:(cc + 1) * 128], coef[t][:, cc * 128:(cc + 1) * 128], identb[:],
                                   [f"coef{t}", "identb"], pk(6 + t))
                            CP("act", coefT[t], ptc[:, 0:512].rearrange("p (c n) -> p c n", c=4), pk(6 + t), [f"coefT{t}"])
                            for n in range(2):
                                for cc in range(4):
                                    MM(PSB(2 * t + n), coefT[t][:, cc, :], vblk[bi][:, cc, n * 512:(n + 1) * 512],
                                       eb == 0 and cc == 0, eb == 31 and cc == 3, [f"coefT{t}", f"vblk{bi}"], pk(2 * t + n))
                for t in range(2):
                    Tt = T[t]
                    tk = f"T{t}"
                    row0 = blk * 256 + t * 128
                    TT("dve", outt, PSB(2 * t, 2), rep[2], ALU.mult, pk(2 * t, 2) + ["rep2"], ["outt"])
                    TT("dve", outt, outt, Tt["xn"], ALU.add, ["outt", tk + "xn"], ["outt"])
                    ACT(junk, outt, AF.Square, ["outt"], ["junk", "ssq"], accum=ssq[:, 0:1])
                    ACT(ssq[:, 1:2], ssq[:, 0:1], AF.Ln, ["ssq", "epsc"], ["ssq"], bias=epsc[:, 0:1], scale=1.0 / D)
                    ACT(ssq[:, 2:3], ssq[:, 1:2], AF.Exp, ["ssq"], ["ssq"], scale=-0.5)
                    STT("dve", outt, outt, ssq[:, 2:3], rep[3], ALU.mult, ALU.mult, ["outt", "ssq", "rep3"], ["outt"])
                    DMA(outd[row0:row0 + 128, :], outt, ["outt"], ())
        cnt = S.emit(nc)
    return nc, cnt


def _consts():
    cst = np.zeros((128, 642), np.float32)
    cst[:, 0:128] = np.eye(128, dtype=np.float32)
    j = np.arange(128)[:, None].astype(np.float32)
    i = np.arange(128)[None, :].astype(np.float32)
    cst[:, 128:256] = np.maximum(i - j, 0.0)
    cst[:, 256:384] = np.maximum(j - i, 0.0)
    cst[:, 384:512] = np.broadcast_to(i + 1.0, (128, 128))
    cst[:, 512:640] = np.broadcast_to(128.0 - i, (128, 128))
    cst[:, 640] = 127.0 - np.arange(128)
    cst[:, 641] = np.arange(128)
    return cst


def _rope_table():
    t = np.arange(NTOK)
    row = (t // 64).astype(np.float32)
    col = (t % 64).astype(np.float32)
    inv = (np.float32(10000.0) ** (-np.arange(32, dtype=np.float32) / np.float32(32))).astype(np.float32)
    ar = (row[:, None] * inv[None, :]).astype(np.float32)
    ac = (col[:, None] * inv[None, :]).astype(np.float32)
    cr, sr, cc, sc = np.cos(ar), np.sin(ar), np.cos(ac), np.sin(ac)
    tab = np.concatenate([cr, cr, cc, cc, -sr, sr, -sc, sc], axis=1).astype(np.float32)
    return tab


def _chunkT(v):
    return np.ascontiguousarray(np.asarray(v, np.float32).reshape(8, 128).T)


def make_in_maps(inputs):
    f = np.float32
    x = np.asarray(inputs["x"], f)
    ctx = np.asarray(inputs["ctx"], f)
    c = np.asarray(inputs["c"], f)
    c_ctx = np.asarray(inputs["c_ctx"], f)
    l = 0
    rope = _rope_table()
    rope_rev = np.ascontiguousarray(rope[::-1])
    cst = _consts()
    gains = np.ascontiguousarray(np.stack([inputs["norm1_g"][l], inputs["norm2_g"][l], inputs["final_g"]]).astype(f))
    keys = np.asarray(inputs["peer_keys"][l], f)
    kbd = np.zeros((128, 8, 256), f)
    for h in range(8):
        for p in range(2):
            kbd[p * 64:(p + 1) * 64, h, p * 128:(p + 1) * 128] = keys[h, p].T
    kbd = np.ascontiguousarray(kbd.reshape(128, 2048))
    uT = np.ascontiguousarray(np.asarray(inputs["peer_u"][l], f).T)
    shared = {
        "cst": cst, "gains": gains,
        "mod_b": np.ascontiguousarray(np.asarray(inputs["mod_b"][l], f).reshape(1, -1)),
        "mod_w": np.ascontiguousarray(np.asarray(inputs["mod_w"][l], f)),
        "w_in": np.ascontiguousarray(np.asarray(inputs["w_in"][l], f)),
        "w_ret_out": np.ascontiguousarray(np.asarray(inputs["w_ret_out"][l], f)),
        "w_lru_out": np.ascontiguousarray(np.asarray(inputs["w_lru_out"][l], f)),
        "w_out": np.ascontiguousarray(np.asarray(inputs["w_out"][l], f)),
        "peer_wq": np.ascontiguousarray(np.asarray(inputs["peer_wq"][l], f)),
        "keysbd": kbd, "peer_uT": uT,
        "peer_v": np.ascontiguousarray(np.asarray(inputs["peer_v"][l], f)),
    }
    cw = np.asarray(inputs["conv_w"][l], f)
    maps = []
    for core in range(8):
        b, s = core // 2, core % 2
        dirs = (0, 1) if s == 0 else (1, 0)
        m = dict(shared)
        if s == 0:
            m["xf"] = np.ascontiguousarray(x[b])
            m["ctxf"] = np.ascontiguousarray(ctx[b])
            m["rope"] = rope
            w5 = np.concatenate([cw, np.zeros((1, D), f)], axis=0)
        else:
            m["xf"] = np.ascontiguousarray(x[b, ::-1])
            m["ctxf"] = np.ascontiguousarray(ctx[b, ::-1])
            m["rope"] = rope_rev
            w5 = np.concatenate([np.zeros((1, D), f), cw[::-1]], axis=0)
        par = np.zeros((128, 112), f)
        cv = np.stack([_chunkT(c[b]), _chunkT(c_ctx)], axis=2)
        par[:, 0:16] = cv.reshape(128, 16)
        lam = np.asarray(inputs["lru_lambda"][l], f)
        lba = np.asarray(inputs["lru_ba"][l], f)
        lbx = np.asarray(inputs["lru_bx"][l], f)
        par[:, 16:32] = np.concatenate([_chunkT(lam[d]) for d in dirs], axis=1)
        par[:, 32:48] = np.concatenate([_chunkT(lba[d]) for d in dirs], axis=1)
        par[:, 48:64] = np.concatenate([_chunkT(lbx[d]) for d in dirs], axis=1)
        par[:, 64:104] = np.concatenate([_chunkT(w5[o]) for o in range(5)], axis=1)
        par[:, 104:112] = _chunkT(inputs["conv_b"][l])
        m["par"] = par
        rd = np.asarray(inputs["ret_decay"][l], f)
        m["dec"] = np.ascontiguousarray(np.concatenate([rd[dirs[0]], rd[dirs[1]]]).reshape(1, 8))
        wa = np.asarray(inputs["lru_wa"][l], f)
        wx = np.asarray(inputs["lru_wx"][l], f)
        gw = np.stack([wa[dirs[0]], wx[dirs[0]], wa[dirs[1]], wx[dirs[1]]], axis=0)
        m["gatew"] = np.ascontiguousarray(gw.transpose(2, 0, 1, 3).reshape(128, 4 * 8 * 128))
        maps.append(m)
    return maps


_CACHE = {}


def kernel(**inputs):
    if "nc" not in _CACHE:
        _CACHE["nc"] = build()[0]
    nc = _CACHE["nc"]
    maps = make_in_maps(inputs)
    res = run_bass_kernel_spmd(nc, maps, core_ids=list(range(8)))
    out = np.zeros((4, NTOK, D), np.float32)
    for core in range(8):
        b, s = core // 2, core % 2
        o = np.asarray(res.results[core]["out"], np.float32)
        if s == 0:
            out[b, 0:OWN] = o
        else:
            out[b, OWN:NTOK] = o[::-1]
    return out
```

```python
import math
from contextlib import ExitStack

import numpy as np
import concourse.bass as bass
import concourse.mybir as mybir
from concourse.bass_utils import run_bass_kernel_spmd

F32 = mybir.dt.float32
BF16 = mybir.dt.bfloat16
ALU = mybir.AluOpType
AF = mybir.ActivationFunctionType

ENG = ("pe", "act", "dve", "pool", "sp")
NDMASEM = 12


class _Op:
    __slots__ = ("eng", "fn", "deps", "is_dma", "sig", "dsem", "dval", "dprev")

    def __init__(self, eng, fn, is_dma):
        self.eng = eng
        self.fn = fn
        self.deps = []
        self.is_dma = is_dma
        self.sig = None
        self.dsem = None
        self.dval = None
        self.dprev = None


class Sched:
    def __init__(self):
        self.ops = []
        self.lastw = {}
        self.readers = {}
        self.ndma = 0
        self.dsem_last = [None] * NDMASEM
        self.last_eng = {}

    def add(self, eng, fn, reads=(), writes=(), dma=False):
        op = _Op(eng, fn, dma)
        deps = []
        for k in reads:
            w = self.lastw.get(k)
            if w is not None:
                deps.append(w)
        for k in writes:
            w = self.lastw.get(k)
            if w is not None:
                deps.append(w)
            deps.extend(self.readers.get(k, ()))
        for k in reads:
            lst = self.readers.setdefault(k, [])
            if not dma:
                lst[:] = [o for o in lst if o.is_dma or o.eng != eng]
            lst.append(op)
        for k in writes:
            self.lastw[k] = op
            self.readers[k] = []
        if dma:
            s = self.ndma % NDMASEM
            self.ndma += 1
            op.dsem = s
            prev = self.dsem_last[s]
            op.dprev = prev
            op.dval = 16 if prev is None else prev.dval + 16
            self.dsem_last[s] = op
        seen = set()
        for d in deps:
            if id(d) in seen or d is op:
                continue
            seen.add(id(d))
            if d.is_dma or d.eng != eng or eng != "pe":
                op.deps.append(d)
        self.ops.append(op)
        if not dma:
            self.last_eng[eng] = op
        return op

    def pe(self, fn, r=(), w=()):
        return self.add("pe", fn, r, w)

    def act(self, fn, r=(), w=()):
        return self.add("act", fn, r, w)

    def dve(self, fn, r=(), w=()):
        return self.add("dve", fn, r, w)

    def pool(self, fn, r=(), w=()):
        return self.add("pool", fn, r, w)

    def dma(self, fn, r=(), w=()):
        return self.add("sp", fn, r, w, dma=True)

    def barrier(self):
        lasts = [o for o in self.last_eng.values()] + [o for o in self.dsem_last if o is not None]
        for e in ENG:
            op = _Op(e, None, False)
            op.deps = [d for d in lasts]
            self.ops.append(op)
        self.lastw.clear()
        self.readers.clear()

    def emit(self, nc, EPOCH=20000):
        need = set()
        for op in self.ops:
            for d in op.deps:
                if not d.is_dma:
                    need.add(id(d))
        cnt = {e: 0 for e in ENG}
        for op in self.ops:
            if not op.is_dma and op.fn is not None and id(op) in need:
                c = cnt[op.eng]
                op.sig = (c // EPOCH, c % EPOCH + 1)
                cnt[op.eng] = c + 1
        nep = {e: (cnt[e] + EPOCH - 1) // EPOCH for e in ENG}
        with ExitStack() as st:
            esem = {e: [st.enter_context(nc.semaphore(f"s_{e}_{i}")) for i in range(nep[e])] for e in ENG}
            dsem = [st.enter_context(nc.semaphore(f"s_dma_{i}")) for i in range(NDMASEM)]
            block = st.enter_context(nc.Block())
            per = {e: [o for o in self.ops if o.eng == e] for e in ENG}

            def run(e, eng):
                waited = {}
                for op in per[e]:
                    wants = {}
                    for d in op.deps:
                        if d.is_dma:
                            key = ("d", d.dsem)
                            val = d.dval
                            sem = dsem[d.dsem]
                        else:
                            if d.sig is None:
                                continue
                            ep, val = d.sig
                            key = (d.eng, ep)
                            sem = esem[d.eng][ep]
                        if wants.get(key, (None, 0))[1] < val:
                            wants[key] = (sem, val)
                    if op.is_dma and op.dprev is not None:
                        key = ("d", op.dsem)
                        if wants.get(key, (None, 0))[1] < op.dprev.dval:
                            wants[key] = (dsem[op.dsem], op.dprev.dval)
                    for key, (sem, val) in wants.items():
                        if waited.get(key, 0) >= val:
                            continue
                        waited[key] = val
                        eng.wait_ge(sem, val)
                    if op.fn is None:
                        continue
                    ins = op.fn(eng)
                    if op.is_dma:
                        ins.then_inc(dsem[op.dsem], 16)
                    elif op.sig is not None:
                        ins.then_inc(esem[e][op.sig[0]], 1)
                if e == "sp":
                    for s in range(NDMASEM):
                        last = self.dsem_last[s]
                        if last is not None and waited.get(("d", s), 0) < last.dval:
                            eng.wait_ge(dsem[s], last.dval)

            @block.tensor
            def _(eng):
                run("pe", eng)

            @block.scalar
            def _(eng):
                run("act", eng)

            @block.vector
            def _(eng):
                run("dve", eng)

            @block.gpsimd
            def _(eng):
                run("pool", eng)

            @block.sync
            def _(eng):
                run("sp", eng)
        return cnt


D = 1024
NTOK = 8192
OWN = 4096
NCH = 64
NOWN = 32
EPS = 1e-6
KSCALE = 128 ** -0.5
ARENA_W = 45056
CQ, CK, CV, CP3, CLX, CP5, CP6, CP7 = 0, 512, 1024, 2048, 3072, 4096, 5120, 6144


def _prod(s):
    n = 1
    for v in s:
        n *= v
    return n


class Arena:
    def __init__(self, t, words):
        self.t = t
        self.n = words
        self.off = 0

    def reset(self):
        self.off = 0

    def _shape(self, v, shape):
        if len(shape) == 1:
            return v
        names = "abcdefg"[: len(shape)]
        pat = "p (" + " ".join(names) + ") -> p " + " ".join(names)
        return v.rearrange(pat, **{n: s for n, s in zip(names, shape)})

    def f32(self, *shape, parts=128):
        n = _prod(shape)
        assert self.off + n <= self.n, ("arena overflow", self.off, n)
        v = self.t[0:parts, self.off:self.off + n]
        self.off += n
        return self._shape(v, shape)

    def bf16(self, *shape, parts=128):
        n = _prod(shape)
        w = (n + 1) // 2
        assert self.off + w <= self.n, ("arena overflow", self.off, w)
        v = self.t[0:parts, self.off:self.off + w].bitcast(BF16)[:, 0:n]
        self.off += w
        return self._shape(v, shape)


def _rev(ap2d):
    dims = ap2d.ap
    pstep, pcnt = dims[0]
    n = dims[-1][1]
    assert len(dims) == 2 and dims[-1][0] == 1
    return bass.AP(ap2d.tensor, ap2d.offset + n - 1, [[pstep, pcnt], [-1, n]])


def build(debug=None, phases="all"):
    nc = bass.Bass("TRN2", target_bir_lowering=False)
    S = Sched()

    def din(name, shape, dt=F32):
        return nc.dram_tensor(name, list(shape), dt, kind="ExternalInput")

    xf_t = din("xf", [NTOK, D])
    ctx_t = din("ctxf", [256, D])
    rope_t = din("rope", [NTOK, 256])
    cst_t = din("cst", [128, 642])
    par_t = din("par", [128, 112])
    dec_t = din("dec", [1, 8])
    gains_t = din("gains", [3, D])
    modb_t = din("mod_b", [1, 6 * D])
    modw_t = din("mod_w", [D, 6 * D])
    win_t = din("w_in", [D, 7168])
    wro_t = din("w_ret_out", [D, D])
    wlo_t = din("w_lru_out", [D, D])
    wo_t = din("w_out", [D, D])
    wq_t = din("peer_wq", [D, D])
    keys_t = din("keysbd", [128, 8 * 256])
    gw_t = din("gatew", [128, 4 * 8 * 128])
    ut_t = din("peer_uT", [D, 16384])
    v_t = din("peer_v", [16384, D])
    out_t = nc.dram_tensor("out", [OWN, D], F32, kind="ExternalOutput")
    modd_t = nc.dram_tensor("modd", [2, 6 * D], F32, kind="Internal")
    utb_t = nc.dram_tensor("utb", [32, 128, 8, 512], BF16, kind="Internal")
    vb_t = nc.dram_tensor("vb", [32, 128, 4096], BF16, kind="Internal")
    sbs_t = nc.dram_tensor("sbs", [NOWN, 128, 1024], BF16, kind="Internal")
    ret_t = nc.dram_tensor("retsc", [OWN, D], F32, kind="Internal")
    lru_t = nc.dram_tensor("lrusc", [OWN, D], F32, kind="Internal")
    xn_t = nc.dram_tensor("xnsc", [OWN, D], F32, kind="Internal")
    dbg = {}
    for nm in (debug.split(",") if debug else []):
        dbg[nm] = nc.dram_tensor("dbg_" + nm, [OWN, D], F32, kind="ExternalOutput")

    xf = xf_t.ap()
    ctxf = ctx_t.ap()
    rope = rope_t.ap()

    with ExitStack() as st:
        def sbt(name, shape, dt=F32):
            return st.enter_context(nc.sbuf_tensor(name, list(shape), dt))

        arena_t = sbt("arena", [128, ARENA_W])
        AR = Arena(arena_t, ARENA_W)
        ps = st.enter_context(nc.psum_tensor("ps", [128, 4096], F32))
        cst = sbt("cstsb", [128, 642])
        identb = sbt("identb", [128, 128], BF16)
        par = sbt("parsb", [128, 112])
        lg = sbt("lg", [128, 8])
        dct = sbt("dct", [128, 4, 128])
        xiA = sbt("xiA", [128, 4, 128])
        xiB = sbt("xiB", [128, 4, 128])
        zeta = sbt("zeta", [128, 8])
        decay = sbt("decay", [128, 8])
        nsp8 = sbt("nsp8", [128, 16])
        hprev = sbt("hprev", [128, 2, 8])
        hbinit = sbt("hbinit", [128, NOWN, 8])
        SA = sbt("SA", [128, 4, 256])
        SB = sbt("SB", [128, 4, 256])
        gwb = sbt("gwb", [128, 4, 8, 128], BF16)
        small = sbt("small", [128, 64])

        ident = cst[:, 0:128]
        Pm = cst[:, 128:256]
        Nm = cst[:, 256:384]
        rampI1 = cst[:, 384:512]
        rampI2 = cst[:, 512:640]
        rampJ = cst[:, 640:641]
        rampJ2 = cst[:, 641:642]
        cvec = par[:, 0:16].rearrange("p (k j) -> p k j", j=2)
        lam = par[:, 16:32]
        ba = par[:, 32:48].rearrange("p (d b) -> p d b", d=2)
        bx = par[:, 48:64].rearrange("p (d b) -> p d b", d=2)
        w5 = par[:, 64:104].rearrange("p (o b) -> p o b", o=5)
        cb = par[:, 104:112]

        def PSB(b, nb=1):
            return ps[:, b * 512:(b + nb) * 512]

        def PSBb(b):
            return ps[:, b * 512:(b + 1) * 512].bitcast(BF16)

        def pk(b, nb=1):
            return [f"ps{i}" for i in range(b, b + nb)]

        def DMA(out, in_, r=(), w=()):
            S.dma(lambda e: e.dma_start(out=out, in_=in_), r, w)

        def MM(out, lhsT, rhs, start, stop, r, w):
            S.pe(lambda e: e.matmul(out, lhsT=lhsT, rhs=rhs, start=start, stop=stop), r, w)

        def TR(out, in_, idn, r, w):
            S.pe(lambda e: e.transpose(out=out, in_=in_, identity=idn), r, w)

        def ACT(out, in_, func, r, w, bias=None, scale=None, accum=None):
            kw = {}
            if bias is not None:
                kw["bias"] = bias
            if scale is not None:
                kw["scale"] = scale
            if accum is not None:
                kw["accum_out"] = accum
            S.act(lambda e: e.activation(out=out, in_=in_, func=func, **kw), r, w)

        def TT(eng, out, in0, in1, op, r, w):
            S.add(eng, lambda e: e.tensor_tensor(out=out, in0=in0, in1=in1, op=op), r, w)

        def TS(eng, out, in0, s1, s2, op0, op1, r, w):
            if s2 is None:
                S.add(eng, lambda e: e.tensor_scalar(out=out, in0=in0, scalar1=s1, scalar2=None, op0=op0), r, w)
            else:
                S.add(eng, lambda e: e.tensor_scalar(out=out, in0=in0, scalar1=s1, scalar2=s2, op0=op0, op1=op1), r, w)

        def STT(eng, out, in0, scalar, in1, op0, op1, r, w):
            S.add(eng, lambda e: e.scalar_tensor_tensor(out=out, in0=in0, scalar=scalar, in1=in1, op0=op0, op1=op1), r, w)

        def CP(eng, out, in_, r, w):
            if eng == "act":
                S.act(lambda e: e.copy(out=out, in_=in_), r, w)
            else:
                S.add(eng, lambda e: e.tensor_copy(out=out, in_=in_), r, w)

        def MEMSET(eng, out, val, w):
            S.add(eng, lambda e: e.memset(out, val), (), w)

        rr = [0]

        def rot3():
            rr[0] += 1
            return ("dve", "pool", "act")[rr[0] % 3]

        def bc_last(ap, n):
            shp = list(ap.shape)
            shp[-1] = n
            return ap.broadcast_to(shp)

        def wload(dst, src2d, ncols, stg, key, piece=256, stgkeys=("stg0", "stg1")):
            npc = ncols // piece
            for i in range(npc):
                sb = stg[i % 2]
                sk = stgkeys[i % 2]
                DMA(sb[:, :, 0:piece], src2d[:, i * piece:(i + 1) * piece].rearrange("(k p) n -> p k n", p=128), (), [sk])
                CP(rot3(), dst[:, :, i * piece:(i + 1) * piece], sb[:, :, 0:piece], [sk], [key])

        def rep_load(dst, row_ap, key):
            DMA(dst, row_ap.partition_broadcast(128), (), [key])

        modd = modd_t.ap()
        gains = gains_t.ap()

        def modrow(r, i):
            return modd[r:r + 1, i * D:(i + 1) * D]

        def make_A(dst, tmp, r, i_scale, gain_idx, key, tkey):
            rep_load(dst, modrow(r, i_scale), key)
            rep_load(tmp, gains[gain_idx:gain_idx + 1, :], tkey)
            STT("dve", dst, dst, 1.0, tmp, ALU.add, ALU.mult, [key, tkey], [key])

        epsc = small[:, 8:9]
        MEMSET("dve", epsc, EPS, ["epsc"])
        DMA(cst[:], cst_t.ap(), (), ["cst"])
        DMA(par[:], par_t.ap(), (), ["par"])
        DMA(lg[:], dec_t.ap().partition_broadcast(128), (), ["lg"])
        CP("dve", identb[:], ident, ["cst"], ["identb"])
        ACT(lg[:], lg[:], AF.Exp, ["lg"], ["lg"], scale=-1.0)
        ACT(lg[:], lg[:], AF.Ln, ["lg"], ["lg"], bias=1.0)
        TS("dve", lg[:], lg[:], -1.0, None, ALU.mult, None, ["lg"], ["lg"])
        lnks = math.log(KSCALE)
        tmpm = small[:, 0:1]
        for h in range(4):
            t0 = AR.f32(128)
            TS("dve", t0, Pm, lg[:, h:h + 1], None, ALU.mult, None, ["cst", "lg"], [f"t0_{h}"])
            STT("dve", t0, Nm, lg[:, 4 + h:5 + h], t0, ALU.mult, ALU.add, ["cst", "lg", f"t0_{h}"], [f"t0_{h}"])
            TS("dve", t0, t0, lnks, None, ALU.add, None, [f"t0_{h}"], [f"t0_{h}"])
            ACT(dct[:, h, :], t0, AF.Exp, [f"t0_{h}"], ["dct"])
            ACT(xiA[:, h, :], rampI1, AF.Exp, ["cst", "lg"], ["xiA"], scale=lg[:, h:h + 1])
            ACT(xiB[:, h, :], rampI2, AF.Exp, ["cst", "lg"], ["xiB"], scale=lg[:, 4 + h:5 + h])
            t1 = AR.f32(2)
            TS("dve", t1[:, 0:1], rampJ, lg[:, h:h + 1], lnks, ALU.mult, ALU.add, ["cst", "lg"], [f"t1_{h}"])
            TS("dve", t1[:, 1:2], rampJ2, lg[:, 4 + h:5 + h], lnks, ALU.mult, ALU.add, ["cst", "lg"], [f"t1_{h}"])
            ACT(zeta[:, h:h + 1], t1[:, 0:1], AF.Exp, [f"t1_{h}"], ["zeta"])
            ACT(zeta[:, 4 + h:5 + h], t1[:, 1:2], AF.Exp, [f"t1_{h}"], ["zeta"])
        ACT(decay[:], lg[:], AF.Exp, ["lg"], ["decay"], scale=128.0)
        ACT(nsp8[:], lam, AF.Exp, ["par"], ["nsp8"], scale=-1.0)
        ACT(nsp8[:], nsp8[:], AF.Ln, ["nsp8"], ["nsp8"], bias=1.0)
        TS("dve", nsp8[:], nsp8[:], -8.0, None, ALU.mult, None, ["nsp8"], ["nsp8"])
        nsp8v = nsp8[:].rearrange("p (d b) -> p d b", d=2)
        gst = AR.f32(4 * 8 * 128)
        DMA(gst, gw_t.ap(), (), ["gst"])
        CP("pool", gwb[:].rearrange("p a b c -> p (a b c)"), gst, ["gst"], ["gwb"])
        cvs = AR.f32(8, 2)
        ACT(cvs, cvec, AF.Silu, ["par"], ["cvs"])
        mwb = [AR.f32(8, 512), AR.f32(8, 512)]
        mb2 = AR.f32(6 * D, parts=2)
        DMA(mb2[0:1, :], modb_t.ap(), (), ["mb2"])
        DMA(mb2[1:2, :], modb_t.ap(), (), ["mb2"])
        msb = AR.f32(6 * D, parts=2)
        for n in range(12):
            mw = mwb[n % 2]
            DMA(mw, modw_t.ap()[:, n * 512:(n + 1) * 512].rearrange("(k p) n -> p k n", p=128), (), [f"mw{n % 2}"])
            for kc in range(8):
                MM(ps[0:2, (n % 2) * 512:(n % 2) * 512 + 512], cvs[:, kc, :], mw[:, kc, :], kc == 0, kc == 7,
                   ["cvs", f"mw{n % 2}"], [f"ps{n % 2}"])
            TT("dve", msb[:, n * 512:(n + 1) * 512], ps[0:2, (n % 2) * 512:(n % 2) * 512 + 512],
               mb2[:, n * 512:(n + 1) * 512], ALU.add, [f"ps{n % 2}", "mb2"], ["msb"])
        DMA(modd, msb, ["msb"], ["modd"])
        S.barrier()
        AR.reset()

        if phases in ("all", "peer"):
            sf = [AR.f32(4096), AR.f32(4096)]
            sbf = [AR.bf16(4096), AR.bf16(4096)]
            utb = utb_t.ap()
            vb = vb_t.ap()
            i = 0
            for kc in range(8):
                for eq in range(4):
                    b = i % 2
                    DMA(sf[b], ut_t.ap()[kc * 128:(kc + 1) * 128, eq * 4096:(eq + 1) * 4096], (), [f"sf{b}"])
                    CP(rot3(), sbf[b], sf[b], [f"sf{b}"], [f"sbf{b}"])
                    DMA(utb[eq * 8:(eq + 1) * 8, :, kc, :].rearrange("e p n -> p e n"),
                        sbf[b].rearrange("p (e n) -> p e n", e=8), [f"sbf{b}"], ())
                    i += 1
            for eb in range(32):
                b = i % 2
                DMA(sf[b].rearrange("p (c d) -> p c d", c=4),
                    v_t.ap()[eb * 512:(eb + 1) * 512, :].rearrange("(c p) d -> p c d", p=128), (), [f"sf{b}"])
                CP(rot3(), sbf[b], sf[b], [f"sf{b}"], [f"sbf{b}"])
                DMA(vb[eb], sbf[b], [f"sbf{b}"], ())
                i += 1
            S.barrier()
            AR.reset()

        def front(xt, xk, Arep, shrep, repkeys, hx, junk, hxT_dst, hxTkey, psbank, ssq, parts=128, ncol=128):
            ACT(junk[0:parts, :], xt, AF.Square, [xk], ["junk", "ssq"], accum=ssq[0:parts, 0:1])
            ACT(ssq[0:parts, 1:2], ssq[0:parts, 0:1], AF.Ln, ["ssq", "epsc"], ["ssq"], bias=epsc[0:parts, 0:1], scale=1.0 / D)
            ACT(ssq[0:parts, 2:3], ssq[0:parts, 1:2], AF.Exp, ["ssq"], ["ssq"], scale=-0.5)
            STT("dve", junk[0:parts, :], xt, ssq[0:parts, 2:3], Arep[0:parts, :], ALU.mult, ALU.mult,
                [xk, "ssq"] + repkeys, ["junk"])
            TT("pool", hx[0:parts, :], junk[0:parts, :], shrep[0:parts, :], ALU.add, ["junk"] + repkeys, ["hx"])
            pt = PSBb(psbank)
            for kc in range(8):
                TR(pt[:, kc * ncol:(kc + 1) * ncol], hx[0:parts, kc * 128:(kc + 1) * 128], identb[0:parts, 0:parts],
                   ["hx", "identb"], pk(psbank))
            CP("act", hxT_dst, pt[:, 0:8 * ncol].rearrange("p (k n) -> p k n", k=8), pk(psbank), [hxTkey])

        def load_phase_common(need_halo):
            d = {}
            d["stg"] = [AR.f32(8, 256), AR.f32(8, 256)]
            d["x"] = [AR.f32(D), AR.f32(D)]
            d["junk"] = AR.f32(D)
            d["hx"] = AR.bf16(D)
            d["ssq"] = AR.f32(4)
            if need_halo:
                d["xh"] = AR.f32(D)
                d["hxh"] = AR.bf16(D)
                d["ssqh"] = AR.f32(4)
                d["junkh"] = AR.f32(D)
                d["hxT"] = AR.bf16(8, 132)
            else:
                d["hxT"] = AR.bf16(8, 128)
            return d

        def lru_gates(W, dr, uc, ucb, hkey_prefix):
            r_, i_, la, a2, bt = W["r"], W["i"], W["la"], W["a2"], W["bt"]
            for blk in range(8):
                MM(PSB(4, 2)[:, blk * 128:(blk + 1) * 128], gwb[:, 2 * dr, blk, :], ucb[:, blk, :], True, True,
                   ["gwb", "ucb"], pk(4, 2))
            for blk in range(8):
                MM(PSB(6, 2)[:, blk * 128:(blk + 1) * 128], gwb[:, 2 * dr + 1, blk, :], ucb[:, blk, :], True, True,
                   ["gwb", "ucb"], pk(6, 2))
            TT("dve", r_, PSB(4, 2).rearrange("p (b t) -> p b t", b=8), bc_last(ba[:, dr, :].unsqueeze(2), 128), ALU.add,
               pk(4, 2) + ["par"], ["r"])
            TT("dve", i_, PSB(6, 2).rearrange("p (b t) -> p b t", b=8), bc_last(bx[:, dr, :].unsqueeze(2), 128), ALU.add,
               pk(6, 2) + ["par"], ["i"])
            ACT(r_, r_, AF.Sigmoid, ["r"], ["r"])
            ACT(i_, i_, AF.Sigmoid, ["i"], ["i"])
            TT("dve", la, r_, bc_last(nsp8v[:, dr, :].unsqueeze(2), 128), ALU.mult, ["r", "nsp8"], ["la"])
            ACT(a2, la, AF.Exp, ["la"], ["a2"], scale=2.0)
            ACT(la, la, AF.Exp, ["la"], ["la"])
            TS("dve", a2, a2, -1.0, 1.0, ALU.mult, ALU.add, ["a2"], ["a2"])
            ACT(a2, a2, AF.Sqrt, ["a2"], ["a2"])
            TT("pool", bt, i_, uc, ALU.mult, ["i", "uc"], ["bt"])
            TT("pool", bt, bt, a2, ALU.mult, ["bt", "a2"], ["bt"])

        def lru_scan(W, dr, hout, hkey):
            la, bt = W["la"], W["bt"]
            first = 0 if dr == 0 else 127
            last = 127 if dr == 0 else 0
            tmp8 = W["tmp8"]
            TT("dve", tmp8, la[:, :, first], hprev[:, dr, :], ALU.mult, ["la", "hprev"], ["tmp8"])
            TT("dve", bt[:, :, first], bt[:, :, first], tmp8, ALU.add, ["bt", "tmp8"], ["bt"])
            MEMSET("dve", la[:, :, first], 0.0, ["la"])
            la2 = la.rearrange("p b t -> p (b t)")
            bt2 = bt.rearrange("p b t -> p (b t)")
            h2 = hout.rearrange("p b t -> p (b t)")
            if dr == 0:
                S.dve(lambda e: e.tensor_tensor_scan(out=h2, data0=la2, data1=bt2, initial=0.0, op0=ALU.mult, op1=ALU.add),
                      ["la", "bt"], [hkey])
            else:
                ro, ra, rb = _rev(h2), _rev(la2), _rev(bt2)
                S.dve(lambda e: e.tensor_tensor_scan(out=ro, data0=ra, data1=rb, initial=0.0, op0=ALU.mult, op1=ALU.add),
                      ["la", "bt"], [hkey])
            CP("dve", hprev[:, dr, :], hout[:, :, last], [hkey], ["hprev"])

        def conv(W, uT, uc, ucb):
            m0, m1 = W["m0"], W["m1"]
            TT("dve", uc, uT[:, :, 0:128], bc_last(w5[:, 0, :].unsqueeze(2), 128), ALU.mult, ["uT", "par"], ["uc"])
            TT("dve", uc, uc, bc_last(cb.unsqueeze(2), 128), ALU.add, ["uc", "par"], ["uc"])
            for o in range(1, 5):
                m = m0 if o % 2 else m1
                mk = "m0" if o % 2 else "m1"
                TT("pool", m, uT[:, :, o:o + 128], bc_last(w5[:, o, :].unsqueeze(2), 128), ALU.mult, ["uT", "par"], [mk])
                TT("dve", uc, uc, m, ALU.add, ["uc", mk], ["uc"])
            CP("act", ucb, uc, ["uc"], ["ucb"])

        def halo_front(W, src, j, nchunks, Arep, shrep, repkeys, psbank):
            xh, hxh = W["xh"], W["hxh"]
            lo = j * 128 - 2
            hi = j * 128 + 128
            lo_c = max(lo, 0)
            hi_c = min(hi, nchunks * 128 - 2)
            DMA(xh[0:2, :], src[lo_c:lo_c + 2, :], (), ["xh"])
            DMA(xh[2:4, :], src[hi_c:hi_c + 2, :], (), ["xh"])
            front(xh[0:4, :], "xh", Arep, shrep, repkeys, hxh, W["junkh"], W["hxT4"], "hxT4", psbank, W["ssqh"], parts=4, ncol=4)
            CP("dve", W["hxT"][:, :, 0:2], W["hxT4"][:, :, 0:2], ["hxT4"], ["hxT"])
            CP("dve", W["hxT"][:, :, 130:132], W["hxT4"][:, :, 2:4], ["hxT4"], ["hxT"])

        def proj_uT(W, wlx, j, nchunks):
            hxT, uT = W["hxT"], W["uT"]
            pu = PSB(4, 4).rearrange("p (b t) -> p b t", b=8)
            for blk in range(8):
                for kc in range(8):
                    MM(pu[:, blk, 0:132], wlx[:, kc, blk * 128:(blk + 1) * 128], hxT[:, kc, :], kc == 0, kc == 7,
                       ["wlx", "hxT"], pk(4, 4))
            CP("act", uT, pu[:, :, 0:132], pk(4, 4), ["uT"])
            if j == 0:
                MEMSET("dve", uT[:, :, 0:2], 0.0, ["uT"])
            if j == nchunks - 1:
                MEMSET("dve", uT[:, :, 130:132], 0.0, ["uT"])

        def rope_tok(W, src_ps, srckeys, rp, rpk, dst, dstkey):
            t1, t2 = W["t1"], W["t2"]
            s4 = src_ps.rearrange("p (h c) -> p h c", h=4)
            Cb = rp[:, 0:128].unsqueeze(1).broadcast_to([128, 4, 128])
            TT("dve", t1, s4, Cb, ALU.mult, srckeys + [rpk], ["t1"])
            s5 = src_ps.rearrange("p (h a b c) -> p h a b c", h=4, a=2, b=2)
            t25 = t2.rearrange("p h (a b c) -> p h a b c", a=2, b=2)
            sg5 = rp[:, 128:256].rearrange("p (a b c) -> p a b c", a=2, b=2)
            TT("dve", t25[:, :, :, 0, :], s5[:, :, :, 1, :], sg5[:, :, 0, :].unsqueeze(1).broadcast_to([128, 4, 2, 32]),
               ALU.mult, srckeys + [rpk], ["t2"])
            TT("dve", t25[:, :, :, 1, :], s5[:, :, :, 0, :], sg5[:, :, 1, :].unsqueeze(1).broadcast_to([128, 4, 2, 32]),
               ALU.mult, srckeys + [rpk], ["t2"])
            TT("pool", t1, t1, t2, ALU.add, ["t1", "t2"], ["t1"])

        sbs = sbs_t.ap()

        if phases in ("all", "mix"):
            W = load_phase_common(True)
            W["hxT4"] = AR.bf16(8, 4)
            wk = AR.bf16(8, 512)
            wv = AR.bf16(8, 1024)
            wlx = AR.bf16(8, 1024)
            rep = [AR.f32(D) for _ in range(4)]
            reptmp = W["junk"]
            W["rp"] = [AR.f32(256), AR.f32(256)]
            W["t1"] = AR.f32(4, 128)
            W["t2"] = AR.f32(4, 128)
            kz = AR.bf16(4, 128)
            vbf = AR.bf16(4, 256)
            sbf_ = AR.bf16(4, 256)
            W["uT"] = AR.f32(8, 132)
            W["m0"] = AR.f32(8, 128)
            W["m1"] = AR.f32(8, 128)
            uc = AR.f32(8, 128)
            ucb = AR.bf16(8, 128)
            for nm in ("r", "i", "la", "a2", "bt"):
                W[nm] = AR.f32(8, 128)
            hh_ = AR.f32(8, 128)
            W["tmp8"] = AR.f32(8)
            wload(wk, win_t.ap()[:, CK:CK + 512], 512, W["stg"], "wk")
            wload(wv, win_t.ap()[:, CV:CV + 1024], 1024, W["stg"], "wv")
            wload(wlx, win_t.ap()[:, CLX:CLX + 1024], 1024, W["stg"], "wlx")
            make_A(rep[0], reptmp, 0, 1, 0, "rep0", "junk")
            rep_load(rep[1], modrow(0, 0), "rep1")
            make_A(rep[2], reptmp, 1, 1, 0, "rep2", "junk")
            rep_load(rep[3], modrow(1, 0), "rep3")
            MEMSET("dve", SA[:], 0.0, ["SA"])
            MEMSET("dve", SB[:], 0.0, ["SB"])
            MEMSET("dve", hprev[:], 0.0, ["hprev"])

            def p1_chunk(src, nchunks, j, dr, is_ctx, xi, store_idx):
                Arep, shrep, rk = (rep[2], rep[3], ["rep2", "rep3"]) if is_ctx else (rep[0], rep[1], ["rep0", "rep1"])
                xt = W["x"][xi]
                xk = f"x{xi}"
                DMA(xt, src[j * 128:(j + 1) * 128, :], (), [xk])
                if not is_ctx:
                    DMA(W["rp"][xi], rope[j * 128:(j + 1) * 128, :], (), [f"rp{xi}"])
                front(xt, xk, Arep, shrep, rk, W["hx"], W["junk"], W["hxT"][:, :, 2:130], "hxT", 0, W["ssq"])
                halo_front(W, src, j, nchunks, Arep, shrep, rk, 0)
                for kc in range(8):
                    MM(PSB(1), W["hxT"][:, kc, 2:130], wk[:, kc, :], kc == 0, kc == 7, ["hxT", "wk"], pk(1))
                for n in range(2):
                    for kc in range(8):
                        MM(PSB(2 + n), W["hxT"][:, kc, 2:130], wv[:, kc, n * 512:(n + 1) * 512], kc == 0, kc == 7,
                           ["hxT", "wv"], pk(2 + n))
                Sd = SA if dr == 0 else SB
                Sk = "SA" if dr == 0 else "SB"
                zt = zeta[:, 4 * dr:4 * dr + 4]
                if is_ctx:
                    TT("dve", kz, PSB(1).rearrange("p (h c) -> p h c", h=4), bc_last(zt.unsqueeze(2), 128), ALU.mult,
                       pk(1) + ["zeta"], ["kz"])
                else:
                    rope_tok(W, PSB(1), pk(1), W["rp"][xi], f"rp{xi}", None, None)
                    TT("dve", kz, W["t1"], bc_last(zt.unsqueeze(2), 128), ALU.mult, ["t1", "zeta"], ["kz"])
                CP("act", vbf, PSB(2, 2).rearrange("p (h c) -> p h c", h=4), pk(2, 2), ["vbf"])
                if store_idx is not None:
                    CP("pool", sbf_, Sd[:], [Sk], ["sbf_"])
                    DMA(sbs[store_idx], sbf_.rearrange("p h c -> p (h c)"), ["sbf_"], ())
                    CP("dve", hbinit[:, store_idx, :], hprev[:, 1, :], ["hprev"], ["hbinit"])
                for hh in range(4):
                    MM(PSB(2, 2)[:, hh * 256:(hh + 1) * 256], kz[:, hh, :], vbf[:, hh, :], True, True, ["kz", "vbf"], pk(2, 2))
                for hh in range(4):
                    STT("dve", Sd[:, hh, :], Sd[:, hh, :], decay[:, 4 * dr + hh:4 * dr + hh + 1], PSB(2, 2)[:, hh * 256:(hh + 1) * 256],
                        ALU.mult, ALU.add, [Sk, "decay"] + pk(2, 2), [Sk])
                proj_uT(W, wlx, j, nchunks)
                conv(W, W["uT"], uc, ucb)
                lru_gates(W, dr, uc, ucb, "h")
                lru_scan(W, dr, hh_, "hh")

            xi = 0
            for (j, dr) in ((0, 0), (1, 0), (1, 1), (0, 1)):
                p1_chunk(ctxf, 2, j, dr, True, xi, None)
                xi ^= 1
            for j in range(NCH - 1, -1, -1):
                p1_chunk(xf, NCH, j, 1, False, xi, j if j < NOWN else None)
                xi ^= 1
            S.barrier()
            AR.reset()

            W = load_phase_common(False)
            wqk = AR.bf16(8, 1024)
            wv = AR.bf16(8, 1024)
            wp3 = AR.bf16(8, 1024)
            wro = AR.bf16(8, 1024)
            rep = [AR.f32(D) for _ in range(2)]
            W["rp"] = [AR.f32(256), AR.f32(256)]
            W["t1"] = AR.f32(4, 128)
            W["t2"] = AR.f32(4, 128)
            qr = AR.bf16(4, 128)
            kr = AR.bf16(4, 128)
            kz = AR.bf16(4, 128)
            qT = AR.bf16(4, 128)
            qxA = AR.bf16(4, 128)
            qxB = AR.bf16(4, 128)
            kT = AR.bf16(4, 128)
            vbf = AR.bf16(4, 256)
            p3s = AR.f32(D)
            sT = AR.bf16(4, 128)
            osb = AR.f32(4, 256)
            osq = AR.f32(4, 256)
            st4 = AR.f32(16)
            retg = AR.bf16(D)
            retgT = AR.bf16(8, 128)
            rout = [AR.f32(D), AR.f32(D)]
            sabf = AR.bf16(4, 256)
            sbbf = [AR.bf16(4, 256), AR.bf16(4, 256)]
            wload(wqk, win_t.ap()[:, CQ:CQ + 1024], 1024, W["stg"], "wqk")
            wload(wv, win_t.ap()[:, CV:CV + 1024], 1024, W["stg"], "wv")
            wload(wp3, win_t.ap()[:, CP3:CP3 + 1024], 1024, W["stg"], "wp3")
            wload(wro, wro_t.ap(), 1024, W["stg"], "wro")
            make_A(rep[0], W["junk"], 0, 1, 0, "rep0", "junk")
            rep_load(rep[1], modrow(0, 0), "rep1")
            CP("pool", sabf, SA[:], ["SA"], ["sabf"])
            retsc = ret_t.ap()
            for j in range(NOWN):
                xi = j % 2
                xt = W["x"][xi]
                xk = f"x{xi}"
                DMA(xt, xf[j * 128:(j + 1) * 128, :], (), [xk])
                DMA(W["rp"][xi], rope[j * 128:(j + 1) * 128, :], (), [f"rp{xi}"])
                DMA(sbbf[xi].rearrange("p h c -> p (h c)"), sbs[j], (), [f"sbbf{xi}"])
                front(xt, xk, rep[0], rep[1], ["rep0", "rep1"], W["hx"], W["junk"], W["hxT"], "hxT", 0, W["ssq"])
                for n in range(2):
                    for kc in range(8):
                        MM(PSB(1 + n), W["hxT"][:, kc, :], wqk[:, kc, n * 512:(n + 1) * 512], kc == 0, kc == 7,
                           ["hxT", "wqk"], pk(1 + n))
                for n in range(2):
                    for kc in range(8):
                        MM(PSB(3 + n), W["hxT"][:, kc, :], wv[:, kc, n * 512:(n + 1) * 512], kc == 0, kc == 7,
                           ["hxT", "wv"], pk(3 + n))
                for n in range(2):
                    for kc in range(8):
                        MM(PSB(5 + n), W["hxT"][:, kc, :], wp3[:, kc, n * 512:(n + 1) * 512], kc == 0, kc == 7,
                           ["hxT", "wp3"], pk(5 + n))
                rope_tok(W, PSB(1), pk(1), W["rp"][xi], f"rp{xi}", None, None)
                CP("act", qr, W["t1"], ["t1"], ["qr"])
                rope_tok(W, PSB(2), pk(2), W["rp"][xi], f"rp{xi}", None, None)
                CP("act", kr, W["t1"], ["t1"], ["kr"])
                TT("dve", kz, W["t1"], bc_last(zeta[:, 0:4].unsqueeze(2), 128), ALU.mult, ["t1", "zeta"], ["kz"])
                CP("act", vbf, PSB(3, 2).rearrange("p (h c) -> p h c", h=4), pk(3, 2), ["vbf"])
                ACT(p3s, PSB(5, 2), AF.Silu, pk(5, 2), ["p3s"])
                pt = PSBb(7)
                for hh in range(4):
                    TR(pt[:, hh * 128:(hh + 1) * 128], qr[:, hh, :], identb[:], ["qr", "identb"], pk(7))
                for hh in range(4):
                    TR(pt[:, 512 + hh * 128:512 + (hh + 1) * 128], kr[:, hh, :], identb[:], ["kr", "identb"], pk(7))
                ptq = pt[:, 0:512].rearrange("p (h t) -> p h t", h=4)
                CP("act", qT, ptq, pk(7), ["qT"])
                TT("dve", qxA, ptq, xiA[:], ALU.mult, pk(7) + ["xiA"], ["qxA"])
                TT("dve", qxB, ptq, xiB[:], ALU.mult, pk(7) + ["xiB"], ["qxB"])
                CP("act", kT, pt[:, 512:1024].rearrange("p (h t) -> p h t", h=4), pk(7), ["kT"])
                for hh in range(4):
                    MM(PSB(1)[:, hh * 128:(hh + 1) * 128], kT[:, hh, :], qT[:, hh, :], True, True, ["kT", "qT"], pk(1))
                TT("dve", sT, PSB(1).rearrange("p (h t) -> p h t", h=4), dct[:], ALU.mult, pk(1) + ["dct"], ["sT"])
                for hh in range(4):
                    o_ps = PSB(3, 2)[:, hh * 256:(hh + 1) * 256]
                    MM(o_ps, sT[:, hh, :], vbf[:, hh, :], True, False, ["sT", "vbf"], pk(3, 2))
                    MM(o_ps, qxA[:, hh, :], sabf[:, hh, :], False, False, ["qxA", "sabf"], pk(3, 2))
                    MM(o_ps, qxB[:, hh, :], sbbf[xi][:, hh, :], False, True, ["qxB", f"sbbf{xi}"], pk(3, 2))
                for hh in range(4):
                    MM(PSB(5, 2)[:, hh * 256:(hh + 1) * 256], kz[:, hh, :], vbf[:, hh, :], True, True, ["kz", "vbf"], pk(5, 2))
                CP("act", osb, PSB(3, 2).rearrange("p (h c) -> p h c", h=4), pk(3, 2), ["osb"])
                ACT(osq, osb, AF.Square, ["osb"], ["osq"])
                S.dve(lambda e, o=st4[:, 0:4], i=osb: e.reduce_sum(out=o, in_=i, axis=mybir.AxisListType.X), ["osb"], ["st4"])
                S.dve(lambda e, o=st4[:, 4:8], i=osq: e.reduce_sum(out=o, in_=i, axis=mybir.AxisListType.X), ["osq"], ["st4"])
                TS("dve", st4[:, 0:8], st4[:, 0:8], 1.0 / 256, None, ALU.mult, None, ["st4"], ["st4"])
                TT("dve", st4[:, 8:12], st4[:, 0:4], st4[:, 0:4], ALU.mult, ["st4"], ["st4"])
                TT("dve", st4[:, 8:12], st4[:, 4:8], st4[:, 8:12], ALU.subtract, ["st4"], ["st4"])
                ACT(st4[:, 8:12], st4[:, 8:12], AF.Ln, ["st4", "epsc"], ["st4"], bias=epsc[:, 0:1])
                ACT(st4[:, 8:12], st4[:, 8:12], AF.Exp, ["st4"], ["st4"], scale=-0.5)
                TT("dve", osb, osb, bc_last(st4[:, 0:4].unsqueeze(2), 256), ALU.subtract, ["osb", "st4"], ["osb"])
                TT("dve", osb, osb, bc_last(st4[:, 8:12].unsqueeze(2), 256), ALU.mult, ["osb", "st4"], ["osb"])
                if "ret" in dbg:
                    DMA(dbg["ret"].ap()[j * 128:(j + 1) * 128, :], osb.rearrange("p h c -> p (h c)"), ["osb"], ())
                TT("pool", retg, osb.rearrange("p h c -> p (h c)"), p3s, ALU.mult, ["osb", "p3s"], ["retg"])
                pt0 = PSBb(0)
                for kc in range(8):
                    TR(pt0[:, kc * 128:(kc + 1) * 128], retg[:, kc * 128:(kc + 1) * 128], identb[:], ["retg", "identb"], pk(0))
                CP("act", retgT, pt0.rearrange("p (k n) -> p k n", k=8), pk(0), ["retgT"])
                for n in range(2):
                    for kc in range(8):
                        MM(PSB(1 + n), retgT[:, kc, :], wro[:, kc, n * 512:(n + 1) * 512], kc == 0, kc == 7,
                           ["retgT", "wro"], pk(1 + n))
                CP("act", rout[xi], PSB(1, 2), pk(1, 2), [f"rout{xi}"])
                DMA(retsc[j * 128:(j + 1) * 128, :], rout[xi], [f"rout{xi}"], ())
                for hh in range(4):
                    STT("dve", SA[:, hh, :], SA[:, hh, :], decay[:, hh:hh + 1], PSB(5, 2)[:, hh * 256:(hh + 1) * 256],
                        ALU.mult, ALU.add, ["SA", "decay"] + pk(5, 2), ["SA"])
                CP("pool", sabf, SA[:], ["SA"], ["sabf"])
            S.barrier()
            AR.reset()

            W = load_phase_common(True)
            W["hxT4"] = AR.bf16(8, 4)
            wlx = AR.bf16(8, 1024)
            wp5 = AR.bf16(8, 1024)
            wlo = AR.bf16(8, 1024)
            rep = [AR.f32(D) for _ in range(2)]
            W["uT"] = AR.f32(8, 132)
            W["m0"] = AR.f32(8, 128)
            W["m1"] = AR.f32(8, 128)
            uc = AR.f32(8, 128)
            ucb = AR.bf16(8, 128)
            for nm in ("r", "i", "la", "a2", "bt"):
                W[nm] = AR.f32(8, 128)
            hA = AR.f32(8, 128)
            hB = AR.f32(8, 128)
            W["tmp8"] = AR.f32(8)
            p5g = AR.f32(8, 128)
            yg = AR.bf16(8, 128)
            lout = [AR.f32(D), AR.f32(D)]
            wload(wlx, win_t.ap()[:, CLX:CLX + 1024], 1024, W["stg"], "wlx")
            wload(wp5, win_t.ap()[:, CP5:CP5 + 1024], 1024, W["stg"], "wp5")
            wload(wlo, wlo_t.ap(), 1024, W["stg"], "wlo")
            make_A(rep[0], W["junk"], 0, 1, 0, "rep0", "junk")
            rep_load(rep[1], modrow(0, 0), "rep1")
            lrusc = lru_t.ap()
            for j in range(NOWN):
                xi = j % 2
                xt = W["x"][xi]
                xk = f"x{xi}"
                DMA(xt, xf[j * 128:(j + 1) * 128, :], (), [xk])
                front(xt, xk, rep[0], rep[1], ["rep0", "rep1"], W["hx"], W["junk"], W["hxT"][:, :, 2:130], "hxT", 0, W["ssq"])
                halo_front(W, xf, j, NCH, rep[0], rep[1], ["rep0", "rep1"], 0)
                proj_uT(W, wlx, j, NCH)
                p5ps = PSB(1, 2).rearrange("p (b t) -> p b t", b=8)
                for blk in range(8):
                    for kc in range(8):
                        MM(p5ps[:, blk, :], wp5[:, kc, blk * 128:(blk + 1) * 128], W["hxT"][:, kc, 2:130], kc == 0, kc == 7,
                           ["wp5", "hxT"], pk(1, 2))
                ACT(p5g, p5ps, AF.Gelu_apprx_tanh, pk(1, 2), ["p5g"])
                conv(W, W["uT"], uc, ucb)
                lru_gates(W, 0, uc, ucb, "hA")
                lru_scan(W, 0, hA, "hA")
                CP("dve", hprev[:, 1, :], hbinit[:, j, :], ["hbinit"], ["hprev"])
                lru_gates(W, 1, uc, ucb, "hB")
                lru_scan(W, 1, hB, "hB")
                TT("pool", hA, hA, hB, ALU.add, ["hA", "hB"], ["hA"])
                TT("dve", yg, hA, p5g, ALU.mult, ["hA", "p5g"], ["yg"])
                for n in range(2):
                    for blk in range(8):
                        MM(PSB(1 + n), yg[:, blk, :], wlo[:, blk, n * 512:(n + 1) * 512], blk == 0, blk == 7,
                           ["yg", "wlo"], pk(1 + n))
                CP("act", lout[xi], PSB(1, 2), pk(1, 2), [f"lout{xi}"])
                DMA(lrusc[j * 128:(j + 1) * 128, :], lout[xi], [f"lout{xi}"], ())
            S.barrier()
            AR.reset()

            W = load_phase_common(False)
            wp67 = AR.bf16(8, 2048)
            wo = AR.bf16(8, 1024)
            rep = [AR.f32(D) for _ in range(3)]
            rin = [AR.f32(D), AR.f32(D)]
            lin = [AR.f32(D), AR.f32(D)]
            g6 = AR.f32(D)
            g7 = AR.f32(D)
            ym = AR.bf16(D)
            ymT = AR.bf16(8, 128)
            xnew = [AR.f32(D), AR.f32(D)]
            wload(wp67, win_t.ap()[:, CP6:CP6 + 2048], 2048, W["stg"], "wp67")
            wload(wo, wo_t.ap(), 1024, W["stg"], "wo")
            make_A(rep[0], W["junk"], 0, 1, 0, "rep0", "junk")
            rep_load(rep[1], modrow(0, 0), "rep1")
            rep_load(rep[2], modrow(0, 2), "rep2")
            xnsc = xn_t.ap()
            for j in range(NOWN):
                xi = j % 2
                xt = W["x"][xi]
                xk = f"x{xi}"
                DMA(xt, xf[j * 128:(j + 1) * 128, :], (), [xk])
                DMA(rin[xi], retsc[j * 128:(j + 1) * 128, :], (), [f"rin{xi}"])
                DMA(lin[xi], lrusc[j * 128:(j + 1) * 128, :], (), [f"lin{xi}"])
                front(xt, xk, rep[0], rep[1], ["rep0", "rep1"], W["hx"], W["junk"], W["hxT"], "hxT", 0, W["ssq"])
                for n in range(4):
                    for kc in range(8):
                        MM(PSB(1 + n), W["hxT"][:, kc, :], wp67[:, kc, n * 512:(n + 1) * 512], kc == 0, kc == 7,
                           ["hxT", "wp67"], pk(1 + n))
                ACT(g6, PSB(1, 2), AF.Sigmoid, pk(1, 2), ["g6"])
                ACT(g7, PSB(3, 2), AF.Sigmoid, pk(3, 2), ["g7"])
                TT("dve", g6, g6, rin[xi], ALU.mult, ["g6", f"rin{xi}"], ["g6"])
                TT("pool", g7, g7, lin[xi], ALU.mult, ["g7", f"lin{xi}"], ["g7"])
                TT("dve", ym, g6, g7, ALU.add, ["g6", "g7"], ["ym"])
                pt0 = PSBb(5)
                for kc in range(8):
                    TR(pt0[:, kc * 128:(kc + 1) * 128], ym[:, kc * 128:(kc + 1) * 128], identb[:], ["ym", "identb"], pk(5))
                CP("act", ymT, pt0.rearrange("p (k n) -> p k n", k=8), pk(5), ["ymT"])
                for n in range(2):
                    for kc in range(8):
                        MM(PSB(6 + n), ymT[:, kc, :], wo[:, kc, n * 512:(n + 1) * 512], kc == 0, kc == 7,
                           ["ymT", "wo"], pk(6 + n))
                TT("dve", xnew[xi], PSB(6, 2), rep[2], ALU.mult, pk(6, 2) + ["rep2"], [f"xnew{xi}"])
                TT("pool", xnew[xi], xnew[xi], xt, ALU.add, [f"xnew{xi}", xk], [f"xnew{xi}"])
                DMA(xnsc[j * 128:(j + 1) * 128, :], xnew[xi], [f"xnew{xi}"], ())
                if "xn" in dbg:
                    DMA(dbg["xn"].ap()[j * 128:(j + 1) * 128, :], xnew[xi], [f"xnew{xi}"], ())
            S.barrier()
            AR.reset()

        if phases in ("all", "peer"):
            xnsc = xn_t.ap() if phases == "all" else xf
            wq = AR.bf16(8, 1024)
            keysb = AR.bf16(8, 256)
            rep = [AR.f32(D) for _ in range(4)]
            junk = AR.f32(D)
            hx2 = AR.bf16(D)
            qTs = AR.bf16(8, 128)
            ssq = AR.f32(4)
            T = []
            for t in range(2):
                T.append(dict(xn=AR.f32(D), hT=AR.bf16(8, 128), s=AR.f32(8, 2, 128), a16=AR.f32(8, 2, 16),
                              top=AR.f32(8, 16), st=AR.f32(32), wsum=AR.f32(512), G=AR.f32(512),
                              coef=AR.bf16(512), coefT=AR.bf16(4, 128)))
            KK = [AR.f32(8, 128) for _ in range(3)]
            K0f, K1f, K2f = [k.rearrange("p a b -> p (a b)") for k in KK]
            CC = [AR.f32(4, 128) for _ in range(3)]
            EE = [AR.bf16(4, 128) for _ in range(3)]
            WW = [AR.bf16(4, 128) for _ in range(8)]
            ublk = [AR.bf16(8, 512), AR.bf16(8, 512)]
            vblk = [AR.bf16(4, 1024) for _ in range(3)]
            outt = AR.f32(D)
            wload(wq, wq_t.ap(), 1024, [KK[1], KK[2]], "wq", piece=128, stgkeys=("K1", "K2"))
            DMA(K0f, keys_t.ap()[:, 0:1024], (), ["K0"])
            CP("dve", keysb.rearrange("p a b -> p (a b)")[:, 0:1024], K0f, ["K0"], ["keysb"])
            DMA(K1f, keys_t.ap()[:, 1024:2048], (), ["K1"])
            CP("dve", keysb.rearrange("p a b -> p (a b)")[:, 1024:2048], K1f, ["K1"], ["keysb"])
            make_A(rep[0], junk, 0, 4, 1, "rep0", "junk")
            rep_load(rep[1], modrow(0, 3), "rep1")
            rep_load(rep[2], modrow(0, 5), "rep2")
            rep_load(rep[3], gains[2:3, :], "rep3")
            utb = utb_t.ap()
            vb = vb_t.ap()
            outd = out_t.ap()
            NBLK = OWN // 256
            if "peer1" in dbg:
                NBLK = 1
            WB = (5, 7)
            for blk in range(NBLK):
                for t in range(2):
                    Tt = T[t]
                    tk = f"T{t}"
                    row0 = blk * 256 + t * 128
                    DMA(Tt["xn"], xnsc[row0:row0 + 128, :], (), [tk + "xn"])
                    front(Tt["xn"], tk + "xn", rep[0], rep[1], ["rep0", "rep1"], hx2, junk, Tt["hT"], tk + "hT", 7, ssq)
                    qps = PSB(5, 2).rearrange("p (h t) -> p h t", h=8)
                    for hh in range(8):
                        for kc in range(8):
                            MM(qps[:, hh, :], wq[:, kc, hh * 128:(hh + 1) * 128], Tt["hT"][:, kc, :], kc == 0, kc == 7,
                               ["wq", tk + "hT"], pk(5, 2))
                    CP("act", qTs, qps, pk(5, 2), ["qTs"])
                    sps = PSB(0, 4).rearrange("p (h c) -> p h c", h=8)
                    for hh in range(8):
                        MM(sps[:, hh, :], qTs[:, hh, :], keysb[:, hh, :], True, True, ["qTs", "keysb"], pk(0, 4))
                    s = Tt["s"]
                    CP("act", s.rearrange("p h a k -> p h (a k)"), sps, pk(0, 4), [tk + "s"])
                    a16 = Tt["a16"]
                    for g in range(2):
                        sl = [(g * 4 + q, p_) for q in range(4) for p_ in range(2)]
                        for n_, (hh, p_) in enumerate(sl):
                            S.dve(lambda e, o=a16[:, hh, p_, 0:8], i=s[:, hh, p_, :]: e.max(out=o, in_=i), [tk + "s"], [tk + "a16"])
                        for n_, (hh, p_) in enumerate(sl):
                            S.dve(lambda e, o=KK[0][:, n_, :], r_=a16[:, hh, p_, 0:8], i=s[:, hh, p_, :]:
                                  e.match_replace(out=o, in_to_replace=r_, in_values=i, imm_value=-1e30),
                                  [tk + "s", tk + "a16"], ["K0"])
                        for n_, (hh, p_) in enumerate(sl):
                            S.dve(lambda e, o=a16[:, hh, p_, 8:16], i=KK[0][:, n_, :]: e.max(out=o, in_=i), ["K0"], [tk + "a16"])
                    cflat = [K1f, K2f]
                    for g in range(2):
                        cbuf = cflat[g].rearrange("p (h r q) -> p h r q", h=4, r=16)
                        ckey = ("K1", "K2")[g]
                        in0 = a16[:, g * 4:(g + 1) * 4, 0, :].unsqueeze(3).broadcast_to([128, 4, 16, 16])
                        in1 = a16[:, g * 4:(g + 1) * 4, 1, :].unsqueeze(2).broadcast_to([128, 4, 16, 16])
                        TT("dve", cbuf, in0, in1, ALU.add, [tk + "a16"], [ckey])
                    top = Tt["top"]
                    for hh in range(8):
                        cv_ = cflat[hh // 4][:, (hh % 4) * 256:(hh % 4 + 1) * 256]
                        ck_ = ("K1", "K2")[hh // 4]
                        S.dve(lambda e, o=top[:, hh, 0:8], i=cv_: e.max(out=o, in_=i), [ck_], [tk + "top"])
                    for hh in range(8):
                        cv_ = cflat[hh // 4][:, (hh % 4) * 256:(hh % 4 + 1) * 256]
                        ck_ = ("K1", "K2")[hh // 4]
                        S.dve(lambda e, o=K0f[:, (hh % 4) * 256:(hh % 4 + 1) * 256],
                              r_=top[:, hh, 0:8], i=cv_: e.match_replace(out=o, in_to_replace=r_, in_values=i, imm_value=-1e30),
                              [ck_, tk + "top"], ["K0"])
                        S.dve(lambda e, o=top[:, hh, 8:16], i=K0f[:, (hh % 4) * 256:(hh % 4 + 1) * 256]:
                              e.max(out=o, in_=i), ["K0"], [tk + "top"])
                    stt_ = Tt["st"]
                    TS("dve", stt_[:, 0:8], top[:, :, 0], -1.0, None, ALU.mult, None, [tk + "top"], [tk + "st"])
                    for hh in range(8):
                        ACT(junk[:, hh * 16:(hh + 1) * 16], top[:, hh, :], AF.Exp, [tk + "top", tk + "st"], ["junk", tk + "st"],
                            bias=stt_[:, hh:hh + 1], accum=stt_[:, 8 + hh:9 + hh])
                    ACT(stt_[:, 16:24], stt_[:, 8:16], AF.Ln, [tk + "st"], [tk + "st"])
                    TT("dve", stt_[:, 16:24], stt_[:, 0:8], stt_[:, 16:24], ALU.subtract, [tk + "st"], [tk + "st"])
                NIT = 64
                pairs = [(i, hh) for i in range(NIT) for hh in range(8)]
                cptr = [0]

                def emit_cadd(upto):
                    while cptr[0] < min(upto, len(pairs)):
                        i_, hh = pairs[cptr[0]]
                        g = cptr[0]
                        eb_, t_ = i_ // 2, i_ % 2
                        s_ = T[t_]["s"]
                        in0 = s_[:, hh, 0, eb_ * 4:(eb_ + 1) * 4].unsqueeze(2).broadcast_to([128, 4, 128])
                        in1 = s_[:, hh, 1, :].unsqueeze(1).broadcast_to([128, 4, 128])
                        TT("dve", CC[g % 3], in0, in1, ALU.add, [f"T{t_}s"], [f"C{g % 3}"])
                        cptr[0] += 1

                def load_u(eb_):
                    DMA(ublk[eb_ % 2].rearrange("p k n -> p (k n)"), utb[eb_].rearrange("p k n -> p (k n)"), (), [f"ublk{eb_ % 2}"])

                def load_v(eb_):
                    DMA(vblk[eb_ % 3].rearrange("p c d -> p (c d)"), vb[eb_], (), [f"vblk{eb_ % 3}"])

                load_u(0)
                load_v(0)
                for it in range(NIT + 3):
                    if it < NIT and it % 2 == 0 and it // 2 + 1 < 32:
                        load_u(it // 2 + 1)
                    if it < NIT and it % 2 == 1 and it // 2 + 1 < 32:
                        load_v(it // 2 + 1)
                    if 0 <= it - 1 < NIT:
                        t = (it - 1) % 2
                        Tt = T[t]
                        tk = f"T{t}"
                        CP("act", Tt["wsum"], PSB(WB[t]), pk(WB[t]), [tk + "wsum"])
                        TT("pool", Tt["coef"], Tt["G"], Tt["wsum"], ALU.mult, [tk + "G", tk + "wsum"], [tk + "coef"])
                    if it < NIT:
                        eb, t = it // 2, it % 2
                        Tt = T[t]
                        tk = f"T{t}"
                        for kc in range(8):
                            MM(PSB(4), Tt["hT"][:, kc, :], ublk[eb % 2][:, kc, :], kc == 0, kc == 7,
                               [tk + "hT", f"ublk{eb % 2}"], pk(4))
                    if 0 <= it - 2 < NIT:
                        t = (it - 2) % 2
                        Tt = T[t]
                        tk = f"T{t}"
                        ptc = PSBb(6)[:, t * 512:(t + 1) * 512]
                        for cc in range(4):
                            TR(ptc[:, cc * 128:(cc + 1) * 128], Tt["coef"][:, cc * 128:(cc + 1) * 128], identb[:],
                               [tk + "coef", "identb"], [f"ps6_{t}"])
                        CP("act", Tt["coefT"], ptc.rearrange("p (c n) -> p c n", c=4), [f"ps6_{t}"], [tk + "coefT"])
                    if 0 <= it - 3 < NIT:
                        i3 = it - 3
                        eb3, t = i3 // 2, i3 % 2
                        Tt = T[t]
                        tk = f"T{t}"
                        for n in range(2):
                            for cc in range(4):
                                MM(PSB(2 * t + n), Tt["coefT"][:, cc, :], vblk[eb3 % 3][:, cc, n * 512:(n + 1) * 512],
                                   eb3 == 0 and cc == 0, eb3 == 31 and cc == 3, [tk + "coefT", f"vblk{eb3 % 3}"], pk(2 * t + n))
                    if it < NIT:
                        eb, t = it // 2, it % 2
                        Tt = T[t]
                        tk = f"T{t}"
                        stt_ = Tt["st"]
                        for hh in range(8):
                            g = it * 8 + hh
                            emit_cadd(g + 3)
                            Cx, Ex, Wx = CC[g % 3], EE[g % 3], WW[hh]
                            ck, ek, wk_ = f"C{g % 3}", f"E{g % 3}", f"W{hh}"
                            ACT(Ex, Cx, AF.Exp, [ck, tk + "st"], [ek], bias=stt_[:, 16 + hh:17 + hh])
                            STT("dve", Wx, Cx, Tt["top"][:, hh, 15:16], Ex, ALU.is_ge, ALU.mult, [ck, ek, tk + "top"], [wk_])
                            MM(PSB(WB[t]), identb[:], Wx.rearrange("p a b -> p (a b)"), hh == 0, hh == 7, ["identb", wk_], pk(WB[t]))
                            if hh == 1:
                                ACT(Tt["G"], PSB(4), AF.Gelu_apprx_tanh, pk(4), [tk + "G"])
                for t in range(2):
                    Tt = T[t]
                    tk = f"T{t}"
                    row0 = blk * 256 + t * 128
                    TT("dve", outt, PSB(2 * t, 2), rep[2], ALU.mult, pk(2 * t, 2) + ["rep2"], ["outt"])
                    TT("dve", outt, outt, Tt["xn"], ALU.add, ["outt", tk + "xn"], ["outt"])
                    ACT(junk, outt, AF.Square, ["outt"], ["junk", "ssq"], accum=ssq[:, 0:1])
                    ACT(ssq[:, 1:2], ssq[:, 0:1], AF.Ln, ["ssq", "epsc"], ["ssq"], bias=epsc[:, 0:1], scale=1.0 / D)
                    ACT(ssq[:, 2:3], ssq[:, 1:2], AF.Exp, ["ssq"], ["ssq"], scale=-0.5)
                    STT("dve", outt, outt, ssq[:, 2:3], rep[3], ALU.mult, ALU.mult, ["outt", "ssq", "rep3"], ["outt"])
                    DMA(outd[row0:row0 + 128, :], outt, ["outt"], ())
        cnt = S.emit(nc)
    return nc, cnt


def _consts():
    cst = np.zeros((128, 642), np.float32)
    cst[:, 0:128] = np.eye(128, dtype=np.float32)
    j = np.arange(128)[:, None].astype(np.float32)
    i = np.arange(128)[None, :].astype(np.float32)
    cst[:, 128:256] = np.maximum(i - j, 0.0)
    cst[:, 256:384] = np.maximum(j - i, 0.0)
    cst[:, 384:512] = np.broadcast_to(i + 1.0, (128, 128))
    cst[:, 512:640] = np.broadcast_to(128.0 - i, (128, 128))
    cst[:, 640] = 127.0 - np.arange(128)
    cst[:, 641] = np.arange(128)
    return cst


def _rope_table():
    t = np.arange(NTOK)
    row = (t // 64).astype(np.float32)
    col = (t % 64).astype(np.float32)
    inv = (np.float32(10000.0) ** (-np.arange(32, dtype=np.float32) / np.float32(32))).astype(np.float32)
    ar = (row[:, None] * inv[None, :]).astype(np.float32)
    ac = (col[:, None] * inv[None, :]).astype(np.float32)
    cr, sr, cc, sc = np.cos(ar), np.sin(ar), np.cos(ac), np.sin(ac)
    tab = np.concatenate([cr, cr, cc, cc, -sr, sr, -sc, sc], axis=1).astype(np.float32)
    return tab


def _chunkT(v):
    return np.ascontiguousarray(np.asarray(v, np.float32).reshape(8, 128).T)


def make_in_maps(inputs):
    f = np.float32
    x = np.asarray(inputs["x"], f)
    ctx = np.asarray(inputs["ctx"], f)
    c = np.asarray(inputs["c"], f)
    c_ctx = np.asarray(inputs["c_ctx"], f)
    l = 0
    rope = _rope_table()
    rope_rev = np.ascontiguousarray(rope[::-1])
    cst = _consts()
    gains = np.ascontiguousarray(np.stack([inputs["norm1_g"][l], inputs["norm2_g"][l], inputs["final_g"]]).astype(f))
    keys = np.asarray(inputs["peer_keys"][l], f)
    kbd = np.zeros((128, 8, 256), f)
    for h in range(8):
        for p in range(2):
            kbd[p * 64:(p + 1) * 64, h, p * 128:(p + 1) * 128] = keys[h, p].T
    kbd = np.ascontiguousarray(kbd.reshape(128, 2048))
    uT = np.ascontiguousarray(np.asarray(inputs["peer_u"][l], f).T)
    shared = {
        "cst": cst, "gains": gains,
        "mod_b": np.ascontiguousarray(np.asarray(inputs["mod_b"][l], f).reshape(1, -1)),
        "mod_w": np.ascontiguousarray(np.asarray(inputs["mod_w"][l], f)),
        "w_in": np.ascontiguousarray(np.asarray(inputs["w_in"][l], f)),
        "w_ret_out": np.ascontiguousarray(np.asarray(inputs["w_ret_out"][l], f)),
        "w_lru_out": np.ascontiguousarray(np.asarray(inputs["w_lru_out"][l], f)),
        "w_out": np.ascontiguousarray(np.asarray(inputs["w_out"][l], f)),
        "peer_wq": np.ascontiguousarray(np.asarray(inputs["peer_wq"][l], f)),
        "keysbd": kbd, "peer_uT": uT,
        "peer_v": np.ascontiguousarray(np.asarray(inputs["peer_v"][l], f)),
    }
    cw = np.asarray(inputs["conv_w"][l], f)
    maps = []
    for core in range(8):
        b, s = core // 2, core % 2
        dirs = (0, 1) if s == 0 else (1, 0)
        m = dict(shared)
        if s == 0:
            m["xf"] = np.ascontiguousarray(x[b])
            m["ctxf"] = np.ascontiguousarray(ctx[b])
            m["rope"] = rope
            w5 = np.concatenate([cw, np.zeros((1, D), f)], axis=0)
        else:
            m["xf"] = np.ascontiguousarray(x[b, ::-1])
            m["ctxf"] = np.ascontiguousarray(ctx[b, ::-1])
            m["rope"] = rope_rev
            w5 = np.concatenate([np.zeros((1, D), f), cw[::-1]], axis=0)
        par = np.zeros((128, 112), f)
        cv = np.stack([_chunkT(c[b]), _chunkT(c_ctx)], axis=2)
        par[:, 0:16] = cv.reshape(128, 16)
        lam = np.asarray(inputs["lru_lambda"][l], f)
        lba = np.asarray(inputs["lru_ba"][l], f)
        lbx = np.asarray(inputs["lru_bx"][l], f)
        par[:, 16:32] = np.concatenate([_chunkT(lam[d]) for d in dirs], axis=1)
        par[:, 32:48] = np.concatenate([_chunkT(lba[d]) for d in dirs], axis=1)
        par[:, 48:64] = np.concatenate([_chunkT(lbx[d]) for d in dirs], axis=1)
        par[:, 64:104] = np.concatenate([_chunkT(w5[o]) for o in range(5)], axis=1)
        par[:, 104:112] = _chunkT(inputs["conv_b"][l])
        m["par"] = par
        rd = np.asarray(inputs["ret_decay"][l], f)
        m["dec"] = np.ascontiguousarray(np.concatenate([rd[dirs[0]], rd[dirs[1]]]).reshape(1, 8))
        wa = np.asarray(inputs["lru_wa"][l], f)
        wx = np.asarray(inputs["lru_wx"][l], f)
        gw = np.stack([wa[dirs[0]], wx[dirs[0]], wa[dirs[1]], wx[dirs[1]]], axis=0)
        m["gatew"] = np.ascontiguousarray(gw.transpose(2, 0, 1, 3).reshape(128, 4 * 8 * 128))
        maps.append(m)
    return maps


_CACHE = {}


def kernel(**inputs):
    if "nc" not in _CACHE:
        _CACHE["nc"] = build()[0]
    nc = _CACHE["nc"]
    maps = make_in_maps(inputs)
    res = run_bass_kernel_spmd(nc, maps, core_ids=list(range(8)))
    out = np.zeros((4, NTOK, D), np.float32)
    for core in range(8):
        b, s = core // 2, core % 2
        o = np.asarray(res.results[core]["out"], np.float32)
        if s == 0:
            out[b, 0:OWN] = o
        else:
            out[b, OWN:NTOK] = o[::-1]
    return out
```

```python
import math
from contextlib import ExitStack

import numpy as np
import concourse.bass as bass
import concourse.mybir as mybir
from concourse.bass_utils import run_bass_kernel_spmd

F32 = mybir.dt.float32
BF16 = mybir.dt.bfloat16
ALU = mybir.AluOpType
AF = mybir.ActivationFunctionType

ENG = ("pe", "act", "dve", "pool", "sp")
NDMASEM = 6


class _Op:
    __slots__ = ("eng", "fn", "deps", "is_dma", "sig", "dsem", "dval", "dprev")

    def __init__(self, eng, fn, is_dma):
        self.eng = eng
        self.fn = fn
        self.deps = []
        self.is_dma = is_dma
        self.sig = None
        self.dsem = None
        self.dval = None
        self.dprev = None


class Sched:
    def __init__(self):
        self.ops = []
        self.lastw = {}
        self.readers = {}
        self.ndma = 0
        self.dsem_last = [None] * NDMASEM
        self.last_eng = {}

    def add(self, eng, fn, reads=(), writes=(), dma=False):
        op = _Op(eng, fn, dma)
        deps = []
        for k in reads:
            w = self.lastw.get(k)
            if w is not None:
                deps.append(w)
        for k in writes:
            w = self.lastw.get(k)
            if w is not None:
                deps.append(w)
            deps.extend(self.readers.get(k, ()))
        for k in reads:
            lst = self.readers.setdefault(k, [])
            if not dma:
                lst[:] = [o for o in lst if o.is_dma or o.eng != eng]
            lst.append(op)
        for k in writes:
            self.lastw[k] = op
            self.readers[k] = []
        if dma:
            s = self.ndma % NDMASEM
            self.ndma += 1
            op.dsem = s
            prev = self.dsem_last[s]
            op.dprev = prev
            op.dval = 16 if prev is None else prev.dval + 16
            self.dsem_last[s] = op
        seen = set()
        for d in deps:
            if id(d) in seen or d is op:
                continue
            seen.add(id(d))
            if d.is_dma or d.eng != eng or eng != "pe":
                op.deps.append(d)
        self.ops.append(op)
        if not dma:
            self.last_eng[eng] = op
        return op

    def pe(self, fn, r=(), w=()):
        return self.add("pe", fn, r, w)

    def act(self, fn, r=(), w=()):
        return self.add("act", fn, r, w)

    def dve(self, fn, r=(), w=()):
        return self.add("dve", fn, r, w)

    def pool(self, fn, r=(), w=()):
        return self.add("pool", fn, r, w)

    def dma(self, fn, r=(), w=()):
        return self.add("sp", fn, r, w, dma=True)

    def barrier(self):
        lasts = [o for o in self.last_eng.values()] + [o for o in self.dsem_last if o is not None]
        for e in ENG:
            op = _Op(e, None, False)
            op.deps = [d for d in lasts]
            self.ops.append(op)
        self.lastw.clear()
        self.readers.clear()

    def emit(self, nc, EPOCH=20000):
        need = set()
        for op in self.ops:
            for d in op.deps:
                if not d.is_dma:
                    need.add(id(d))
        cnt = {e: 0 for e in ENG}
        for op in self.ops:
            if not op.is_dma and op.fn is not None and id(op) in need:
                c = cnt[op.eng]
                op.sig = (c // EPOCH, c % EPOCH + 1)
                cnt[op.eng] = c + 1
        nep = {e: (cnt[e] + EPOCH - 1) // EPOCH for e in ENG}
        with ExitStack() as st:
            esem = {e: [st.enter_context(nc.semaphore(f"s_{e}_{i}")) for i in range(nep[e])] for e in ENG}
            dsem = [st.enter_context(nc.semaphore(f"s_dma_{i}")) for i in range(NDMASEM)]
            block = st.enter_context(nc.Block())
            per = {e: [o for o in self.ops if o.eng == e] for e in ENG}

            def run(e, eng):
                waited = {}
                for op in per[e]:
                    wants = {}
                    for d in op.deps:
                        if d.is_dma:
                            key = ("d", d.dsem)
                            val = d.dval
                            sem = dsem[d.dsem]
                        else:
                            if d.sig is None:
                                continue
                            ep, val = d.sig
                            key = (d.eng, ep)
                            sem = esem[d.eng][ep]
                        if wants.get(key, (None, 0))[1] < val:
                            wants[key] = (sem, val)
                    if op.is_dma and op.dprev is not None:
                        key = ("d", op.dsem)
                        if wants.get(key, (None, 0))[1] < op.dprev.dval:
                            wants[key] = (dsem[op.dsem], op.dprev.dval)
                    for key, (sem, val) in wants.items():
                        if waited.get(key, 0) >= val:
                            continue
                        waited[key] = val
                        eng.wait_ge(sem, val)
                    if op.fn is None:
                        continue
                    ins = op.fn(eng)
                    if op.is_dma:
                        ins.then_inc(dsem[op.dsem], 16)
                    elif op.sig is not None:
                        ins.then_inc(esem[e][op.sig[0]], 1)
                if e == "sp":
                    for s in range(NDMASEM):
                        last = self.dsem_last[s]
                        if last is not None and waited.get(("d", s), 0) < last.dval:
                            eng.wait_ge(dsem[s], last.dval)

            @block.tensor
            def _(eng):
                run("pe", eng)

            @block.scalar
            def _(eng):
                run("act", eng)

            @block.vector
            def _(eng):
                run("dve", eng)

            @block.gpsimd
            def _(eng):
                run("pool", eng)

            @block.sync
            def _(eng):
                run("sp", eng)
        return cnt


D = 1024
NTOK = 8192
OWN = 4096
NCH = 64
NOWN = 32
EPS = 1e-6
KSCALE = 128 ** -0.5
ARENA_W = 45056
CQ, CK, CV, CP3, CLX, CP5, CP6, CP7 = 0, 512, 1024, 2048, 3072, 4096, 5120, 6144


def _prod(s):
    n = 1
    for v in s:
        n *= v
    return n


class Arena:
    def __init__(self, t, words):
        self.t = t
        self.n = words
        self.off = 0

    def reset(self):
        self.off = 0

    def _shape(self, v, shape):
        if len(shape) == 1:
            return v
        names = "abcdefg"[: len(shape)]
        pat = "p (" + " ".join(names) + ") -> p " + " ".join(names)
        return v.rearrange(pat, **{n: s for n, s in zip(names, shape)})

    def f32(self, *shape, parts=128):
        n = _prod(shape)
        assert self.off + n <= self.n, ("arena overflow", self.off, n)
        v = self.t[0:parts, self.off:self.off + n]
        self.off += n
        return self._shape(v, shape)

    def bf16(self, *shape, parts=128):
        n = _prod(shape)
        w = (n + 1) // 2
        assert self.off + w <= self.n, ("arena overflow", self.off, w)
        v = self.t[0:parts, self.off:self.off + w].bitcast(BF16)[:, 0:n]
        self.off += w
        return self._shape(v, shape)


def _rev(ap2d):
    dims = ap2d.ap
    pstep, pcnt = dims[0]
    n = dims[-1][1]
    assert len(dims) == 2 and dims[-1][0] == 1
    return bass.AP(ap2d.tensor, ap2d.offset + n - 1, [[pstep, pcnt], [-1, n]])


def build(debug=None, phases="all"):
    nc = bass.Bass("TRN2", target_bir_lowering=False)
    S = Sched()

    def din(name, shape, dt=F32):
        return nc.dram_tensor(name, list(shape), dt, kind="ExternalInput")

    xf_t = din("xf", [NTOK, D])
    ctx_t = din("ctxf", [256, D])
    rope_t = din("rope", [NTOK, 256])
    cst_t = din("cst", [128, 642])
    par_t = din("par", [128, 112])
    dec_t = din("dec", [1, 8])
    gains_t = din("gains", [3, D])
    modb_t = din("mod_b", [1, 6 * D])
    modw_t = din("mod_w", [D, 6 * D])
    win_t = din("w_in", [D, 7168])
    wro_t = din("w_ret_out", [D, D])
    wlo_t = din("w_lru_out", [D, D])
    wo_t = din("w_out", [D, D])
    wq_t = din("peer_wq", [D, D])
    keys_t = din("keysbd", [128, 8 * 256])
    gw_t = din("gatew", [128, 4 * 8 * 128])
    ut_t = din("peer_uT", [D, 16384])
    v_t = din("peer_v", [16384, D])
    out_t = nc.dram_tensor("out", [OWN, D], F32, kind="ExternalOutput")
    modd_t = nc.dram_tensor("modd", [2, 6 * D], F32, kind="Internal")
    utb_t = nc.dram_tensor("utb", [32, 128, 8, 512], BF16, kind="Internal")
    vb_t = nc.dram_tensor("vb", [32, 128, 4096], BF16, kind="Internal")
    sbs_t = nc.dram_tensor("sbs", [NOWN, 128, 1024], BF16, kind="Internal")
    ret_t = nc.dram_tensor("retsc", [OWN, D], F32, kind="Internal")
    lru_t = nc.dram_tensor("lrusc", [OWN, D], F32, kind="Internal")
    xn_t = nc.dram_tensor("xnsc", [OWN, D], F32, kind="Internal")
    dbg = {}
    for nm in (debug.split(",") if debug else []):
        dbg[nm] = nc.dram_tensor("dbg_" + nm, [OWN, D], F32, kind="ExternalOutput")

    xf = xf_t.ap()
    ctxf = ctx_t.ap()
    rope = rope_t.ap()

    with ExitStack() as st:
        def sbt(name, shape, dt=F32):
            return st.enter_context(nc.sbuf_tensor(name, list(shape), dt))

        arena_t = sbt("arena", [128, ARENA_W])
        AR = Arena(arena_t, ARENA_W)
        ps = st.enter_context(nc.psum_tensor("ps", [128, 4096], F32))
        cst = sbt("cstsb", [128, 642])
        identb = sbt("identb", [128, 128], BF16)
        par = sbt("parsb", [128, 112])
        lg = sbt("lg", [128, 8])
        dct = sbt("dct", [128, 4, 128])
        xiA = sbt("xiA", [128, 4, 128])
        xiB = sbt("xiB", [128, 4, 128])
        zeta = sbt("zeta", [128, 8])
        decay = sbt("decay", [128, 8])
        nsp8 = sbt("nsp8", [128, 16])
        hprev = sbt("hprev", [128, 2, 8])
        hbinit = sbt("hbinit", [128, NOWN, 8])
        SA = sbt("SA", [128, 4, 256])
        SB = sbt("SB", [128, 4, 256])
        gwb = sbt("gwb", [128, 4, 8, 128], BF16)
        small = sbt("small", [128, 64])

        ident = cst[:, 0:128]
        Pm = cst[:, 128:256]
        Nm = cst[:, 256:384]
        rampI1 = cst[:, 384:512]
        rampI2 = cst[:, 512:640]
        rampJ = cst[:, 640:641]
        rampJ2 = cst[:, 641:642]
        cvec = par[:, 0:16].rearrange("p (k j) -> p k j", j=2)
        lam = par[:, 16:32]
        ba = par[:, 32:48].rearrange("p (d b) -> p d b", d=2)
        bx = par[:, 48:64].rearrange("p (d b) -> p d b", d=2)
        w5 = par[:, 64:104].rearrange("p (o b) -> p o b", o=5)
        cb = par[:, 104:112]

        def PSB(b, nb=1):
            return ps[:, b * 512:(b + nb) * 512]

        def PSBb(b):
            return ps[:, b * 512:(b + 1) * 512].bitcast(BF16)

        def pk(b, nb=1):
            return [f"ps{i}" for i in range(b, b + nb)]

        def DMA(out, in_, r=(), w=()):
            S.dma(lambda e: e.dma_start(out=out, in_=in_), r, w)

        def MM(out, lhsT, rhs, start, stop, r, w):
            S.pe(lambda e: e.matmul(out, lhsT=lhsT, rhs=rhs, start=start, stop=stop), r, w)

        def TR(out, in_, idn, r, w):
            S.pe(lambda e: e.transpose(out=out, in_=in_, identity=idn), r, w)

        def ACT(out, in_, func, r, w, bias=None, scale=None, accum=None):
            kw = {}
            if bias is not None:
                kw["bias"] = bias
            if scale is not None:
                kw["scale"] = scale
            if accum is not None:
                kw["accum_out"] = accum
            S.act(lambda e: e.activation(out=out, in_=in_, func=func, **kw), r, w)

        def TT(eng, out, in0, in1, op, r, w):
            S.add(eng, lambda e: e.tensor_tensor(out=out, in0=in0, in1=in1, op=op), r, w)

        def TS(eng, out, in0, s1, s2, op0, op1, r, w):
            if s2 is None:
                S.add(eng, lambda e: e.tensor_scalar(out=out, in0=in0, scalar1=s1, scalar2=None, op0=op0), r, w)
            else:
                S.add(eng, lambda e: e.tensor_scalar(out=out, in0=in0, scalar1=s1, scalar2=s2, op0=op0, op1=op1), r, w)

        def STT(eng, out, in0, scalar, in1, op0, op1, r, w):
            S.add(eng, lambda e: e.scalar_tensor_tensor(out=out, in0=in0, scalar=scalar, in1=in1, op0=op0, op1=op1), r, w)

        def CP(eng, out, in_, r, w):
            if eng == "act":
                S.act(lambda e: e.copy(out=out, in_=in_), r, w)
            else:
                S.add(eng, lambda e: e.tensor_copy(out=out, in_=in_), r, w)

        def MEMSET(eng, out, val, w):
            S.add(eng, lambda e: e.memset(out, val), (), w)

        rr = [0]

        def rot3():
            rr[0] += 1
            return ("dve", "act")[rr[0] % 2]

        def bc_last(ap, n):
            shp = list(ap.shape)
            shp[-1] = n
            return ap.broadcast_to(shp)

        def wload(dst, src2d, ncols, stg, key, piece=256, stgkeys=("stg0", "stg1")):
            npc = ncols // piece
            for i in range(npc):
                sb = stg[i % 2]
                sk = stgkeys[i % 2]
                DMA(sb[:, :, 0:piece], src2d[:, i * piece:(i + 1) * piece].rearrange("(k p) n -> p k n", p=128), (), [sk])
                CP(rot3(), dst[:, :, i * piece:(i + 1) * piece], sb[:, :, 0:piece], [sk], [key])

        def rep_load(dst, row_ap, key):
            DMA(dst, row_ap.partition_broadcast(128), (), [key])

        modd = modd_t.ap()
        gains = gains_t.ap()

        def modrow(r, i):
            return modd[r:r + 1, i * D:(i + 1) * D]

        def make_A(dst, tmp, r, i_scale, gain_idx, key, tkey):
            rep_load(dst, modrow(r, i_scale), key)
            rep_load(tmp, gains[gain_idx:gain_idx + 1, :], tkey)
            STT("dve", dst, dst, 1.0, tmp, ALU.add, ALU.mult, [key, tkey], [key])

        epsc = small[:, 8:9]
        MEMSET("dve", epsc, EPS, ["epsc"])
        DMA(cst[:], cst_t.ap(), (), ["cst"])
        DMA(par[:], par_t.ap(), (), ["par"])
        DMA(lg[:], dec_t.ap().partition_broadcast(128), (), ["lg"])
        CP("dve", identb[:], ident, ["cst"], ["identb"])
        ACT(lg[:], lg[:], AF.Exp, ["lg"], ["lg"], scale=-1.0)
        ACT(lg[:], lg[:], AF.Ln, ["lg"], ["lg"], bias=1.0)
        TS("dve", lg[:], lg[:], -1.0, None, ALU.mult, None, ["lg"], ["lg"])
        lnks = math.log(KSCALE)
        tmpm = small[:, 0:1]
        for h in range(4):
            t0 = AR.f32(128)
            TS("dve", t0, Pm, lg[:, h:h + 1], None, ALU.mult, None, ["cst", "lg"], [f"t0_{h}"])
            STT("dve", t0, Nm, lg[:, 4 + h:5 + h], t0, ALU.mult, ALU.add, ["cst", "lg", f"t0_{h}"], [f"t0_{h}"])
            TS("dve", t0, t0, lnks, None, ALU.add, None, [f"t0_{h}"], [f"t0_{h}"])
            ACT(dct[:, h, :], t0, AF.Exp, [f"t0_{h}"], ["dct"])
            ACT(xiA[:, h, :], rampI1, AF.Exp, ["cst", "lg"], ["xiA"], scale=lg[:, h:h + 1])
            ACT(xiB[:, h, :], rampI2, AF.Exp, ["cst", "lg"], ["xiB"], scale=lg[:, 4 + h:5 + h])
            t1 = AR.f32(2)
            TS("dve", t1[:, 0:1], rampJ, lg[:, h:h + 1], lnks, ALU.mult, ALU.add, ["cst", "lg"], [f"t1_{h}"])
            TS("dve", t1[:, 1:2], rampJ2, lg[:, 4 + h:5 + h], lnks, ALU.mult, ALU.add, ["cst", "lg"], [f"t1_{h}"])
            ACT(zeta[:, h:h + 1], t1[:, 0:1], AF.Exp, [f"t1_{h}"], ["zeta"])
            ACT(zeta[:, 4 + h:5 + h], t1[:, 1:2], AF.Exp, [f"t1_{h}"], ["zeta"])
        ACT(decay[:], lg[:], AF.Exp, ["lg"], ["decay"], scale=128.0)
        ACT(nsp8[:], lam, AF.Exp, ["par"], ["nsp8"], scale=-1.0)
        ACT(nsp8[:], nsp8[:], AF.Ln, ["nsp8"], ["nsp8"], bias=1.0)
        TS("dve", nsp8[:], nsp8[:], -8.0, None, ALU.mult, None, ["nsp8"], ["nsp8"])
        nsp8v = nsp8[:].rearrange("p (d b) -> p d b", d=2)
        gst = AR.f32(4 * 8 * 128)
        DMA(gst, gw_t.ap(), (), ["gst"])
        CP("pool", gwb[:].rearrange("p a b c -> p (a b c)"), gst, ["gst"], ["gwb"])
        cvs = AR.f32(8, 2)
        ACT(cvs, cvec, AF.Silu, ["par"], ["cvs"])
        mwb = [AR.f32(8, 512), AR.f32(8, 512)]
        mb2 = AR.f32(6 * D, parts=2)
        DMA(mb2[0:1, :], modb_t.ap(), (), ["mb2"])
        DMA(mb2[1:2, :], modb_t.ap(), (), ["mb2"])
        msb = AR.f32(6 * D, parts=2)
        for n in range(12):
            mw = mwb[n % 2]
            DMA(mw, modw_t.ap()[:, n * 512:(n + 1) * 512].rearrange("(k p) n -> p k n", p=128), (), [f"mw{n % 2}"])
            for kc in range(8):
                MM(ps[0:2, (n % 2) * 512:(n % 2) * 512 + 512], cvs[:, kc, :], mw[:, kc, :], kc == 0, kc == 7,
                   ["cvs", f"mw{n % 2}"], [f"ps{n % 2}"])
            TT("dve", msb[:, n * 512:(n + 1) * 512], ps[0:2, (n % 2) * 512:(n % 2) * 512 + 512],
               mb2[:, n * 512:(n + 1) * 512], ALU.add, [f"ps{n % 2}", "mb2"], ["msb"])
        DMA(modd, msb, ["msb"], ["modd"])
        S.barrier()
        AR.reset()

        if phases in ("all", "peer"):
            sf = [AR.f32(4096), AR.f32(4096)]
            sbf = [AR.bf16(4096), AR.bf16(4096)]
            utb = utb_t.ap()
            vb = vb_t.ap()
            i = 0
            for eb in range(32):
                b = i % 2
                DMA(sf[b].rearrange("p (k e) -> p k e", k=8),
                    ut_t.ap()[:, eb * 512:(eb + 1) * 512].rearrange("(k p) e -> p k e", p=128), (), [f"sf{b}"])
                CP(rot3(), sbf[b], sf[b], [f"sf{b}"], [f"sbf{b}"])
                DMA(utb[eb].rearrange("p k n -> p (k n)"), sbf[b], [f"sbf{b}"], ())
                i += 1
            for eb in range(32):
                b = i % 2
                DMA(sf[b].rearrange("p (c d) -> p c d", c=4),
                    v_t.ap()[eb * 512:(eb + 1) * 512, :].rearrange("(c p) d -> p c d", p=128), (), [f"sf{b}"])
                CP(rot3(), sbf[b], sf[b], [f"sf{b}"], [f"sbf{b}"])
                DMA(vb[eb], sbf[b], [f"sbf{b}"], ())
                i += 1
            S.barrier()
            AR.reset()

        def front(xt, xk, Arep, shrep, repkeys, hx, junk, hxT_dst, hxTkey, psbank, ssq, parts=128, ncol=128):
            ACT(junk[0:parts, :], xt, AF.Square, [xk], ["junk", "ssq"], accum=ssq[0:parts, 0:1])
            ACT(ssq[0:parts, 1:2], ssq[0:parts, 0:1], AF.Ln, ["ssq", "epsc"], ["ssq"], bias=epsc[0:parts, 0:1], scale=1.0 / D)
            ACT(ssq[0:parts, 2:3], ssq[0:parts, 1:2], AF.Exp, ["ssq"], ["ssq"], scale=-0.5)
            STT("dve", junk[0:parts, :], xt, ssq[0:parts, 2:3], Arep[0:parts, :], ALU.mult, ALU.mult,
                [xk, "ssq"] + repkeys, ["junk"])
            TT("pool", hx[0:parts, :], junk[0:parts, :], shrep[0:parts, :], ALU.add, ["junk"] + repkeys, ["hx"])
            pt = PSBb(psbank)
            for kc in range(8):
                TR(pt[:, kc * ncol:(kc + 1) * ncol], hx[0:parts, kc * 128:(kc + 1) * 128], identb[0:parts, 0:parts],
                   ["hx", "identb"], pk(psbank))
            CP("act", hxT_dst, pt[:, 0:8 * ncol].rearrange("p (k n) -> p k n", k=8), pk(psbank), [hxTkey])

        def load_phase_common(need_halo):
            d = {}
            d["stg"] = [AR.f32(8, 256), AR.f32(8, 256)]
            d["x"] = [AR.f32(D), AR.f32(D)]
            d["junk"] = AR.f32(D)
            d["hx"] = AR.bf16(D)
            d["ssq"] = AR.f32(4)
            if need_halo:
                d["xh"] = AR.f32(D)
                d["hxh"] = AR.bf16(D)
                d["ssqh"] = AR.f32(4)
                d["junkh"] = AR.f32(D)
                d["hxT"] = AR.bf16(8, 132)
            else:
                d["hxT"] = AR.bf16(8, 128)
            return d

        def lru_gates(W, dr, uc, ucb, hkey_prefix):
            r_, i_, la, a2, bt = W["r"], W["i"], W["la"], W["a2"], W["bt"]
            for blk in range(8):
                MM(PSB(4, 2)[:, blk * 128:(blk + 1) * 128], gwb[:, 2 * dr, blk, :], ucb[:, blk, :], True, True,
                   ["gwb", "ucb"], pk(4, 2))
            for blk in range(8):
                MM(PSB(6, 2)[:, blk * 128:(blk + 1) * 128], gwb[:, 2 * dr + 1, blk, :], ucb[:, blk, :], True, True,
                   ["gwb", "ucb"], pk(6, 2))
            TT("dve", r_, PSB(4, 2).rearrange("p (b t) -> p b t", b=8), bc_last(ba[:, dr, :].unsqueeze(2), 128), ALU.add,
               pk(4, 2) + ["par"], ["r"])
            TT("dve", i_, PSB(6, 2).rearrange("p (b t) -> p b t", b=8), bc_last(bx[:, dr, :].unsqueeze(2), 128), ALU.add,
               pk(6, 2) + ["par"], ["i"])
            ACT(r_, r_, AF.Sigmoid, ["r"], ["r"])
            ACT(i_, i_, AF.Sigmoid, ["i"], ["i"])
            TT("dve", la, r_, bc_last(nsp8v[:, dr, :].unsqueeze(2), 128), ALU.mult, ["r", "nsp8"], ["la"])
            ACT(a2, la, AF.Exp, ["la"], ["a2"], scale=2.0)
            ACT(la, la, AF.Exp, ["la"], ["la"])
            TS("dve", a2, a2, -1.0, 1.0, ALU.mult, ALU.add, ["a2"], ["a2"])
            ACT(a2, a2, AF.Sqrt, ["a2"], ["a2"])
            TT("pool", bt, i_, uc, ALU.mult, ["i", "uc"], ["bt"])
            TT("pool", bt, bt, a2, ALU.mult, ["bt", "a2"], ["bt"])

        def lru_scan(W, dr, hout, hkey):
            la, bt = W["la"], W["bt"]
            first = 0 if dr == 0 else 127
            last = 127 if dr == 0 else 0
            tmp8 = W["tmp8"]
            TT("dve", tmp8, la[:, :, first], hprev[:, dr, :], ALU.mult, ["la", "hprev"], ["tmp8"])
            TT("dve", bt[:, :, first], bt[:, :, first], tmp8, ALU.add, ["bt", "tmp8"], ["bt"])
            MEMSET("dve", la[:, :, first], 0.0, ["la"])
            la2 = la.rearrange("p b t -> p (b t)")
            bt2 = bt.rearrange("p b t -> p (b t)")
            h2 = hout.rearrange("p b t -> p (b t)")
            if dr == 0:
                S.dve(lambda e: e.tensor_tensor_scan(out=h2, data0=la2, data1=bt2, initial=0.0, op0=ALU.mult, op1=ALU.add),
                      ["la", "bt"], [hkey])
            else:
                ro, ra, rb = _rev(h2), _rev(la2), _rev(bt2)
                S.dve(lambda e: e.tensor_tensor_scan(out=ro, data0=ra, data1=rb, initial=0.0, op0=ALU.mult, op1=ALU.add),
                      ["la", "bt"], [hkey])
            CP("dve", hprev[:, dr, :], hout[:, :, last], [hkey], ["hprev"])

        def conv(W, uT, uc, ucb):
            m0, m1 = W["m0"], W["m1"]
            TT("dve", uc, uT[:, :, 0:128], bc_last(w5[:, 0, :].unsqueeze(2), 128), ALU.mult, ["uT", "par"], ["uc"])
            TT("dve", uc, uc, bc_last(cb.unsqueeze(2), 128), ALU.add, ["uc", "par"], ["uc"])
            for o in range(1, 5):
                m = m0 if o % 2 else m1
                mk = "m0" if o % 2 else "m1"
                TT("pool", m, uT[:, :, o:o + 128], bc_last(w5[:, o, :].unsqueeze(2), 128), ALU.mult, ["uT", "par"], [mk])
                TT("dve", uc, uc, m, ALU.add, ["uc", mk], ["uc"])
            CP("act", ucb, uc, ["uc"], ["ucb"])

        def halo_front(W, src, j, nchunks, Arep, shrep, repkeys, psbank):
            xh, hxh = W["xh"], W["hxh"]
            lo = j * 128 - 2
            hi = j * 128 + 128
            lo_c = max(lo, 0)
            hi_c = min(hi, nchunks * 128 - 2)
            DMA(xh[0:2, :], src[lo_c:lo_c + 2, :], (), ["xh"])
            DMA(xh[2:4, :], src[hi_c:hi_c + 2, :], (), ["xh"])
            front(xh[0:4, :], "xh", Arep, shrep, repkeys, hxh, W["junkh"], W["hxT4"], "hxT4", psbank, W["ssqh"], parts=4, ncol=4)
            CP("dve", W["hxT"][:, :, 0:2], W["hxT4"][:, :, 0:2], ["hxT4"], ["hxT"])
            CP("dve", W["hxT"][:, :, 130:132], W["hxT4"][:, :, 2:4], ["hxT4"], ["hxT"])

        def proj_uT(W, wlx, j, nchunks):
            hxT, uT = W["hxT"], W["uT"]
            pu = PSB(4, 4).rearrange("p (b t) -> p b t", b=8)
            for blk in range(8):
                for kc in range(8):
                    MM(pu[:, blk, 0:132], wlx[:, kc, blk * 128:(blk + 1) * 128], hxT[:, kc, :], kc == 0, kc == 7,
                       ["wlx", "hxT"], pk(4, 4))
            CP("act", uT, pu[:, :, 0:132], pk(4, 4), ["uT"])
            if j == 0:
                MEMSET("dve", uT[:, :, 0:2], 0.0, ["uT"])
            if j == nchunks - 1:
                MEMSET("dve", uT[:, :, 130:132], 0.0, ["uT"])

        def rope_tok(W, src_ps, srckeys, rp, rpk, dst, dstkey):
            t1, t2 = W["t1"], W["t2"]
            s4 = src_ps.rearrange("p (h c) -> p h c", h=4)
            Cb = rp[:, 0:128].unsqueeze(1).broadcast_to([128, 4, 128])
            TT("dve", t1, s4, Cb, ALU.mult, srckeys + [rpk], ["t1"])
            s5 = src_ps.rearrange("p (h a b c) -> p h a b c", h=4, a=2, b=2)
            t25 = t2.rearrange("p h (a b c) -> p h a b c", a=2, b=2)
            sg5 = rp[:, 128:256].rearrange("p (a b c) -> p a b c", a=2, b=2)
            TT("dve", t25[:, :, :, 0, :], s5[:, :, :, 1, :], sg5[:, :, 0, :].unsqueeze(1).broadcast_to([128, 4, 2, 32]),
               ALU.mult, srckeys + [rpk], ["t2"])
            TT("dve", t25[:, :, :, 1, :], s5[:, :, :, 0, :], sg5[:, :, 1, :].unsqueeze(1).broadcast_to([128, 4, 2, 32]),
               ALU.mult, srckeys + [rpk], ["t2"])
            TT("pool", t1, t1, t2, ALU.add, ["t1", "t2"], ["t1"])

        sbs = sbs_t.ap()

        if phases in ("all", "mix"):
            W = load_phase_common(True)
            W["hxT4"] = AR.bf16(8, 4)
            wk = AR.bf16(8, 512)
            wv = AR.bf16(8, 1024)
            wlx = AR.bf16(8, 1024)
            rep = [AR.f32(D) for _ in range(4)]
            reptmp = W["junk"]
            W["rp"] = [AR.f32(256), AR.f32(256)]
            W["t1"] = AR.f32(4, 128)
            W["t2"] = AR.f32(4, 128)
            kz = AR.bf16(4, 128)
            vbf = AR.bf16(4, 256)
            sbf_ = AR.bf16(4, 256)
            W["uT"] = AR.f32(8, 132)
            W["m0"] = AR.f32(8, 128)
            W["m1"] = AR.f32(8, 128)
            uc = AR.f32(8, 128)
            ucb = AR.bf16(8, 128)
            for nm in ("r", "i", "la", "a2", "bt"):
                W[nm] = AR.f32(8, 128)
            hh_ = AR.f32(8, 128)
            W["tmp8"] = AR.f32(8)
            wload(wk, win_t.ap()[:, CK:CK + 512], 512, W["stg"], "wk")
            wload(wv, win_t.ap()[:, CV:CV + 1024], 1024, W["stg"], "wv")
            wload(wlx, win_t.ap()[:, CLX:CLX + 1024], 1024, W["stg"], "wlx")
            make_A(rep[0], reptmp, 0, 1, 0, "rep0", "junk")
            rep_load(rep[1], modrow(0, 0), "rep1")
            make_A(rep[2], reptmp, 1, 1, 0, "rep2", "junk")
            rep_load(rep[3], modrow(1, 0), "rep3")
            MEMSET("dve", SA[:], 0.0, ["SA"])
            MEMSET("dve", SB[:], 0.0, ["SB"])
            MEMSET("dve", hprev[:], 0.0, ["hprev"])

            def p1_chunk(src, nchunks, j, dr, is_ctx, xi, store_idx):
                Arep, shrep, rk = (rep[2], rep[3], ["rep2", "rep3"]) if is_ctx else (rep[0], rep[1], ["rep0", "rep1"])
                xt = W["x"][xi]
                xk = f"x{xi}"
                DMA(xt, src[j * 128:(j + 1) * 128, :], (), [xk])
                if not is_ctx:
                    DMA(W["rp"][xi], rope[j * 128:(j + 1) * 128, :], (), [f"rp{xi}"])
                front(xt, xk, Arep, shrep, rk, W["hx"], W["junk"], W["hxT"][:, :, 2:130], "hxT", 0, W["ssq"])
                halo_front(W, src, j, nchunks, Arep, shrep, rk, 0)
                for kc in range(8):
                    MM(PSB(1), W["hxT"][:, kc, 2:130], wk[:, kc, :], kc == 0, kc == 7, ["hxT", "wk"], pk(1))
                for n in range(2):
                    for kc in range(8):
                        MM(PSB(2 + n), W["hxT"][:, kc, 2:130], wv[:, kc, n * 512:(n + 1) * 512], kc == 0, kc == 7,
                           ["hxT", "wv"], pk(2 + n))
                Sd = SA if dr == 0 else SB
                Sk = "SA" if dr == 0 else "SB"
                zt = zeta[:, 4 * dr:4 * dr + 4]
                if is_ctx:
                    TT("dve", kz, PSB(1).rearrange("p (h c) -> p h c", h=4), bc_last(zt.unsqueeze(2), 128), ALU.mult,
                       pk(1) + ["zeta"], ["kz"])
                else:
                    rope_tok(W, PSB(1), pk(1), W["rp"][xi], f"rp{xi}", None, None)
                    TT("dve", kz, W["t1"], bc_last(zt.unsqueeze(2), 128), ALU.mult, ["t1", "zeta"], ["kz"])
                CP("act", vbf, PSB(2, 2).rearrange("p (h c) -> p h c", h=4), pk(2, 2), ["vbf"])
                if store_idx is not None:
                    CP("pool", sbf_, Sd[:], [Sk], ["sbf_"])
                    DMA(sbs[store_idx], sbf_.rearrange("p h c -> p (h c)"), ["sbf_"], ())
                    CP("dve", hbinit[:, store_idx, :], hprev[:, 1, :], ["hprev"], ["hbinit"])
                for hh in range(4):
                    MM(PSB(2, 2)[:, hh * 256:(hh + 1) * 256], kz[:, hh, :], vbf[:, hh, :], True, True, ["kz", "vbf"], pk(2, 2))
                for hh in range(4):
                    STT("dve", Sd[:, hh, :], Sd[:, hh, :], decay[:, 4 * dr + hh:4 * dr + hh + 1], PSB(2, 2)[:, hh * 256:(hh + 1) * 256],
                        ALU.mult, ALU.add, [Sk, "decay"] + pk(2, 2), [Sk])
                proj_uT(W, wlx, j, nchunks)
                conv(W, W["uT"], uc, ucb)
                lru_gates(W, dr, uc, ucb, "h")
                lru_scan(W, dr, hh_, "hh")

            xi = 0
            for (j, dr) in ((0, 0), (1, 0), (1, 1), (0, 1)):
                p1_chunk(ctxf, 2, j, dr, True, xi, None)
                xi ^= 1
            for j in range(NCH - 1, -1, -1):
                p1_chunk(xf, NCH, j, 1, False, xi, j if j < NOWN else None)
                xi ^= 1
            S.barrier()
            AR.reset()

            W = load_phase_common(False)
            wqk = AR.bf16(8, 1024)
            wv = AR.bf16(8, 1024)
            wp3 = AR.bf16(8, 1024)
            wro = AR.bf16(8, 1024)
            rep = [AR.f32(D) for _ in range(2)]
            W["rp"] = [AR.f32(256), AR.f32(256)]
            W["t1"] = AR.f32(4, 128)
            W["t2"] = AR.f32(4, 128)
            qr = AR.bf16(4, 128)
            kr = AR.bf16(4, 128)
            kz = AR.bf16(4, 128)
            qT = AR.bf16(4, 128)
            qxA = AR.bf16(4, 128)
            qxB = AR.bf16(4, 128)
            kT = AR.bf16(4, 128)
            vbf = AR.bf16(4, 256)
            p3s = AR.f32(D)
            sT = AR.bf16(4, 128)
            osb = AR.f32(4, 256)
            osq = AR.f32(4, 256)
            st4 = AR.f32(16)
            retg = AR.bf16(D)
            retgT = AR.bf16(8, 128)
            rout = [AR.f32(D), AR.f32(D)]
            sabf = AR.bf16(4, 256)
            sbbf = [AR.bf16(4, 256), AR.bf16(4, 256)]
            wload(wqk, win_t.ap()[:, CQ:CQ + 1024], 1024, W["stg"], "wqk")
            wload(wv, win_t.ap()[:, CV:CV + 1024], 1024, W["stg"], "wv")
            wload(wp3, win_t.ap()[:, CP3:CP3 + 1024], 1024, W["stg"], "wp3")
            wload(wro, wro_t.ap(), 1024, W["stg"], "wro")
            make_A(rep[0], W["junk"], 0, 1, 0, "rep0", "junk")
            rep_load(rep[1], modrow(0, 0), "rep1")
            CP("pool", sabf, SA[:], ["SA"], ["sabf"])
            retsc = ret_t.ap()
            for j in range(NOWN):
                xi = j % 2
                xt = W["x"][xi]
                xk = f"x{xi}"
                DMA(xt, xf[j * 128:(j + 1) * 128, :], (), [xk])
                DMA(W["rp"][xi], rope[j * 128:(j + 1) * 128, :], (), [f"rp{xi}"])
                DMA(sbbf[xi].rearrange("p h c -> p (h c)"), sbs[j], (), [f"sbbf{xi}"])
                front(xt, xk, rep[0], rep[1], ["rep0", "rep1"], W["hx"], W["junk"], W["hxT"], "hxT", 0, W["ssq"])
                for n in range(2):
                    for kc in range(8):
                        MM(PSB(1 + n), W["hxT"][:, kc, :], wqk[:, kc, n * 512:(n + 1) * 512], kc == 0, kc == 7,
                           ["hxT", "wqk"], pk(1 + n))
                for n in range(2):
                    for kc in range(8):
                        MM(PSB(3 + n), W["hxT"][:, kc, :], wv[:, kc, n * 512:(n + 1) * 512], kc == 0, kc == 7,
                           ["hxT", "wv"], pk(3 + n))
                for n in range(2):
                    for kc in range(8):
                        MM(PSB(5 + n), W["hxT"][:, kc, :], wp3[:, kc, n * 512:(n + 1) * 512], kc == 0, kc == 7,
                           ["hxT", "wp3"], pk(5 + n))
                rope_tok(W, PSB(1), pk(1), W["rp"][xi], f"rp{xi}", None, None)
                CP("act", qr, W["t1"], ["t1"], ["qr"])
                rope_tok(W, PSB(2), pk(2), W["rp"][xi], f"rp{xi}", None, None)
                CP("act", kr, W["t1"], ["t1"], ["kr"])
                TT("dve", kz, W["t1"], bc_last(zeta[:, 0:4].unsqueeze(2), 128), ALU.mult, ["t1", "zeta"], ["kz"])
                CP("act", vbf, PSB(3, 2).rearrange("p (h c) -> p h c", h=4), pk(3, 2), ["vbf"])
                ACT(p3s, PSB(5, 2), AF.Silu, pk(5, 2), ["p3s"])
                pt = PSBb(7)
                for hh in range(4):
                    TR(pt[:, hh * 128:(hh + 1) * 128], qr[:, hh, :], identb[:], ["qr", "identb"], pk(7))
                for hh in range(4):
                    TR(pt[:, 512 + hh * 128:512 + (hh + 1) * 128], kr[:, hh, :], identb[:], ["kr", "identb"], pk(7))
                ptq = pt[:, 0:512].rearrange("p (h t) -> p h t", h=4)
                CP("act", qT, ptq, pk(7), ["qT"])
                TT("dve", qxA, ptq, xiA[:], ALU.mult, pk(7) + ["xiA"], ["qxA"])
                TT("dve", qxB, ptq, xiB[:], ALU.mult, pk(7) + ["xiB"], ["qxB"])
                CP("act", kT, pt[:, 512:1024].rearrange("p (h t) -> p h t", h=4), pk(7), ["kT"])
                for hh in range(4):
                    MM(PSB(1)[:, hh * 128:(hh + 1) * 128], kT[:, hh, :], qT[:, hh, :], True, True, ["kT", "qT"], pk(1))
                TT("dve", sT, PSB(1).rearrange("p (h t) -> p h t", h=4), dct[:], ALU.mult, pk(1) + ["dct"], ["sT"])
                for hh in range(4):
                    o_ps = PSB(3, 2)[:, hh * 256:(hh + 1) * 256]
                    MM(o_ps, sT[:, hh, :], vbf[:, hh, :], True, False, ["sT", "vbf"], pk(3, 2))
                    MM(o_ps, qxA[:, hh, :], sabf[:, hh, :], False, False, ["qxA", "sabf"], pk(3, 2))
                    MM(o_ps, qxB[:, hh, :], sbbf[xi][:, hh, :], False, True, ["qxB", f"sbbf{xi}"], pk(3, 2))
                for hh in range(4):
                    MM(PSB(5, 2)[:, hh * 256:(hh + 1) * 256], kz[:, hh, :], vbf[:, hh, :], True, True, ["kz", "vbf"], pk(5, 2))
                CP("act", osb, PSB(3, 2).rearrange("p (h c) -> p h c", h=4), pk(3, 2), ["osb"])
                ACT(osq, osb, AF.Square, ["osb"], ["osq"])
                S.dve(lambda e, o=st4[:, 0:4], i=osb: e.reduce_sum(out=o, in_=i, axis=mybir.AxisListType.X), ["osb"], ["st4"])
                S.dve(lambda e, o=st4[:, 4:8], i=osq: e.reduce_sum(out=o, in_=i, axis=mybir.AxisListType.X), ["osq"], ["st4"])
                TS("dve", st4[:, 0:8], st4[:, 0:8], 1.0 / 256, None, ALU.mult, None, ["st4"], ["st4"])
                TT("dve", st4[:, 8:12], st4[:, 0:4], st4[:, 0:4], ALU.mult, ["st4"], ["st4"])
                TT("dve", st4[:, 8:12], st4[:, 4:8], st4[:, 8:12], ALU.subtract, ["st4"], ["st4"])
                ACT(st4[:, 8:12], st4[:, 8:12], AF.Ln, ["st4", "epsc"], ["st4"], bias=epsc[:, 0:1])
                ACT(st4[:, 8:12], st4[:, 8:12], AF.Exp, ["st4"], ["st4"], scale=-0.5)
                TT("dve", osb, osb, bc_last(st4[:, 0:4].unsqueeze(2), 256), ALU.subtract, ["osb", "st4"], ["osb"])
                TT("dve", osb, osb, bc_last(st4[:, 8:12].unsqueeze(2), 256), ALU.mult, ["osb", "st4"], ["osb"])
                if "ret" in dbg:
                    DMA(dbg["ret"].ap()[j * 128:(j + 1) * 128, :], osb.rearrange("p h c -> p (h c)"), ["osb"], ())
                TT("pool", retg, osb.rearrange("p h c -> p (h c)"), p3s, ALU.mult, ["osb", "p3s"], ["retg"])
                pt0 = PSBb(0)
                for kc in range(8):
                    TR(pt0[:, kc * 128:(kc + 1) * 128], retg[:, kc * 128:(kc + 1) * 128], identb[:], ["retg", "identb"], pk(0))
                CP("act", retgT, pt0.rearrange("p (k n) -> p k n", k=8), pk(0), ["retgT"])
                for n in range(2):
                    for kc in range(8):
                        MM(PSB(1 + n), retgT[:, kc, :], wro[:, kc, n * 512:(n + 1) * 512], kc == 0, kc == 7,
                           ["retgT", "wro"], pk(1 + n))
                CP("act", rout[xi], PSB(1, 2), pk(1, 2), [f"rout{xi}"])
                DMA(retsc[j * 128:(j + 1) * 128, :], rout[xi], [f"rout{xi}"], ())
                for hh in range(4):
                    STT("dve", SA[:, hh, :], SA[:, hh, :], decay[:, hh:hh + 1], PSB(5, 2)[:, hh * 256:(hh + 1) * 256],
                        ALU.mult, ALU.add, ["SA", "decay"] + pk(5, 2), ["SA"])
                CP("pool", sabf, SA[:], ["SA"], ["sabf"])
            S.barrier()
            AR.reset()

            W = load_phase_common(True)
            W["hxT4"] = AR.bf16(8, 4)
            wlx = AR.bf16(8, 1024)
            wp5 = AR.bf16(8, 1024)
            wlo = AR.bf16(8, 1024)
            rep = [AR.f32(D) for _ in range(2)]
            W["uT"] = AR.f32(8, 132)
            W["m0"] = AR.f32(8, 128)
            W["m1"] = AR.f32(8, 128)
            uc = AR.f32(8, 128)
            ucb = AR.bf16(8, 128)
            for nm in ("r", "i", "la", "a2", "bt"):
                W[nm] = AR.f32(8, 128)
            hA = AR.f32(8, 128)
            hB = AR.f32(8, 128)
            W["tmp8"] = AR.f32(8)
            p5g = AR.f32(8, 128)
            yg = AR.bf16(8, 128)
            lout = [AR.f32(D), AR.f32(D)]
            wload(wlx, win_t.ap()[:, CLX:CLX + 1024], 1024, W["stg"], "wlx")
            wload(wp5, win_t.ap()[:, CP5:CP5 + 1024], 1024, W["stg"], "wp5")
            wload(wlo, wlo_t.ap(), 1024, W["stg"], "wlo")
            make_A(rep[0], W["junk"], 0, 1, 0, "rep0", "junk")
            rep_load(rep[1], modrow(0, 0), "rep1")
            lrusc = lru_t.ap()
            for j in range(NOWN):
                xi = j % 2
                xt = W["x"][xi]
                xk = f"x{xi}"
                DMA(xt, xf[j * 128:(j + 1) * 128, :], (), [xk])
                front(xt, xk, rep[0], rep[1], ["rep0", "rep1"], W["hx"], W["junk"], W["hxT"][:, :, 2:130], "hxT", 0, W["ssq"])
                halo_front(W, xf, j, NCH, rep[0], rep[1], ["rep0", "rep1"], 0)
                proj_uT(W, wlx, j, NCH)
                p5ps = PSB(1, 2).rearrange("p (b t) -> p b t", b=8)
                for blk in range(8):
                    for kc in range(8):
                        MM(p5ps[:, blk, :], wp5[:, kc, blk * 128:(blk + 1) * 128], W["hxT"][:, kc, 2:130], kc == 0, kc == 7,
                           ["wp5", "hxT"], pk(1, 2))
                ACT(p5g, p5ps, AF.Gelu_apprx_tanh, pk(1, 2), ["p5g"])
                conv(W, W["uT"], uc, ucb)
                lru_gates(W, 0, uc, ucb, "hA")
                lru_scan(W, 0, hA, "hA")
                CP("dve", hprev[:, 1, :], hbinit[:, j, :], ["hbinit"], ["hprev"])
                lru_gates(W, 1, uc, ucb, "hB")
                lru_scan(W, 1, hB, "hB")
                TT("pool", hA, hA, hB, ALU.add, ["hA", "hB"], ["hA"])
                TT("dve", yg, hA, p5g, ALU.mult, ["hA", "p5g"], ["yg"])
                for n in range(2):
                    for blk in range(8):
                        MM(PSB(1 + n), yg[:, blk, :], wlo[:, blk, n * 512:(n + 1) * 512], blk == 0, blk == 7,
                           ["yg", "wlo"], pk(1 + n))
                CP("act", lout[xi], PSB(1, 2), pk(1, 2), [f"lout{xi}"])
                DMA(lrusc[j * 128:(j + 1) * 128, :], lout[xi], [f"lout{xi}"], ())
            S.barrier()
            AR.reset()

            W = load_phase_common(False)
            wp67 = AR.bf16(8, 2048)
            wo = AR.bf16(8, 1024)
            rep = [AR.f32(D) for _ in range(3)]
            rin = [AR.f32(D), AR.f32(D)]
            lin = [AR.f32(D), AR.f32(D)]
            g6 = AR.f32(D)
            g7 = AR.f32(D)
            ym = AR.bf16(D)
            ymT = AR.bf16(8, 128)
            xnew = [AR.f32(D), AR.f32(D)]
            wload(wp67, win_t.ap()[:, CP6:CP6 + 2048], 2048, W["stg"], "wp67")
            wload(wo, wo_t.ap(), 1024, W["stg"], "wo")
            make_A(rep[0], W["junk"], 0, 1, 0, "rep0", "junk")
            rep_load(rep[1], modrow(0, 0), "rep1")
            rep_load(rep[2], modrow(0, 2), "rep2")
            xnsc = xn_t.ap()
            for j in range(NOWN):
                xi = j % 2
                xt = W["x"][xi]
                xk = f"x{xi}"
                DMA(xt, xf[j * 128:(j + 1) * 128, :], (), [xk])
                DMA(rin[xi], retsc[j * 128:(j + 1) * 128, :], (), [f"rin{xi}"])
                DMA(lin[xi], lrusc[j * 128:(j + 1) * 128, :], (), [f"lin{xi}"])
                front(xt, xk, rep[0], rep[1], ["rep0", "rep1"], W["hx"], W["junk"], W["hxT"], "hxT", 0, W["ssq"])
                for n in range(4):
                    for kc in range(8):
                        MM(PSB(1 + n), W["hxT"][:, kc, :], wp67[:, kc, n * 512:(n + 1) * 512], kc == 0, kc == 7,
                           ["hxT", "wp67"], pk(1 + n))
                ACT(g6, PSB(1, 2), AF.Sigmoid, pk(1, 2), ["g6"])
                ACT(g7, PSB(3, 2), AF.Sigmoid, pk(3, 2), ["g7"])
                TT("dve", g6, g6, rin[xi], ALU.mult, ["g6", f"rin{xi}"], ["g6"])
                TT("pool", g7, g7, lin[xi], ALU.mult, ["g7", f"lin{xi}"], ["g7"])
                TT("dve", ym, g6, g7, ALU.add, ["g6", "g7"], ["ym"])
                pt0 = PSBb(5)
                for kc in range(8):
                    TR(pt0[:, kc * 128:(kc + 1) * 128], ym[:, kc * 128:(kc + 1) * 128], identb[:], ["ym", "identb"], pk(5))
                CP("act", ymT, pt0.rearrange("p (k n) -> p k n", k=8), pk(5), ["ymT"])
                for n in range(2):
                    for kc in range(8):
                        MM(PSB(6 + n), ymT[:, kc, :], wo[:, kc, n * 512:(n + 1) * 512], kc == 0, kc == 7,
                           ["ymT", "wo"], pk(6 + n))
                TT("dve", xnew[xi], PSB(6, 2), rep[2], ALU.mult, pk(6, 2) + ["rep2"], [f"xnew{xi}"])
                TT("pool", xnew[xi], xnew[xi], xt, ALU.add, [f"xnew{xi}", xk], [f"xnew{xi}"])
                DMA(xnsc[j * 128:(j + 1) * 128, :], xnew[xi], [f"xnew{xi}"], ())
                if "xn" in dbg:
                    DMA(dbg["xn"].ap()[j * 128:(j + 1) * 128, :], xnew[xi], [f"xnew{xi}"], ())
            S.barrier()
            AR.reset()

        if phases in ("all", "peer"):
            xnsc = xn_t.ap() if phases == "all" else xf
            wq = AR.bf16(8, 1024)
            keysb = AR.bf16(8, 256)
            rep = [AR.f32(D) for _ in range(4)]
            junk = AR.f32(D)
            hx2 = AR.bf16(D)
            qTs = AR.bf16(8, 128)
            ssq = AR.f32(4)
            T = []
            for t in range(2):
                T.append(dict(xn=AR.f32(D), hT=AR.bf16(8, 128), s=AR.f32(8, 2, 128), a16=AR.f32(8, 2, 16),
                              top=AR.f32(8, 16), st=AR.f32(32), wsum=AR.f32(512), G=AR.f32(512),
                              coef=AR.bf16(512), coefT=AR.bf16(4, 128)))
            KK = [AR.f32(8, 128) for _ in range(3)]
            K0f, K1f, K2f = [k.rearrange("p a b -> p (a b)") for k in KK]
            CC = [AR.f32(4, 128) for _ in range(3)]
            EE = [AR.bf16(4, 128) for _ in range(3)]
            WW = [AR.bf16(4, 128) for _ in range(8)]
            ublk = [AR.bf16(8, 512), AR.bf16(8, 512)]
            vblk = [AR.bf16(4, 1024) for _ in range(3)]
            outt = AR.f32(D)
            wload(wq, wq_t.ap(), 1024, [KK[1], KK[2]], "wq", piece=128, stgkeys=("K1", "K2"))
            DMA(K0f, keys_t.ap()[:, 0:1024], (), ["K0"])
            CP("dve", keysb.rearrange("p a b -> p (a b)")[:, 0:1024], K0f, ["K0"], ["keysb"])
            DMA(K1f, keys_t.ap()[:, 1024:2048], (), ["K1"])
            CP("dve", keysb.rearrange("p a b -> p (a b)")[:, 1024:2048], K1f, ["K1"], ["keysb"])
            make_A(rep[0], junk, 0, 4, 1, "rep0", "junk")
            rep_load(rep[1], modrow(0, 3), "rep1")
            rep_load(rep[2], modrow(0, 5), "rep2")
            rep_load(rep[3], gains[2:3, :], "rep3")
            utb = utb_t.ap()
            vb = vb_t.ap()
            outd = out_t.ap()
            NBLK = OWN // 256
            if "peer1" in dbg:
                NBLK = 1
            WB = (5, 7)
            for blk in range(NBLK):
                for t in range(2):
                    Tt = T[t]
                    tk = f"T{t}"
                    row0 = blk * 256 + t * 128
                    DMA(Tt["xn"], xnsc[row0:row0 + 128, :], (), [tk + "xn"])
                    front(Tt["xn"], tk + "xn", rep[0], rep[1], ["rep0", "rep1"], hx2, junk, Tt["hT"], tk + "hT", 7, ssq)
                    qps = PSB(5, 2).rearrange("p (h t) -> p h t", h=8)
                    for hh in range(8):
                        for kc in range(8):
                            MM(qps[:, hh, :], wq[:, kc, hh * 128:(hh + 1) * 128], Tt["hT"][:, kc, :], kc == 0, kc == 7,
                               ["wq", tk + "hT"], pk(5, 2))
                    CP("act", qTs, qps, pk(5, 2), ["qTs"])
                    sps = PSB(0, 4).rearrange("p (h c) -> p h c", h=8)
                    for hh in range(8):
                        MM(sps[:, hh, :], qTs[:, hh, :], keysb[:, hh, :], True, True, ["qTs", "keysb"], pk(0, 4))
                    s = Tt["s"]
                    CP("act", s.rearrange("p h a k -> p h (a k)"), sps, pk(0, 4), [tk + "s"])
                    a16 = Tt["a16"]
                    for g in range(2):
                        sl = [(g * 4 + q, p_) for q in range(4) for p_ in range(2)]
                        for n_, (hh, p_) in enumerate(sl):
                            S.dve(lambda e, o=a16[:, hh, p_, 0:8], i=s[:, hh, p_, :]: e.max(out=o, in_=i), [tk + "s"], [tk + "a16"])
                        for n_, (hh, p_) in enumerate(sl):
                            S.dve(lambda e, o=KK[0][:, n_, :], r_=a16[:, hh, p_, 0:8], i=s[:, hh, p_, :]:
                                  e.match_replace(out=o, in_to_replace=r_, in_values=i, imm_value=-1e30),
                                  [tk + "s", tk + "a16"], ["K0"])
                        for n_, (hh, p_) in enumerate(sl):
                            S.dve(lambda e, o=a16[:, hh, p_, 8:16], i=KK[0][:, n_, :]: e.max(out=o, in_=i), ["K0"], [tk + "a16"])
                    cflat = [K1f, K2f]
                    for g in range(2):
                        cbuf = cflat[g].rearrange("p (h r q) -> p h r q", h=4, r=16)
                        ckey = ("K1", "K2")[g]
                        in0 = a16[:, g * 4:(g + 1) * 4, 0, :].unsqueeze(3).broadcast_to([128, 4, 16, 16])
                        in1 = a16[:, g * 4:(g + 1) * 4, 1, :].unsqueeze(2).broadcast_to([128, 4, 16, 16])
                        TT("dve", cbuf, in0, in1, ALU.add, [tk + "a16"], [ckey])
                    top = Tt["top"]
                    for hh in range(8):
                        cv_ = cflat[hh // 4][:, (hh % 4) * 256:(hh % 4 + 1) * 256]
                        ck_ = ("K1", "K2")[hh // 4]
                        S.dve(lambda e, o=top[:, hh, 0:8], i=cv_: e.max(out=o, in_=i), [ck_], [tk + "top"])
                    for hh in range(8):
                        cv_ = cflat[hh // 4][:, (hh % 4) * 256:(hh % 4 + 1) * 256]
                        ck_ = ("K1", "K2")[hh // 4]
                        S.dve(lambda e, o=K0f[:, (hh % 4) * 256:(hh % 4 + 1) * 256],
                              r_=top[:, hh, 0:8], i=cv_: e.match_replace(out=o, in_to_replace=r_, in_values=i, imm_value=-1e30),
                              [ck_, tk + "top"], ["K0"])
                        S.dve(lambda e, o=top[:, hh, 8:16], i=K0f[:, (hh % 4) * 256:(hh % 4 + 1) * 256]:
                              e.max(out=o, in_=i), ["K0"], [tk + "top"])
                    stt_ = Tt["st"]
                    TS("dve", stt_[:, 0:8], top[:, :, 0], -1.0, None, ALU.mult, None, [tk + "top"], [tk + "st"])
                    for hh in range(8):
                        ACT(junk[:, hh * 16:(hh + 1) * 16], top[:, hh, :], AF.Exp, [tk + "top", tk + "st"], ["junk", tk + "st"],
                            bias=stt_[:, hh:hh + 1], accum=stt_[:, 8 + hh:9 + hh])
                    ACT(stt_[:, 16:24], stt_[:, 8:16], AF.Ln, [tk + "st"], [tk + "st"])
                    TT("dve", stt_[:, 16:24], stt_[:, 0:8], stt_[:, 16:24], ALU.subtract, [tk + "st"], [tk + "st"])
                NIT = 64
                pairs = [(i, hh) for i in range(NIT) for hh in range(8)]
                cptr = [0]

                def emit_cadd(upto):
                    while cptr[0] < min(upto, len(pairs)):
                        i_, hh = pairs[cptr[0]]
                        g = cptr[0]
                        eb_, t_ = i_ // 2, i_ % 2
                        s_ = T[t_]["s"]
                        in0 = s_[:, hh, 0, eb_ * 4:(eb_ + 1) * 4].unsqueeze(2).broadcast_to([128, 4, 128])
                        in1 = s_[:, hh, 1, :].unsqueeze(1).broadcast_to([128, 4, 128])
                        TT("dve", CC[g % 3], in0, in1, ALU.add, [f"T{t_}s"], [f"C{g % 3}"])
                        cptr[0] += 1

                def load_u(eb_):
                    DMA(ublk[eb_ % 2].rearrange("p k n -> p (k n)"), utb[eb_].rearrange("p k n -> p (k n)"), (), [f"ublk{eb_ % 2}"])

                def load_v(eb_):
                    DMA(vblk[eb_ % 3].rearrange("p c d -> p (c d)"), vb[eb_], (), [f"vblk{eb_ % 3}"])

                load_u(0)
                load_v(0)
                for it in range(NIT + 3):
                    if it < NIT and it % 2 == 0 and it // 2 + 1 < 32:
                        load_u(it // 2 + 1)
                    if it < NIT and it % 2 == 1 and it // 2 + 1 < 32:
                        load_v(it // 2 + 1)
                    if 0 <= it - 1 < NIT:
                        t = (it - 1) % 2
                        Tt = T[t]
                        tk = f"T{t}"
                        CP("act", Tt["wsum"], PSB(WB[t]), pk(WB[t]), [tk + "wsum"])
                        TT("pool", Tt["coef"], Tt["G"], Tt["wsum"], ALU.mult, [tk + "G", tk + "wsum"], [tk + "coef"])
                    if it < NIT:
                        eb, t = it // 2, it % 2
                        Tt = T[t]
                        tk = f"T{t}"
                        for kc in range(8):
                            MM(PSB(4), Tt["hT"][:, kc, :], ublk[eb % 2][:, kc, :], kc == 0, kc == 7,
                               [tk + "hT", f"ublk{eb % 2}"], pk(4))
                    if 0 <= it - 2 < NIT:
                        t = (it - 2) % 2
                        Tt = T[t]
                        tk = f"T{t}"
                        ptc = PSBb(6)[:, t * 512:(t + 1) * 512]
                        for cc in range(4):
                            TR(ptc[:, cc * 128:(cc + 1) * 128], Tt["coef"][:, cc * 128:(cc + 1) * 128], identb[:],
                               [tk + "coef", "identb"], [f"ps6_{t}"])
                        CP("act", Tt["coefT"], ptc.rearrange("p (c n) -> p c n", c=4), [f"ps6_{t}"], [tk + "coefT"])
                    if 0 <= it - 3 < NIT:
                        i3 = it - 3
                        eb3, t = i3 // 2, i3 % 2
                        Tt = T[t]
                        tk = f"T{t}"
                        for n in range(2):
                            for cc in range(4):
                                MM(PSB(2 * t + n), Tt["coefT"][:, cc, :], vblk[eb3 % 3][:, cc, n * 512:(n + 1) * 512],
                                   eb3 == 0 and cc == 0, eb3 == 31 and cc == 3, [tk + "coefT", f"vblk{eb3 % 3}"], pk(2 * t + n))
                    if it < NIT:
                        eb, t = it // 2, it % 2
                        Tt = T[t]
                        tk = f"T{t}"
                        stt_ = Tt["st"]
                        for hh in range(8):
                            g = it * 8 + hh
                            emit_cadd(g + 3)
                            Cx, Ex, Wx = CC[g % 3], EE[g % 3], WW[hh]
                            ck, ek, wk_ = f"C{g % 3}", f"E{g % 3}", f"W{hh}"
                            ACT(Ex, Cx, AF.Exp, [ck, tk + "st"], [ek], bias=stt_[:, 16 + hh:17 + hh])
                            STT("dve", Wx, Cx, Tt["top"][:, hh, 15:16], Ex, ALU.is_ge, ALU.mult, [ck, ek, tk + "top"], [wk_])
                            MM(PSB(WB[t]), identb[:], Wx.rearrange("p a b -> p (a b)"), hh == 0, hh == 7, ["identb", wk_], pk(WB[t]))
                            if hh == 1:
                                ACT(Tt["G"], PSB(4), AF.Gelu_apprx_tanh, pk(4), [tk + "G"])
                for t in range(2):
                    Tt = T[t]
                    tk = f"T{t}"
                    row0 = blk * 256 + t * 128
                    TT("dve", outt, PSB(2 * t, 2), rep[2], ALU.mult, pk(2 * t, 2) + ["rep2"], ["outt"])
                    TT("dve", outt, outt, Tt["xn"], ALU.add, ["outt", tk + "xn"], ["outt"])
                    ACT(junk, outt, AF.Square, ["outt"], ["junk", "ssq"], accum=ssq[:, 0:1])
                    ACT(ssq[:, 1:2], ssq[:, 0:1], AF.Ln, ["ssq", "epsc"], ["ssq"], bias=epsc[:, 0:1], scale=1.0 / D)
                    ACT(ssq[:, 2:3], ssq[:, 1:2], AF.Exp, ["ssq"], ["ssq"], scale=-0.5)
                    STT("dve", outt, outt, ssq[:, 2:3], rep[3], ALU.mult, ALU.mult, ["outt", "ssq", "rep3"], ["outt"])
                    DMA(outd[row0:row0 + 128, :], outt, ["outt"], ())
        cnt = S.emit(nc)
    return nc, cnt


def _consts():
    cst = np.zeros((128, 642), np.float32)
    cst[:, 0:128] = np.eye(128, dtype=np.float32)
    j = np.arange(128)[:, None].astype(np.float32)
    i = np.arange(128)[None, :].astype(np.float32)
    cst[:, 128:256] = np.maximum(i - j, 0.0)
    cst[:, 256:384] = np.maximum(j - i, 0.0)
    cst[:, 384:512] = np.broadcast_to(i + 1.0, (128, 128))
    cst[:, 512:640] = np.broadcast_to(128.0 - i, (128, 128))
    cst[:, 640] = 127.0 - np.arange(128)
    cst[:, 641] = np.arange(128)
    return cst


def _rope_table():
    t = np.arange(NTOK)
    row = (t // 64).astype(np.float32)
    col = (t % 64).astype(np.float32)
    inv = (np.float32(10000.0) ** (-np.arange(32, dtype=np.float32) / np.float32(32))).astype(np.float32)
    ar = (row[:, None] * inv[None, :]).astype(np.float32)
    ac = (col[:, None] * inv[None, :]).astype(np.float32)
    cr, sr, cc, sc = np.cos(ar), np.sin(ar), np.cos(ac), np.sin(ac)
    tab = np.concatenate([cr, cr, cc, cc, -sr, sr, -sc, sc], axis=1).astype(np.float32)
    return tab


def _chunkT(v):
    return np.ascontiguousarray(np.asarray(v, np.float32).reshape(8, 128).T)


def make_in_maps(inputs):
    f = np.float32
    x = np.asarray(inputs["x"], f)
    ctx = np.asarray(inputs["ctx"], f)
    c = np.asarray(inputs["c"], f)
    c_ctx = np.asarray(inputs["c_ctx"], f)
    l = 0
    rope = _rope_table()
    rope_rev = np.ascontiguousarray(rope[::-1])
    cst = _consts()
    gains = np.ascontiguousarray(np.stack([inputs["norm1_g"][l], inputs["norm2_g"][l], inputs["final_g"]]).astype(f))
    keys = np.asarray(inputs["peer_keys"][l], f)
    kbd = np.zeros((128, 8, 256), f)
    for h in range(8):
        for p in range(2):
            kbd[p * 64:(p + 1) * 64, h, p * 128:(p + 1) * 128] = keys[h, p].T
    kbd = np.ascontiguousarray(kbd.reshape(128, 2048))
    uT = np.ascontiguousarray(np.asarray(inputs["peer_u"][l], f).T)
    shared = {
        "cst": cst, "gains": gains,
        "mod_b": np.ascontiguousarray(np.asarray(inputs["mod_b"][l], f).reshape(1, -1)),
        "mod_w": np.ascontiguousarray(np.asarray(inputs["mod_w"][l], f)),
        "w_in": np.ascontiguousarray(np.asarray(inputs["w_in"][l], f)),
        "w_ret_out": np.ascontiguousarray(np.asarray(inputs["w_ret_out"][l], f)),
        "w_lru_out": np.ascontiguousarray(np.asarray(inputs["w_lru_out"][l], f)),
        "w_out": np.ascontiguousarray(np.asarray(inputs["w_out"][l], f)),
        "peer_wq": np.ascontiguousarray(np.asarray(inputs["peer_wq"][l], f)),
        "keysbd": kbd, "peer_uT": uT,
        "peer_v": np.ascontiguousarray(np.asarray(inputs["peer_v"][l], f)),
    }
    cw = np.asarray(inputs["conv_w"][l], f)
    maps = []
    for core in range(8):
        b, s = core // 2, core % 2
        dirs = (0, 1) if s == 0 else (1, 0)
        m = dict(shared)
        if s == 0:
            m["xf"] = np.ascontiguousarray(x[b])
            m["ctxf"] = np.ascontiguousarray(ctx[b])
            m["rope"] = rope
            w5 = np.concatenate([cw, np.zeros((1, D), f)], axis=0)
        else:
            m["xf"] = np.ascontiguousarray(x[b, ::-1])
            m["ctxf"] = np.ascontiguousarray(ctx[b, ::-1])
            m["rope"] = rope_rev
            w5 = np.concatenate([np.zeros((1, D), f), cw[::-1]], axis=0)
        par = np.zeros((128, 112), f)
        cv = np.stack([_chunkT(c[b]), _chunkT(c_ctx)], axis=2)
        par[:, 0:16] = cv.reshape(128, 16)
        lam = np.asarray(inputs["lru_lambda"][l], f)
        lba = np.asarray(inputs["lru_ba"][l], f)
        lbx = np.asarray(inputs["lru_bx"][l], f)
        par[:, 16:32] = np.concatenate([_chunkT(lam[d]) for d in dirs], axis=1)
        par[:, 32:48] = np.concatenate([_chunkT(lba[d]) for d in dirs], axis=1)
        par[:, 48:64] = np.concatenate([_chunkT(lbx[d]) for d in dirs], axis=1)
        par[:, 64:104] = np.concatenate([_chunkT(w5[o]) for o in range(5)], axis=1)
        par[:, 104:112] = _chunkT(inputs["conv_b"][l])
        m["par"] = par
        rd = np.asarray(inputs["ret_decay"][l], f)
        m["dec"] = np.ascontiguousarray(np.concatenate([rd[dirs[0]], rd[dirs[1]]]).reshape(1, 8))
        wa = np.asarray(inputs["lru_wa"][l], f)
        wx = np.asarray(inputs["lru_wx"][l], f)
        gw = np.stack([wa[dirs[0]], wx[dirs[0]], wa[dirs[1]], wx[dirs[1]]], axis=0)
        m["gatew"] = np.ascontiguousarray(gw.transpose(2, 0, 1, 3).reshape(128, 4 * 8 * 128))
        maps.append(m)
    return maps


_CACHE = {}


def kernel(**inputs):
    if "nc" not in _CACHE:
        _CACHE["nc"] = build()[0]
    nc = _CACHE["nc"]
    maps = make_in_maps(inputs)
    res = run_bass_kernel_spmd(nc, maps, core_ids=list(range(8)))
    out = np.zeros((4, NTOK, D), np.float32)
    for core in range(8):
        b, s = core // 2, core % 2
        o = np.asarray(res.results[core]["out"], np.float32)
        if s == 0:
            out[b, 0:OWN] = o
        else:
            out[b, OWN:NTOK] = o[::-1]
    return out
```
